# Optimizing a Trainium2 kernel written in Bass

```python
import math
import jax, jax.numpy as jnp
from jax import lax
import numpy as np

D_MODEL = 1024
BATCH = 2
SEQ = 8192
DEPTH = 1

EPS = 1e-6
D_FF = 2816
N_MOD = 9
GM_CHUNK = 128
GM_WIDTH = D_MODEL
GM_GROUPS = 8
GM_GROUP_DIM = GM_WIDTH // GM_GROUPS
DN_HEADS = 8
DN_HEAD_DIM = 128
DN_WIDTH = DN_HEADS * DN_HEAD_DIM
DN_CONV = 4
DN_CHUNK = 64
N_BRANCH = 2
IN_SPLITS = (GM_WIDTH, GM_WIDTH, DN_WIDTH, DN_WIDTH, DN_WIDTH, DN_WIDTH,
             DN_HEADS, DN_HEADS, N_BRANCH * D_MODEL)
IN_WIDTH = sum(IN_SPLITS)
IN_OFFSETS = tuple(int(o) for o in np.cumsum(IN_SPLITS)[:-1])

kernel_name = "hybrid_gmlp_gdn_macaron_adaln"


def rms_norm(x, g):
    xf = x.astype(jnp.float32)
    y = xf * lax.rsqrt(jnp.mean(xf * xf, axis=-1, keepdims=True) + EPS)
    return (y * g.astype(jnp.float32)).astype(x.dtype)


def layer_norm(x, g, b):
    xf = x.astype(jnp.float32)
    mu = jnp.mean(xf, axis=-1, keepdims=True)
    var = jnp.mean(jnp.square(xf - mu), axis=-1, keepdims=True)
    y = (xf - mu) * lax.rsqrt(var + EPS)
    return (y * g.astype(jnp.float32) + b.astype(jnp.float32)).astype(x.dtype)


def modulate(h, shift, scale):
    return h * (1.0 + scale[:, None, :]) + shift[:, None, :]


def swiglu(h, w_in, w_out):
    a, b = jnp.split(h @ w_in, 2, axis=-1)
    return (jax.nn.silu(a) * b) @ w_out


def l2_normalize(x):
    return x * lax.rsqrt(jnp.sum(x * x, axis=-1, keepdims=True) + EPS)


def causal_depthwise_conv(x, w):
    ch = x.shape[-1]
    return lax.conv_general_dilated(
        x, w[:, None, :].astype(x.dtype), window_strides=(1,),
        padding=[(DN_CONV - 1, 0)], dimension_numbers=('NWC', 'WIO', 'NWC'),
        feature_group_count=ch)


def chunked_spatial_gating(u, v, ln_g, ln_b, w_s, b_s):
    bsz, seq, _ = v.shape
    v = layer_norm(v, ln_g, ln_b)
    vc = v.reshape(bsz, seq // GM_CHUNK, GM_CHUNK, GM_GROUPS, GM_GROUP_DIM)
    causal = jnp.tril(jnp.ones((GM_CHUNK, GM_CHUNK), dtype=bool))
    w = jnp.where(causal, w_s, 0.0).astype(v.dtype)
    mixed = jnp.einsum('gts,bnsgc->bntgc', w, vc) + b_s.T[:, :, None]
    return u * mixed.reshape(bsz, seq, GM_WIDTH)


def chunk_gated_delta_rule(q, k, v, g, beta):
    bsz, seq, nh, dk = q.shape
    dv = v.shape[-1]
    C = DN_CHUNK
    n_chunks = seq // C

    def chunks(t):
        return t.reshape(bsz, n_chunks, C, nh, t.shape[-1]).transpose(1, 0, 3, 2, 4)

    q, k, v = chunks(q), chunks(k), chunks(v)
    g = chunks(g[..., None])[..., 0]
    beta = chunks(beta[..., None])
    g_cum = jnp.cumsum(g, axis=-1)
    causal = jnp.tril(jnp.ones((C, C), dtype=bool))
    strict = jnp.tril(jnp.ones((C, C), dtype=bool), -1)
    decay = jnp.exp(jnp.where(causal, g_cum[..., :, None] - g_cum[..., None, :], -jnp.inf))
    k_beta = k * beta
    a_low = jnp.where(strict, jnp.einsum('nbhcd,nbhed->nbhce', k_beta, k) * decay, 0.0)
    tri = a_low + jnp.eye(C, dtype=q.dtype)
    rhs = jnp.concatenate([v * beta, k_beta * jnp.exp(g_cum)[..., None]], axis=-1)
    sol = lax.linalg.triangular_solve(tri, rhs, left_side=True, lower=True, unit_diagonal=True)
    u, w = sol[..., :dv], sol[..., dv:]
    qk = jnp.einsum('nbhcd,nbhed->nbhce', q, k) * decay
    q_dec = q * jnp.exp(g_cum)[..., None]
    k_dec = k * jnp.exp(g_cum[..., -1:] - g_cum)[..., None]
    g_last = jnp.exp(g_cum[..., -1])[..., None, None]

    def step(state, xs):
        u_n, w_n, qk_n, qd_n, kd_n, gl_n = xs
        v_new = u_n - jnp.einsum('bhck,bhkv->bhcv', w_n, state)
        o_n = (jnp.einsum('bhck,bhkv->bhcv', qd_n, state)
               + jnp.einsum('bhce,bhev->bhcv', qk_n, v_new))
        state = state * gl_n + jnp.einsum('bhck,bhcv->bhkv', kd_n, v_new)
        return state, o_n

    state0 = jnp.zeros((bsz, nh, dk, dv), q.dtype)
    _, o = lax.scan(step, state0, (u, w, qk, q_dec, k_dec, g_last))
    return o.transpose(1, 0, 3, 2, 4).reshape(bsz, seq, nh, dv)


def gated_deltanet(q, k, v, z, a, b, conv_w, a_log, dt_bias, norm_g):
    bsz, seq, _ = q.shape
    f32 = jnp.float32
    qkv = jax.nn.silu(causal_depthwise_conv(jnp.concatenate([q, k, v], axis=-1), conv_w))
    q, k, v = jnp.split(qkv, 3, axis=-1)
    shp = (bsz, seq, DN_HEADS, DN_HEAD_DIM)
    q = l2_normalize(q.reshape(shp).astype(f32)) * (DN_HEAD_DIM ** -0.5)
    k = l2_normalize(k.reshape(shp).astype(f32))
    v = v.reshape(shp).astype(f32)
    beta = jax.nn.sigmoid(b.astype(f32))
    g = -jnp.exp(a_log.astype(f32)) * jax.nn.softplus(a.astype(f32) + dt_bias.astype(f32))
    o = chunk_gated_delta_rule(q, k, v, g, beta)
    o = o * lax.rsqrt(jnp.mean(o * o, axis=-1, keepdims=True) + EPS) * norm_g.astype(f32)
    o = o * jax.nn.silu(z.reshape(shp).astype(f32))
    return o.reshape(bsz, seq, DN_WIDTH).astype(z.dtype)


def setup_inputs(seed: int = 0) -> dict:
    key = jax.random.key(seed)
    ks = iter(jax.random.split(key, 32))
    f32 = jnp.float32

    def nrm(shape, fan_in, gain=1.0):
        return jax.random.normal(next(ks), shape, f32) * (gain * fan_in ** -0.5)

    def gain_vec(shape):
        return 1.0 + 0.02 * jax.random.normal(next(ks), shape, f32)

    L = DEPTH
    x = jax.random.normal(next(ks), (BATCH, SEQ, D_MODEL), f32)
    c = jax.random.normal(next(ks), (BATCH, D_MODEL), f32)
    w_ada = nrm((L, D_MODEL, N_MOD * D_MODEL), D_MODEL, 0.5)
    b_ada = 0.01 * jax.random.normal(next(ks), (L, N_MOD * D_MODEL), f32)
    norm1_g = gain_vec((L, D_MODEL))
    ffn1_w_in = nrm((L, D_MODEL, 2 * D_FF), D_MODEL)
    ffn1_w_out = nrm((L, D_FF, D_MODEL), D_FF)
    norm2_g = gain_vec((L, D_MODEL))
    w_in = nrm((L, D_MODEL, IN_WIDTH), D_MODEL)
    conv_w = nrm((L, DN_CONV, 3 * DN_WIDTH), DN_CONV)
    a_log = jnp.log(jax.random.uniform(next(ks), (L, DN_HEADS), f32, 1.0, 16.0))
    dt = jnp.exp(jax.random.uniform(next(ks), (L, DN_HEADS), f32,
                                    math.log(1e-3), math.log(1e-1)))
    dt_bias = dt + jnp.log(-jnp.expm1(-dt))
    dn_norm_g = gain_vec((L, DN_HEAD_DIM))
    gm_ln_g = gain_vec((L, GM_WIDTH))
    gm_ln_b = 0.01 * jax.random.normal(next(ks), (L, GM_WIDTH), f32)
    gm_w_s = nrm((L, GM_GROUPS, GM_CHUNK, GM_CHUNK), GM_CHUNK)
    gm_b_s = 1.0 + 0.02 * jax.random.normal(next(ks), (L, GM_GROUPS, GM_CHUNK), f32)
    w_branch = nrm((L, N_BRANCH, GM_WIDTH, D_MODEL), GM_WIDTH)
    w_out = nrm((L, D_MODEL, D_MODEL), D_MODEL)
    norm3_g = gain_vec((L, D_MODEL))
    ffn2_w_in = nrm((L, D_MODEL, 2 * D_FF), D_MODEL)
    ffn2_w_out = nrm((L, D_FF, D_MODEL), D_FF)
    final_g = gain_vec((D_MODEL,))
    return {"x": x, "c": c, "w_ada": w_ada, "b_ada": b_ada,
            "norm1_g": norm1_g, "ffn1_w_in": ffn1_w_in, "ffn1_w_out": ffn1_w_out,
            "norm2_g": norm2_g, "w_in": w_in, "conv_w": conv_w, "a_log": a_log,
            "dt_bias": dt_bias, "dn_norm_g": dn_norm_g, "gm_ln_g": gm_ln_g,
            "gm_ln_b": gm_ln_b, "gm_w_s": gm_w_s, "gm_b_s": gm_b_s,
            "w_branch": w_branch, "w_out": w_out, "norm3_g": norm3_g,
            "ffn2_w_in": ffn2_w_in, "ffn2_w_out": ffn2_w_out, "final_g": final_g}


def reference(x, c, w_ada, b_ada, norm1_g, ffn1_w_in, ffn1_w_out, norm2_g, w_in, conv_w,
              a_log, dt_bias, dn_norm_g, gm_ln_g, gm_ln_b, gm_w_s, gm_b_s, w_branch,
              w_out, norm3_g, ffn2_w_in, ffn2_w_out, final_g):
    c_act = jax.nn.silu(c)
    for l in range(DEPTH):
        mod = c_act @ w_ada[l] + b_ada[l]
        sh1, sc1, ga1, sh2, sc2, ga2, sh3, sc3, ga3 = jnp.split(mod, N_MOD, axis=-1)

        h = modulate(rms_norm(x, norm1_g[l]), sh1, sc1)
        x = x + 0.5 * ga1[:, None, :] * swiglu(h, ffn1_w_in[l], ffn1_w_out[l])

        h = modulate(rms_norm(x, norm2_g[l]), sh2, sc2)
        proj = h @ w_in[l]
        u_a, v_a, q_b, k_b, v_b, z_b, a_b, b_b, gate_logits = jnp.split(proj, IN_OFFSETS, axis=-1)
        u_a = jax.nn.gelu(u_a, approximate=False)
        v_a = jax.nn.gelu(v_a, approximate=False)
        o_a = chunked_spatial_gating(u_a, v_a, gm_ln_g[l], gm_ln_b[l], gm_w_s[l], gm_b_s[l])
        o_b = gated_deltanet(q_b, k_b, v_b, z_b, a_b, b_b, conv_w[l], a_log[l],
                             dt_bias[l], dn_norm_g[l])
        g_a, g_b = jnp.split(jax.nn.sigmoid(gate_logits), N_BRANCH, axis=-1)
        merged = g_a * (o_a @ w_branch[l, 0]) + g_b * (o_b @ w_branch[l, 1])
        x = x + ga2[:, None, :] * (merged @ w_out[l])

        h = modulate(rms_norm(x, norm3_g[l]), sh3, sc3)
        x = x + 0.5 * ga3[:, None, :] * swiglu(h, ffn2_w_in[l], ffn2_w_out[l])
    return rms_norm(x, final_g)
```

```python
import numpy as np
import ml_dtypes
from contextlib import ExitStack
import concourse.bass as bass
import concourse.mybir as mybir
from concourse.bass_utils import run_bass_kernel_spmd

F32 = mybir.dt.float32
BF16 = mybir.dt.bfloat16
AF = mybir.ActivationFunctionType
ALU = mybir.AluOpType
AX = mybir.AxisListType
EPS = 1e-6
NEGBIG = -1.0e30


class Cfg:
    def __init__(self, D=1024, DFF=2816, H=8, NT=2048, TB=512, G=4, NCORES=8, WG=True):
        self.WG = WG
        self.STOP = 0
        self.D, self.DFF, self.H, self.NT, self.TB, self.G, self.NCORES = D, DFF, H, NT, TB, G, NCORES
        self.KT = D // 128
        self.FT = DFF // 128
        self.NB = NT // TB
        self.CPB = TB // 128
        self.NCH = NT // 128
        assert H * 128 == D
        self.INW = 6 * D + 2 * H + 2 * D
        self.OFF_U, self.OFF_V, self.OFF_Q, self.OFF_Z = 0, D, 2 * D, 5 * D
        self.OFF_AB = 6 * D
        self.OFF_GATE = 6 * D + 2 * H
        KT = self.KT
        self.V_N1, self.V_N2, self.V_N3, self.V_NF = 0, KT, 2 * KT, 3 * KT
        self.V_BADA = 4 * KT
        self.V_LNG = 13 * KT
        self.V_DNG = 14 * KT
        self.V_CONV = 14 * KT + 1
        self.NV = self.V_CONV + 4 * 3 * KT
        self.WSLOT = max(self.FT * 128, KT * 256)


class Stop(Exception):
    pass


class Buf:
    __slots__ = ("w", "r", "name")

    def __init__(self, name=""):
        self.w = None
        self.r = {}
        self.name = name


class Emit:
    def __init__(self, nc, plan_only, wplan=None):
        self.nc = nc
        self.plan_only = plan_only
        self.ops = {e: [] for e in ("pe", "act", "dve", "pool", "sp")}
        self.cnt = {e: 0 for e in ("pe", "act", "dve", "pool")}
        self.waited = {e: {} for e in self.ops}
        self.dcnt = {}
        self.wreq = []
        self.wplan = wplan
        self.wissued = 0
        self.wconsumed = 0
        self.pend = {}

    def barrier(self):
        snap = dict(self.cnt)
        snap.update(self.dcnt)
        for e in self.ops:
            p = self.pend.get(e) or {}
            for k, v in snap.items():
                if v > p.get(k, 0):
                    p[k] = v
            self.pend[e] = p

    def op(self, eng, fn, reads=(), writes=(), dma=None):
        waits = {}

        def need(tok):
            if tok is None:
                return
            key, val = tok
            if key == eng and eng == "pe":
                return
            if self.waited[eng].get(key, 0) >= val:
                return
            if waits.get(key, 0) < val:
                waits[key] = val

        p = self.pend.get(eng)
        if p:
            for k, v in p.items():
                if v > 0:
                    need((k, v))
            self.pend[eng] = None
        for b in reads:
            need(b.w)
        for b in writes:
            need(b.w)
            for k, v in b.r.items():
                need((k, v))
        for k, v in waits.items():
            self.waited[eng][k] = v
        if dma is None:
            self.cnt[eng] += 1
            tok = (eng, self.cnt[eng])
            inc = (eng, 1)
        else:
            self.dcnt[dma] = self.dcnt.get(dma, 0) + 16
            tok = (dma, self.dcnt[dma])
            inc = (dma, 16)
        for b in reads:
            if b.r.get(tok[0], 0) < tok[1]:
                b.r[tok[0]] = tok[1]
        for b in writes:
            b.w = tok
            b.r = {}
        if not self.plan_only:
            self.ops[eng].append((list(waits.items()), fn, inc))
        return tok


def build(cfg, part=0):
    c = cfg
    D, KT, DFF, FT, H, NT, TB, NB, CPB, NCH, G = c.D, c.KT, c.DFF, c.FT, c.H, c.NT, c.TB, c.NB, c.CPB, c.NCH, c.G
    nc = bass.Bass("TRN2", target_bir_lowering=False)
    FUSED = (part == 3)

    def din(name, shape, dt=F32):
        return nc.dram_tensor(name, list(shape), dt, kind="ExternalInput").ap()

    xT_d = (din("xT", [128, KT, G * NT]) if FUSED else din("xT", [128, KT, NT + 4])) if part != 2 else None
    cT_d = din("cT", [128, KT])
    vecs_d = din("vecs", [128, c.NV])
    rowv_d = din("rowv", [1, 2 * D])
    bc8_d = din("bc8", [128, 2 * H])
    masks_d = din("masks", [128, G])
    bmask_d = din("bmask", [128, 4, 128])
    NCs = c.NCORES
    WSPEC = [("w_ada", D, 9 * D), ("ffn1_w_in", D, 2 * DFF), ("ffn1_w_out", DFF, D), ("w_in", D, c.INW),
             ("w_branch", 2 * D, D), ("w_out", D, D), ("ffn2_w_in", D, 2 * DFF), ("ffn2_w_out", DFF, D)]
    wext, wbnc, wfull = {}, {}, {}
    wap = {}
    if part == 1:
        WSPEC = [w for w in WSPEC if w[0] in ("w_ada", "ffn1_w_in", "ffn1_w_out", "w_in")]
    if part == 2:
        WSPEC = [w for w in WSPEC if w[0] not in ("ffn1_w_in", "ffn1_w_out")]
    for (wn, wr, wc) in WSPEC:
        if c.WG:
            wext[wn] = din(wn, [wr // NCs, wc])
            wbnc[wn] = nc.dram_tensor(wn + "_bnc", [wr // NCs, wc], F32)
            wfull[wn] = nc.dram_tensor(wn + "_full", [wr, wc], F32)
            wap[wn] = wfull[wn].ap()
        else:
            wap[wn] = din(wn, [wr, wc])
    w_ada_d = wap["w_ada"]
    f1in_d = wap.get("ffn1_w_in")
    f1out_d = wap.get("ffn1_w_out")
    win_d = wap["w_in"]
    wsT_d = din("w_sT", [KT, 128, 128])
    wbr_full = wap.get("w_branch")
    wbr_d = [wbr_full[0:D, :], wbr_full[D:2 * D, :]] if wbr_full is not None else None
    wout_d = wap.get("w_out")
    f3in_d = wap.get("ffn2_w_in")
    f3out_d = wap.get("ffn2_w_out")
    outT_d = nc.dram_tensor("outT", [128, KT, NT], F32, kind="ExternalOutput").ap() if part != 1 else None

    skw = {} if part in (0, 3) else {"kind": ("ExternalOutput" if part == 1 else "ExternalInput")}
    x1s_d = nc.dram_tensor("x1s", [128, KT, NT], F32, **skw).ap()
    h2s_d = nc.dram_tensor("h2s", [128, KT, NT], BF16, **skw).ap()
    os_d = nc.dram_tensor("o_s", [NCH, 128, H * 128], BF16, **skw).ap()
    rs_d = nc.dram_tensor("r_s", [NCH, 128, H * 128], BF16, **skw).ap()
    if part != 2:
        st_in_t = nc.dram_tensor("st_in", [128, H * 256], F32, **skw)
    akw = {} if part in (0, 3) else {"kind": "ExternalInput"}
    if part != 1:
        st_all_t = nc.dram_tensor("st_all", [G * 128, H * 256], F32, **akw)

    es = ExitStack()
    T = {}

    def sb(name, shape, dt):
        T[name] = nc.alloc_sbuf_tensor(name, list(shape), dt)
        return T[name]

    ARENA_E = 57 * 1024
    arena = nc.alloc_sbuf_tensor("arena", [128, ARENA_E], BF16)
    vptr = {}

    def cv(view, name, shape, dt):
        n = 1
        for d_ in shape[1:]:
            n *= d_
        ne = n * (2 if dt == F32 else 1)
        ne = (ne + 15) // 16 * 16
        off = vptr.get(view, 0)
        vptr[view] = off + ne
        assert off + ne <= ARENA_E, (view, name, off + ne)
        ap = arena[:, off:off + ne]
        if dt == F32:
            ap = ap.bitcast(F32)
        ap = ap[:, 0:n]
        if len(shape) == 3:
            ap = ap.rearrange("p (a b) -> p a b", a=shape[1])
        T[name] = ap
        return ap

    identb = sb("identb", [128, 128], BF16)
    identf = sb("identf", [128, 128], F32)
    onesb = sb("onesb", [128, 128], BF16)
    onesD = sb("onesD", [128, 128], BF16)
    onesV = sb("onesV", [128, 128], BF16)
    onesf = sb("onesf", [128, 128], F32)
    tri = sb("tri", [128, 128], F32)
    sel127 = sb("sel127", [128, 128], F32)
    neg1 = sb("neg1", [128, 128], F32)
    neg2 = sb("neg2", [128, 128], F32)
    vecs = sb("vecs_sb", [128, c.NV], F32)
    bc8 = sb("bc8_sb", [128, 2 * H], F32)
    masks = sb("masks_sb", [128, G], F32)
    bmask = sb("bmask_sb", [128, 4, 128], F32)
    ealog = sb("ealog", [128, H], F32)
    cTf = sb("cTf", [128, KT], F32)
    cact = sb("cact", [128, KT], BF16)
    modT = sb("modT", [128, 9 * KT], F32)
    gsc = sb("gsc", [128, 3 * KT], F32)
    hga = sb("hga", [128, 3 * KT], F32)
    wab = sb("wab", [128, KT, 2 * H], BF16)
    WmT = sb("WmT", [128, KT, 128], BF16)
    BiasG = sb("BiasG", [128, KT, 128], F32)
    NWS = 3
    wslot = [sb(f"wslot{i}", [128, c.WSLOT], BF16) for i in range(NWS)]
    xt = [sb("xt0", [128, KT, TB], F32)]
    xt.append(xt[0])
    h2 = sb("h2", [128, KT, TB], BF16)
    sq = [sb(f"sq{i}", [128, TB], BF16) for i in range(2)]
    rt = sb("rt", [128, TB], F32)
    rstd = sb("rstd", [128, TB], F32)
    ntmp = [sb(f"ntmp{i}", [128, TB], F32) for i in range(2)]
    tails = sb("tails", [128, 3 * KT, 4], BF16)
    upad = sb("upad", [128, H, 256], BF16)
    Sf = sb("Sf", [128, H, 256], F32)
    Sb = sb("Sb", [128, H, 256], BF16)
    ostage = [sb(f"ostage{i}", [128, H, 128], BF16) if not FUSED else None for i in range(2)]
    rstage = [sb(f"rstage{i}", [128, H, 128], BF16) if not FUSED else None for i in range(2)]
    Sstb = sb("Sstb", [128, H, 128], BF16) if not FUSED else None
    oTtb = sb("oTtb", [128, H, TB], BF16) if FUSED else None
    rowv = cv("S", "rowv_sb", [1, 2 * D], F32)[0:1, :]
    wsTf = cv("S", "wsTf", [128, KT, 128], F32)
    rsrow = cv("S", "rsrow", [1, KT * 128], F32)[0:1, :]
    pre = [cv("P1", f"pre{i}", [128, 4 + TB], BF16) for i in range(3)]
    dslot = [cv("P1", f"dslot{i}", [128, 4, 128], BF16) for i in range(2)]
    qraw = [cv("P1", f"qraw{i}", [128, TB], F32) for i in range(2)]
    knT = cv("P1", "knT", [128, H, TB], BF16)
    vT = cv("P1", "vT", [128, H, TB], BF16)
    ab = cv("P1", "ab", [128, 2 * H], F32)
    abT = cv("P1", "abT", [128, CPB, 2 * H], F32)
    sm = {n: cv("P1", "sm_" + n, [128, H], F32) for n in
          ("e", "beta", "x", "ex", "sp", "g", "CB", "eCB", "kbs", "dka", "dke", "gle")}
    smT = {n: cv("P1", "smT_" + n, [128, CPB * H], F32) for n in
           ("e", "beta", "x", "ex", "sp", "g", "CB", "eCB", "kbs", "dka", "dke", "gle")}
    dg = cv("P1", "tw", [128, H, 128], F32)
    t1 = dg
    t2 = dg
    RB = cv("P1", "RB", [128, H, 128], F32)
    E1 = cv("P1", "E1", [128, H, 128], BF16)
    E1b = E1
    Nb = [cv("P1", f"Nb{i}", [128, H, 128], BF16) for i in range(2)]
    NTb = [cv("P1", f"NTb{i}", [128, H, 128], BF16) for i in range(2)]
    Ub = [cv("P1", f"Ub{i}", [128, H, 128], BF16) for i in range(2)]
    Afull = cv("P1", "Afull", [128, H, 128], BF16)
    ATfull = cv("P1", "ATfull", [128, H, 128], BF16)
    As1 = cv("P1", "As", [128, H, 128], BF16)
    Ms1 = cv("P1", "Ms", [128, H, 128], BF16)
    Tb = [cv("P1", f"Tb{i}", [128, H, 128], BF16) for i in range(2)]
    Wb = cv("P1", "Wb", [128, H, 128], BF16)
    Vb = cv("P1", "Vb", [128, H, 128], BF16)
    kd = cv("P1", "kd", [128, H, 128], BF16)
    kbg = cv("P1", "kbg", [128, H, 128], BF16)
    vb = cv("P1", "vb", [128, H, 128], BF16)
    wT = cv("P1", "wT", [128, H, 128], BF16)
    vnew = cv("P1", "vnew", [128, H, 256], BF16)
    vptr["F"] = vptr["P1"]
    hT = cv("F", "hT", [128, KT, TB], BF16)
    gT = cv("F", "gT", [128, FT, TB], BF16)
    sa = [cv("F", f"sa{i}", [128, TB], BF16) for i in range(4)]
    ostg = [cv("FO", "ostg0", [128, KT, TB], F32)]
    ostg.append(ostg[0])
    qnT = cv("P1", "qnT", [128, H, TB], BF16)
    E2 = cv("P1", "E2", [128, H, 128], BF16)
    Eg = cv("P1", "Eg", [128, H, 128], BF16)
    qkT = cv("P1", "qkT", [128, H, 128], BF16)
    qdT = cv("P1", "qdT", [128, H, 128], BF16)
    stg = [cv("X", "stg0", [128, H, 256], F32)] * max(G - 1, 1)
    PiT = cv("X", "PiT", [128, H, 128], F32)
    Sst = cv("X", "Sst", [128, H, 128], F32)
    cand = cv("X", "cand", [128, H, 128], F32)
    otrue = cv("P2", "otrue", [128, H, 128], F32)
    osq = cv("P2", "osq", [128, H, 128], BF16)
    ort = cv("P2", "ort", [128, H, 128], F32)
    ors = ort
    obT = cv("P2", "obT", [128, KT, TB], BF16)
    zs = cv("P2", "zs", [128, KT, TB], BF16)
    ug = cv("P2", "ug", [128, KT, TB], BF16)
    vg = cv("P2", "vg", [128, D], F32)
    nrm = cv("P2", "nrm", [128, D], BF16)
    st = {n: cv("P2", "st_" + n, [128, 1], F32) for n in ("s1", "s2", "mean", "msq", "var", "sd", "rstd")}
    mtmp = cv("P2", "mtmp", [128, KT, 128], F32)
    vsq = mtmp.rearrange("p a b -> p (a b)")
    oaT = cv("P2", "oaT", [128, KT, TB], BF16)
    gt = cv("P2", "gt", [128, 2 * KT, TB], BF16)
    mA = cv("P2", "mA", [128, KT, TB], BF16)
    mB = cv("P2", "mB", [128, TB], F32)
    mergedT = cv("P2", "mergedT", [128, KT, TB], BF16)
    wv = cv("P2", "wv", [128, KT, D], BF16)

    ps = [nc.alloc_psum_tensor(f"ps{i}", [128, 512], F32) for i in range(8)]

    def program(E):
        B = {}

        def chk(k):
            if c.STOP == k:
                raise Stop()

        def bf(name):
            if name not in B:
                B[name] = Buf(name)
            return B[name]

        psb = [Buf(f"ps{i}") for i in range(8)]
        pstate = [0]

        def psum():
            i = pstate[0] % 8
            pstate[0] += 1
            return ps[i], psb[i]

        def dma(q, out, in_, reads, writes, sem):
            E.op(q, lambda e, o=out, i=in_: e.dma_start(out=o, in_=i), reads, writes, dma=sem)

        def act(out, in_, func, reads, writes, bias=None, scale=None):
            kw = {}
            if bias is not None:
                kw["bias"] = bias
            if scale is not None:
                kw["scale"] = scale
            E.op("act", lambda e, o=out, i=in_, f=func, k=kw: e.activation(out=o, in_=i, func=f, **k), reads, writes)

        def tt(out, in0, in1, op, reads, writes, eng="dve"):
            E.op(eng, lambda e, o=out, a=in0, b=in1, p=op: e.tensor_tensor(out=o, in0=a, in1=b, op=p), reads, writes)

        def ts(out, in0, s1, s2, op0, op1, reads, writes, eng="dve"):
            if s2 is None:
                E.op(eng, lambda e, o=out, a=in0, x=s1, p=op0: e.tensor_scalar(out=o, in0=a, scalar1=x, scalar2=None, op0=p),
                     reads, writes)
            else:
                E.op(eng, lambda e, o=out, a=in0, x=s1, y=s2, p=op0, q=op1: e.tensor_scalar(
                    out=o, in0=a, scalar1=x, scalar2=y, op0=p, op1=q), reads, writes)

        def stt(out, in0, scalar, in1, op0, op1, reads, writes, eng="dve"):
            E.op(eng, lambda e, o=out, a=in0, s=scalar, b=in1, p=op0, q=op1: e.scalar_tensor_tensor(
                out=o, in0=a, scalar=s, in1=b, op0=p, op1=q), reads, writes)

        def cp(out, in_, reads, writes, eng="dve"):
            if eng == "act":
                E.op("act", lambda e, o=out, i=in_: e.activation(out=o, in_=i, func=AF.Copy), reads, writes)
            else:
                E.op(eng, lambda e, o=out, i=in_: e.tensor_copy(out=o, in_=i), reads, writes)

        def recip(out, in_, reads, writes):
            E.op("dve", lambda e, o=out, i=in_: e.reciprocal(out=o, in_=i), reads, writes)

        def mm(out, lhsT, rhs, start, stop, reads, writes):
            E.op("pe", lambda e, o=out, l=lhsT, r=rhs, s=start, p=stop: e.matmul(o, lhsT=l, rhs=r, start=s, stop=p),
                 reads, writes)

        def tr(out, in_, ident, reads, writes):
            E.op("pe", lambda e, o=out, i=in_, d=ident: e.transpose(o, i, d), reads, writes)

        def bc_mid(ap2d, n):
            return ap2d.unsqueeze(1).broadcast_to([128, n, ap2d.shape[1]])

        def bc_last(ap2d, n):
            return ap2d.unsqueeze(2).broadcast_to([128, ap2d.shape[1], n])

        wslot_b = [Buf(f"wslot{i}") for i in range(NWS)]

        def w_issue(upto):
            plan = E.wplan
            while E.wissued < min(upto, len(plan)):
                j = E.wissued
                s = j % NWS
                for (o_fn, src, dep) in plan[j]:
                    dma("pool", o_fn(wslot[s]), src, [bf(dep)] if dep else [], [wslot_b[s]], f"w{s}")
                E.wissued += 1

        def w_get(loads):
            j = E.wconsumed
            E.wconsumed += 1
            if E.plan_only:
                E.wreq.append(loads)
                return wslot[j % NWS], wslot_b[j % NWS]
            w_issue(j + NWS - 1)
            return wslot[j % NWS], wslot_b[j % NWS]

        def linear_fm(Wd, col0, nft, KTin, rhs, N, evac, CW=2):
            for _ in linear_fm_gen(Wd, col0, nft, KTin, rhs, N, evac, CW):
                pass

        def linear_fm_gen(Wd, col0, nft, KTin, rhs, N, evac, CW=2):
            nchunk = (nft + CW - 1) // CW
            for ci in range(nchunk):
                f0 = ci * CW
                nf = min(CW, nft - f0)
                ncols = nf * 128
                src = Wd[:, col0 + f0 * 128: col0 + f0 * 128 + ncols].rearrange("(kt p) n -> p kt n", p=128)
                nm = Wd.name
                dep = ("g_" + nm[:-5]) if nm.endswith("_full") else None
                slot, sbuf_ = w_get([(lambda s, k=KTin, n=ncols: s[:, 0:k * n].rearrange("p (k n) -> p k n", k=k), src, dep)])
                wv_ = slot[:, 0:KTin * ncols].rearrange("p (k n) -> p k n", k=KTin)
                for fi in range(nf):
                    p_t, p_b = psum()
                    for kt in range(KTin):
                        r_ap, r_bufs = rhs(kt)
                        mm(p_t[:, 0:N], wv_[:, kt, fi * 128:(fi + 1) * 128], r_ap, kt == 0, kt == KTin - 1,
                           [sbuf_] + r_bufs, [p_b])
                    evac(f0 + fi, p_t[:, 0:N], p_b)
                    yield

        allc = [list(range(NCs))]
        for (wn, wr, wc) in (WSPEC if c.WG else []):
            dma("sp", wbnc[wn].ap(), wext[wn], [], [bf("b_" + wn)], "wb_" + wn)
            E.op("pool", lambda e, a=wbnc[wn], o=wfull[wn]: e.collective_compute(
                "AllGather", ALU.bypass, replica_groups=allc, ins=[a.ap().opt()], outs=[o.ap().opt()]),
                [bf("b_" + wn)], [bf("g_" + wn)], dma="cc_" + wn)
        E.op("pool", lambda e: e.memset(identf[:], 0.0), [], [bf("identf")])
        E.op("pool", lambda e: e.affine_select(out=identf[:], in_=identf[:], pattern=[[-1, 128]], compare_op=ALU.not_equal,
                                               fill=1.0, base=0, channel_multiplier=1), [bf("identf")], [bf("identf")])
        E.op("pool", lambda e: e.tensor_copy(out=identb[:], in_=identf[:]), [bf("identf")], [bf("identb")])
        E.op("pool", lambda e: e.memset(onesb[:], 1.0), [], [bf("onesb")])
        E.op("pool", lambda e: e.memset(onesD[:], 1.0 / D), [], [bf("onesD")])
        E.op("pool", lambda e: e.memset(onesV[:], 1.0 / 128), [], [bf("onesV")])
        E.op("pool", lambda e: e.memset(onesf[:], 1.0), [], [bf("onesf")])
        E.op("pool", lambda e: e.memset(tri[:], 1.0), [], [bf("tri")])
        E.op("pool", lambda e: e.affine_select(out=tri[:], in_=tri[:], pattern=[[1, 128]], compare_op=ALU.is_ge,
                                               fill=0.0, base=0, channel_multiplier=-1), [bf("tri")], [bf("tri")])
        E.op("pool", lambda e: e.memset(sel127[:], 1.0), [], [bf("sel127")])
        E.op("pool", lambda e: e.affine_select(out=sel127[:], in_=sel127[:], pattern=[[0, 128]], compare_op=ALU.is_ge,
                                               fill=0.0, base=-127, channel_multiplier=1), [bf("sel127")], [bf("sel127")])
        E.op("pool", lambda e: e.memset(neg1[:], 0.0), [], [bf("neg1")])
        E.op("pool", lambda e: e.affine_select(out=neg1[:], in_=neg1[:], pattern=[[-1, 128]], compare_op=ALU.is_gt,
                                               fill=NEGBIG, base=0, channel_multiplier=1), [bf("neg1")], [bf("neg1")])
        E.op("pool", lambda e: e.memset(neg2[:], 0.0), [], [bf("neg2")])
        E.op("pool", lambda e: e.affine_select(out=neg2[:], in_=neg2[:], pattern=[[1, 128]], compare_op=ALU.is_ge,
                                               fill=NEGBIG, base=0, channel_multiplier=-1), [bf("neg2")], [bf("neg2")])
        E.op("pool", lambda e: e.memset(upad[:], 0.0), [], [bf(f"upad#{hb}") for hb in range(2 if H >= 8 else 1)])

        dma("sp", vecs[:], vecs_d, [], [bf("vecs")], "c_vecs")
        dma("sp", rowv[:], rowv_d, [], [bf("rowv")], "c_rowv")
        dma("sp", bc8[:], bc8_d, [], [bf("bc8")], "c_bc8")
        dma("sp", masks[:], masks_d, [], [bf("masks")], "c_masks")
        dma("sp", bmask[:], bmask_d, [], [bf("bmask")], "c_bmask")
        dma("sp", cTf[:], cT_d, [], [bf("cTf")], "c_cT")
        dma("sp", wsTf[:], wsT_d.rearrange("g s t -> s g t"), [], [bf("wsTf")], "c_wsT")
        dma("pool", wab[:], win_d[:, c.OFF_AB:c.OFF_AB + 2 * H].rearrange("(kt p) n -> p kt n", p=128), [bf("g_w_in")], [bf("wab")],
            "c_wab")

        act(ealog[:], bc8[:, 0:H], AF.Exp, [bf("bc8")], [bf("ealog")])
        act(cact[:], cTf[:], AF.Silu, [bf("cTf")], [bf("cact")])

        chk(1)
        def mod_evac(ft, p_ap, p_b):
            tt(modT[:, ft:ft + 1], p_ap, vecs[:, c.V_BADA + ft:c.V_BADA + ft + 1], ALU.add, [p_b, bf("vecs")], [bf("modT")])

        linear_fm(w_ada_d, 0, 9 * KT, KT, lambda kt: (cact[:, kt:kt + 1], [bf("cact")]), 1, mod_evac)
        for i, (vn, half) in enumerate(((c.V_N1, 0.5), (c.V_N2, 1.0), (c.V_N3, 0.5))):
            stt(gsc[:, i * KT:(i + 1) * KT], modT[:, (3 * i + 1) * KT:(3 * i + 2) * KT], 1.0, vecs[:, vn:vn + KT],
                ALU.add, ALU.mult, [bf("modT"), bf("vecs")], [bf("gsc")])
            ts(hga[:, i * KT:(i + 1) * KT], modT[:, (3 * i + 2) * KT:(3 * i + 3) * KT], half, None, ALU.mult, None,
               [bf("modT")], [bf("hga")])

        tt(WmT[:], wsTf[:], bc_mid(tri[:], KT), ALU.mult, [bf("wsTf"), bf("tri")], [bf("WmT")])
        WmT_flat = WmT[:].rearrange("p g t -> p (g t)")
        for o0 in range(0, KT * 128, 512):
            n = min(512, KT * 128 - o0)
            p_t, p_b = psum()
            mm(p_t[0:1, 0:n], onesb[:, 0:1], WmT_flat[:, o0:o0 + n], True, True, [bf("onesb"), bf("WmT")], [p_b])
            cp(rsrow[0:1, o0:o0 + n], p_t[0:1, 0:n], [p_b], [bf("rsrow")])
        for g in range(KT):
            p_t, p_b = psum()
            mm(p_t[:, 0:128], rowv[0:1, g * 128:(g + 1) * 128], rsrow[0:1, g * 128:(g + 1) * 128], True, False,
               [bf("rowv"), bf("rsrow")], [p_b])
            mm(p_t[:, 0:128], onesf[0:1, 0:128], rowv[0:1, D + g * 128:D + (g + 1) * 128], False, True,
               [bf("rowv"), bf("onesf")], [p_b])
            cp(BiasG[:, g, :], p_t[:, 0:128], [p_b], [bf("BiasG")])

        chk(2)
        E.barrier()

        def rms_stats(xin, xb, N, ones_t, ones_b, out_rstd=rstd, kts=None):
            kts = range(KT) if kts is None else kts
            p_t, p_b = psum()
            kl = list(kts)
            for i, kt in enumerate(kl):
                s = sq[i % 2]
                act(s[:, 0:N], xin(kt), AF.Square, xb, [bf(f"sq{i % 2}")])
                mm(p_t[:, 0:N], ones_t[:], s[:, 0:N], i == 0, i == len(kl) - 1, [bf(f"sq{i % 2}"), ones_b], [p_b])
            act(rt[:, 0:N], p_t[:, 0:N], AF.Ln, [p_b], [bf("rt")], bias=EPS)
            act(out_rstd[:, 0:N], rt[:, 0:N], AF.Exp, [bf("rt")], [bf("rstd")], scale=-0.5)

        def norm_mod(xcur, xb, N, which, dst, dstb):
            rms_stats(lambda kt: xcur[:, kt, 0:N], xb, N, onesD, bf("onesD"))
            for kt in range(KT):
                tmp = ntmp[kt % 2]
                stt(tmp[:, 0:N], xcur[:, kt, 0:N], gsc[:, which * KT + kt:which * KT + kt + 1], rstd[:, 0:N], ALU.mult,
                    ALU.mult, xb + [bf("gsc"), bf("rstd")], [bf(f"ntmp{kt % 2}")])
                shc = (3 * which) * KT + kt
                act(dst[:, kt, 0:N], tmp[:, 0:N], AF.Identity, [bf(f"ntmp{kt % 2}"), bf("modT")], dstb,
                    bias=modT[:, shc:shc + 1])

        def ffn(xcur, xb, N, which, win_d_, wout_d_):
            for _ in ffn_gen(xcur, xb, N, which, win_d_, wout_d_):
                pass

        def ffn_gen(xcur, xb, N, which, win_d_, wout_d_):
            norm_mod(xcur, xb, N, which, hT, [bf("hT")])
            yield

            for j0 in range(0, FT, 2):
                nj = min(2, FT - j0)

                base = (j0 // 2 % 2) * 2

                def ev_a2(ft, p_ap, p_b, base=base):
                    act(sa[base + ft][:, 0:N], p_ap, AF.Silu, [p_b], [bf(f"sa{base + ft}")])

                def ev_b2(ft, p_ap, p_b, base=base, j0=j0):
                    tt(gT[:, j0 + ft, 0:N], sa[base + ft][:, 0:N], p_ap, ALU.mult, [bf(f"sa{base + ft}"), p_b], [bf("gT")])

                yield from linear_fm_gen(win_d_, j0 * 128, nj, KT, lambda kt: (hT[:, kt, 0:N], [bf("hT")]), N, ev_a2)
                yield from linear_fm_gen(win_d_, DFF + j0 * 128, nj, KT, lambda kt: (hT[:, kt, 0:N], [bf("hT")]), N, ev_b2)

            def ev_out(ft, p_ap, p_b):
                stt(xcur[:, ft, 0:N], p_ap, hga[:, which * KT + ft:which * KT + ft + 1], xcur[:, ft, 0:N], ALU.mult, ALU.add,
                    [p_b, bf("hga")] + xb, xb)

            yield from linear_fm_gen(wout_d_, 0, KT, FT, lambda kt: (gT[:, kt, 0:N], [bf("gT")]), N, ev_out, CW=1)

        _sfb = [bf(f"Sf#{hb}") for hb in range(2 if H >= 8 else 1)]
        _sbb = [bf(f"Sb#{hb}") for hb in range(2 if H >= 8 else 1)]
        E.op("dve", lambda e: e.memset(Sf[:], 0.0), [], _sfb)
        cp(Sf[:, :, 128:256], bc_mid(identf[:], H), [bf("identf")] + _sfb, _sfb)
        cp(Sb[:], Sf[:], _sfb, _sbb, eng="act")

        def qkv_conv(N, xb_h2, is_halo, vm=None, own=True, q_tails_only=False):
            ft_off = 0 if own else KT
            pending = []

            def flush():
                bufsets = [(rt, "rt", rstd, "rstd"), (ntmp[0], "ntmp0", ntmp[1], "ntmp1")]
                banks = []
                for i_, (ft_, h_, isq_, qr_, qrb_) in enumerate(pending):
                    p2, p2b = psum()
                    s_ = sq[i_]
                    act(s_[:, 0:N], qr_[:, 0:N], AF.Square, [qrb_], [bf(f"sq{i_}")])
                    mm(p2[:, 0:N], onesb[:], s_[:, 0:N], True, True, [bf(f"sq{i_}"), bf("onesb")], [p2b])
                    banks.append((p2, p2b))
                for i_ in range(len(pending)):
                    a_, an_, b_, bn_ = bufsets[i_]
                    act(a_[:, 0:N], banks[i_][0][:, 0:N], AF.Ln, [banks[i_][1]], [bf(an_)], bias=EPS)
                for i_ in range(len(pending)):
                    a_, an_, b_, bn_ = bufsets[i_]
                    act(b_[:, 0:N], a_[:, 0:N], AF.Exp, [bf(an_)], [bf(bn_)], scale=-0.5)
                for i_, (ft_, h_, isq_, qr_, qrb_) in enumerate(pending):
                    a_, an_, b_, bn_ = bufsets[i_]
                    dst = qnT if isq_ else knT
                    stt(dst[:, h_, 0:N], qr_[:, 0:N], (128.0 ** -0.5) if isq_ else 1.0, b_[:, 0:N], ALU.mult, ALU.mult,
                        [qrb_, bf(bn_)], [bf("qnT" if isq_ else "knT")])
                pending.clear()

            def ev_pre(ft, p_ap, p_b):
                ft = ft + ft_off
                if is_halo:
                    ts(tails[:, ft, :], p_ap, masks[:, 0:1], None, ALU.mult, None, [p_b, bf("masks")], [bf("tails")])
                    return
                slot = pre[ft % 3]
                slb = bf(f"pre{ft % 3}")
                cp(slot[:, 0:4], tails[:, ft, :], [bf("tails")], [slb], eng="act")
                if vm is not None:
                    ts(slot[:, 4:4 + N], p_ap, vm, None, ALU.mult, None, [p_b, bf("masks")], [slb])
                else:
                    cp(slot[:, 4:4 + N], p_ap, [p_b], [slb], eng=("act" if ft % 2 else "dve"))
                cp(tails[:, ft, :], slot[:, N:N + 4], [slb], [bf("tails")])
                if q_tails_only and ft < KT:
                    return
                ds = dslot[ft % 2]
                dsb = bf(f"dslot{ft % 2}")
                for j in range(4):
                    col = c.V_CONV + j * 3 * KT + ft
                    act(ds[:, j, :], identb[:], AF.Copy, [bf("identb"), bf("vecs")], [dsb], scale=vecs[:, col:col + 1])
                p_t, p_b2 = psum()
                for j in range(4):
                    mm(p_t[:, 0:N], ds[:, j, :], slot[:, 1 + j:1 + j + N], j == 0, j == 3, [dsb, slb], [p_b2])
                if ft < 2 * KT:
                    h = ft % KT
                    isq = ft < KT
                    qr = qraw[ft % 2]
                    qrb = bf(f"qraw{ft % 2}")
                    act(qr[:, 0:N], p_t[:, 0:N], AF.Silu, [p_b2], [qrb])
                    pending.append((ft, h, isq, qr, qrb))
                    if len(pending) == 2:
                        flush()
                else:
                    act(vT[:, ft - 2 * KT, 0:N], p_t[:, 0:N], AF.Silu, [p_b2], [bf("vT")])

            linear_fm(win_d, c.OFF_Q + ft_off * 128, 3 * KT - ft_off, KT, lambda kt: (h2[:, kt, 0:N], xb_h2), N, ev_pre)
            if pending:
                flush()

        HB = 4 if H >= 4 else H
        NHB = H // HB

        FILL = [None]

        def fill(k=1):
            g_ = FILL[0]
            if g_ is None:
                return
            for _ in range(k):
                try:
                    next(g_)
                except StopIteration:
                    FILL[0] = None
                    return

        def ab_proj(c0, ci):
            csl = slice(c0, c0 + 128)
            p_t, p_b = psum()
            for kt in range(KT):
                mm(p_t[:, 0:2 * H], h2[:, kt, csl], wab[:, kt, :], kt == 0, kt == KT - 1, [bf("h2"), bf("wab")], [p_b])
            cp(abT[:, ci, :], p_t[:, 0:2 * H], [p_b], [bf("abT")])

        def small_chain_tb(vm):
            Z = lambda n: smT[n][:]
            zB = lambda n: bf("smT_" + n)
            abv = abT[:]
            a_ap = abv[:, :, 0:H]
            b_ap = abv[:, :, H:2 * H]
            v3 = lambda n: smT[n][:].rearrange("p (c h) -> p c h", c=CPB)
            act(v3("e"), b_ap, AF.Exp, [bf("abT")], [zB("e")], scale=-1.0)
            ts(Z("e"), Z("e"), 1.0, None, ALU.add, None, [zB("e")], [zB("e")])
            recip(Z("beta"), Z("e"), [zB("e")], [zB("beta")])
            if vm is not None:
                ts(Z("beta"), Z("beta"), vm, None, ALU.mult, None, [zB("beta"), bf("masks")], [zB("beta")])
            tt(v3("x"), a_ap, bc_mid(bc8[:, H:2 * H], CPB), ALU.add, [bf("abT"), bf("bc8")], [zB("x")])
            act(Z("ex"), Z("x"), AF.Exp, [zB("x")], [zB("ex")])
            act(Z("sp"), Z("ex"), AF.Ln, [zB("ex")], [zB("sp")], bias=1.0)
            stt(v3("g"), v3("sp"), -1.0, bc_mid(ealog[:], CPB), ALU.mult, ALU.mult, [zB("sp"), bf("ealog")], [zB("g")])
            p_t, p_b = psum()
            mm(p_t[:, 0:CPB * H], tri[:], Z("g"), True, True, [bf("tri"), zB("g")], [p_b])
            cp(Z("CB"), p_t[:, 0:CPB * H], [p_b], [zB("CB")])
            act(Z("eCB"), Z("CB"), AF.Exp, [zB("CB")], [zB("eCB")])
            tt(Z("kbs"), Z("eCB"), Z("beta"), ALU.mult, [zB("eCB"), zB("beta")], [zB("kbs")])
            p_t, p_b = psum()
            mm(p_t[:, 0:CPB * H], sel127[:], Z("CB"), True, True, [bf("sel127"), zB("CB")], [p_b])
            tt(Z("dka"), p_t[:, 0:CPB * H], Z("CB"), ALU.subtract, [p_b, zB("CB")], [zB("dka")])
            act(Z("gle"), p_t[:, 0:CPB * H], AF.Exp, [p_b], [zB("gle")])
            act(Z("dke"), Z("dka"), AF.Exp, [zB("dka")], [zB("dke")])

        def chunk_solve(c0, gci, own=True, vm=None, ci=None):
            csl = slice(c0, c0 + 128)
            assert ci is not None
            sm = {n: smT[n][:, ci * H:(ci + 1) * H] for n in smT}
            S = lambda n: sm[n]
            sB = lambda n: bf("smT_" + n)
            fill()

            HS = [slice(hb * HB, (hb + 1) * HB) for hb in range(NHB)]
            hbf = lambda name, hb: bf(f"{name}#{hb}")
            f2 = lambda t, hb: t[:, HS[hb], :].rearrange("p h f -> p (h f)")
            W_ = HB * 128

            def each(fn):
                for hb in range(NHB):
                    fn(hb)
                fill()

            def s_rb(hb):
                hs = HS[hb]
                tt(dg[:, hs, :], bc_mid(identf[:], HB), bc_last(sm["CB"][:, hs], 128), ALU.mult, [bf("identf"), sB("CB")], [hbf("tw", hb)])
                p_t, p_b = psum()
                mm(p_t[:, 0:W_], onesf[:], f2(dg, hb), True, True, [bf("onesf"), hbf("tw", hb)], [p_b])
                cp(f2(RB, hb), p_t[:, 0:W_], [p_b], [hbf("RB", hb)], eng="act")
            each(s_rb)

            def s_e1(hb):
                hs = HS[hb]
                tt(t1[:, hs, :], bc_mid(neg1[:], HB), RB[:, hs, :], ALU.subtract, [bf("neg1"), hbf("RB", hb)], [hbf("tw", hb)])
                tt(t1[:, hs, :], t1[:, hs, :], bc_last(sm["CB"][:, hs], 128), ALU.add, [hbf("tw", hb), sB("CB")], [hbf("tw", hb)])
                act(E1[:, hs, :], t1[:, hs, :], AF.Exp, [hbf("tw", hb)], [hbf("E1", hb)])
                tt(E1[:, hs, :], E1[:, hs, :], bc_last(sm["beta"][:, hs], 128), ALU.mult, [hbf("E1", hb), sB("beta")], [hbf("E1", hb)])
            each(s_e1)

            def s_gram(hb):
                pG, pGb = psum()
                for hh in range(HB):
                    h = hb * HB + hh
                    mm(pG[:, hh * 128:(hh + 1) * 128], knT[:, h, csl], knT[:, h, csl], True, True, [bf("knT")], [pGb])
                tt(f2(Afull, hb), pG[:, 0:W_], f2(E1, hb), ALU.mult, [pGb, hbf("E1", hb)], [hbf("Afull", hb)])
            each(s_gram)
            if own:
                def s_e2(hb):
                    hs = HS[hb]
                    tt(t2[:, hs, :], RB[:, hs, :], bc_mid(neg2[:], HB), ALU.add, [bf("neg2"), hbf("RB", hb)], [hbf("tw", hb)])
                    tt(t2[:, hs, :], t2[:, hs, :], bc_last(sm["CB"][:, hs], 128), ALU.subtract, [hbf("tw", hb), sB("CB")], [hbf("tw", hb)])
                    act(E2[:, hs, :], t2[:, hs, :], AF.Exp, [hbf("tw", hb)], [hbf("E2", hb)])
                    act(Eg[:, hs, :], RB[:, hs, :], AF.Exp, [hbf("RB", hb)], [hbf("Eg", hb)])
                    pQ, pQb = psum()
                    for hh in range(HB):
                        h = hb * HB + hh
                        mm(pQ[:, hh * 128:(hh + 1) * 128], knT[:, h, csl], qnT[:, h, csl], True, True, [bf("knT"), bf("qnT")], [pQb])
                    tt(f2(qkT, hb), pQ[:, 0:W_], f2(E2, hb), ALU.mult, [pQb, hbf("E2", hb)], [hbf("qkT", hb)])
                    tt(qdT[:, hs, :], qnT[:, hs, csl], Eg[:, hs, :], ALU.mult, [bf("qnT"), hbf("Eg", hb)], [hbf("qdT", hb)])
                each(s_e2)

            mk = lambda i: bc_mid(bmask[:, i, :], HB)

            def transp(src, srcb_fn, hb):
                pT, pTb = psum()
                pTv = pT[:].bitcast(BF16)
                for hh in range(HB):
                    h = hb * HB + hh
                    tr(pTv[:, hh * 128:(hh + 1) * 128], src(h), identb[:], [srcb_fn(hb), bf("identb")], [pTb])
                return pTv, pTb

            def s_at(hb):
                hs = HS[hb]
                pTv, pTb = transp(lambda h: Afull[:, h, :], lambda hb_: hbf("Afull", hb_), hb)
                cp(f2(ATfull, hb), pTv[:, 0:W_], [pTb], [hbf("ATfull", hb)], eng="act")
                tt(Nb[0][:, hs, :], Afull[:, hs, :], mk(0), ALU.mult, [hbf("Afull", hb), bf("bmask")], [hbf("Nb0", hb)])
                tt(NTb[0][:, hs, :], ATfull[:, hs, :], mk(0), ALU.mult, [hbf("ATfull", hb), bf("bmask")], [hbf("NTb0", hb)])
                tt(Ub[0][:, hs, :], bc_mid(identb[:], HB), NTb[0][:, hs, :], ALU.subtract, [hbf("NTb0", hb), bf("identb")], [hbf("Ub0", hb)])
            each(s_at)
            cur = 0
            NLEV = 3
            for k in range(1, NLEV + 1):
                nxt = 1 - cur

                def s_lev(hb, cur=cur, nxt=nxt, k=k):
                    pN, pNb = psum()
                    for hh in range(HB):
                        h = hb * HB + hh
                        mm(pN[:, hh * 128:(hh + 1) * 128], NTb[cur][:, h, :], Nb[cur][:, h, :], True, True,
                           [hbf(f"NTb{cur}", hb), hbf(f"Nb{cur}", hb)], [pNb])
                    cp(f2(Nb[nxt], hb), pN[:, 0:W_], [pNb], [hbf(f"Nb{nxt}", hb)], eng="act")
                    if k < NLEV:
                        pM, pMb = psum()
                        for hh in range(HB):
                            h = hb * HB + hh
                            mm(pM[:, hh * 128:(hh + 1) * 128], Nb[cur][:, h, :], NTb[cur][:, h, :], True, True,
                               [hbf(f"NTb{cur}", hb), hbf(f"Nb{cur}", hb)], [pMb])
                        cp(f2(NTb[nxt], hb), pM[:, 0:W_], [pMb], [hbf(f"NTb{nxt}", hb)])
                    pU, pUb = psum()
                    for hh in range(HB):
                        h = hb * HB + hh
                        mm(pU[:, hh * 128:(hh + 1) * 128], Nb[nxt][:, h, :], Ub[cur][:, h, :], True, True,
                           [hbf(f"Nb{nxt}", hb), hbf(f"Ub{cur}", hb)], [pUb])
                    tt(f2(Ub[nxt], hb), f2(Ub[cur], hb), pU[:, 0:W_], ALU.add, [pUb, hbf(f"Ub{cur}", hb)], [hbf(f"Ub{nxt}", hb)])
                each(s_lev)
                cur = nxt

            def s_td(hb, cur=cur):
                pTv, pTb = transp(lambda h: Ub[cur][:, h, :], lambda hb_: hbf(f"Ub{cur}", hb_), hb)
                cp(f2(Tb[0], hb), pTv[:, 0:W_], [pTb], [hbf("Tb0", hb)], eng="act")
            each(s_td)
            tcur = 0
            for i in range(3):
                nxt = 1 - cur
                tnx = 1 - tcur
                lastm = (i == 2)

                def s_mrg(hb, i=i, cur=cur, nxt=nxt, tcur=tcur, tnx=tnx, lastm=lastm):
                    hs = HS[hb]
                    tt(As1[:, hs, :], Afull[:, hs, :], mk(1 + i), ALU.mult, [hbf("Afull", hb), bf("bmask")], [hbf("As", hb)])
                    pW, pWb = psum()
                    for hh in range(HB):
                        h = hb * HB + hh
                        mm(pW[:, hh * 128:(hh + 1) * 128], As1[:, h, :], Ub[cur][:, h, :], True, True,
                           [hbf("As", hb), hbf(f"Ub{cur}", hb)], [pWb])
                    cp(f2(Wb, hb), pW[:, 0:W_], [pWb], [hbf("Wb", hb)], eng="act")
                    if not lastm:
                        tt(Ms1[:, hs, :], ATfull[:, hs, :], mk(1 + i), ALU.mult, [hbf("ATfull", hb), bf("bmask")], [hbf("Ms", hb)])
                        pV, pVb = psum()
                        for hh in range(HB):
                            h = hb * HB + hh
                            mm(pV[:, hh * 128:(hh + 1) * 128], Ms1[:, h, :], Tb[tcur][:, h, :], True, True,
                               [hbf("Ms", hb), hbf(f"Tb{tcur}", hb)], [pVb])
                        cp(f2(Vb, hb), pV[:, 0:W_], [pVb], [hbf("Vb", hb)])
                    pU, pUb = psum()
                    for hh in range(HB):
                        h = hb * HB + hh
                        mm(pU[:, hh * 128:(hh + 1) * 128], Tb[tcur][:, h, :], Wb[:, h, :], True, True,
                           [hbf(f"Tb{tcur}", hb), hbf("Wb", hb)], [pUb])
                    tt(f2(Ub[nxt], hb), f2(Ub[cur], hb), pU[:, 0:W_], ALU.subtract, [pUb, hbf(f"Ub{cur}", hb)], [hbf(f"Ub{nxt}", hb)])
                    if not lastm:
                        pX, pXb = psum()
                        for hh in range(HB):
                            h = hb * HB + hh
                            mm(pX[:, hh * 128:(hh + 1) * 128], Ub[cur][:, h, :], Vb[:, h, :], True, True,
                               [hbf(f"Ub{cur}", hb), hbf("Vb", hb)], [pXb])
                        tt(f2(Tb[tnx], hb), f2(Tb[tcur], hb), pX[:, 0:W_], ALU.subtract, [pXb, hbf(f"Tb{tcur}", hb)], [hbf(f"Tb{tnx}", hb)])
                each(s_mrg)
                cur = nxt
                tcur = tnx
            U = Ub[cur]
            Ubn = f"Ub{cur}"

            def s_kv(hb):
                hs = HS[hb]
                pTv, pTb = transp(lambda h: knT[:, h, csl], lambda hb_: bf("knT"), hb)
                pT3 = pTv[:, 0:W_].rearrange("p (h f) -> p h f", h=HB)
                tt(kd[:, hs, :], pT3, bc_last(sm["dke"][:, hs], 128), ALU.mult, [pTb, sB("dke")], [hbf("kd", hb)])
                tt(kbg[:, hs, :], pT3, bc_last(sm["kbs"][:, hs], 128), ALU.mult, [pTb, sB("kbs")], [hbf("kbg", hb)])
                pTv, pTb = transp(lambda h: vT[:, h, csl], lambda hb_: bf("vT"), hb)
                pT3 = pTv[:, 0:W_].rearrange("p (h f) -> p h f", h=HB)
                tt(vb[:, hs, :], pT3, bc_last(sm["beta"][:, hs], 128), ALU.mult, [pTb, sB("beta")], [hbf("vb", hb)])
            each(s_kv)

            def s_uw(hb):
                hs = HS[hb]
                pu, pub = psum()
                pw, pwb = psum()
                for hh in range(HB):
                    h = hb * HB + hh
                    mm(pu[:, hh * 128:(hh + 1) * 128], U[:, h, :], vb[:, h, :], True, True, [hbf(Ubn, hb), hbf("vb", hb)], [pub])
                    mm(pw[:, hh * 128:(hh + 1) * 128], kbg[:, h, :], U[:, h, :], True, True, [hbf(Ubn, hb), hbf("kbg", hb)], [pwb])
                cp(upad[:, hs, 0:128], pu[:, 0:W_].rearrange("p (h f) -> p h f", h=HB), [pub], [hbf("upad", hb)], eng="act")
                cp(f2(wT, hb), pw[:, 0:W_], [pwb], [hbf("wT", hb)])
            each(s_uw)

            so = ostage[gci % 2]
            sr = rstage[gci % 2]
            sob, srb = bf(f"ostage{gci % 2}"), bf(f"rstage{gci % 2}")

            def s_vn(hb):
                for h2i in range(hb * HB, (hb + 1) * HB, 2):
                    p1, p1b = psum()
                    for hh in range(2):
                        h = h2i + hh
                        mm(p1[:, hh * 256:(hh + 1) * 256], wT[:, h, :], Sb[:, h, :], True, True, [hbf("wT", hb), hbf("Sb", hb)], [p1b])
                    tt(vnew[:, h2i:h2i + 2, :].rearrange("p h f -> p (h f)"), upad[:, h2i:h2i + 2, :].rearrange("p h f -> p (h f)"),
                       p1[:, 0:512], ALU.subtract, [p1b, hbf("upad", hb)], [hbf("vnew", hb)])
            each(s_vn)
            if own:
                def s_o(hb):
                    hs = HS[hb]
                    po, pob = psum()
                    for hh in range(HB):
                        h = hb * HB + hh
                        mm(po[:, hh * 128:(hh + 1) * 128], Sb[:, h, 0:128], qdT[:, h, :], True, False, [hbf("Sb", hb), hbf("qdT", hb)], [pob])
                        mm(po[:, hh * 128:(hh + 1) * 128], vnew[:, h, 0:128], qkT[:, h, :], False, True, [hbf("vnew", hb), hbf("qkT", hb)], [pob])
                    if FUSED:
                        cp(oTtb[:, hs, csl], po[:, 0:W_].rearrange("p (h f) -> p h f", h=HB), [pob], [bf("oTtb")], eng="act")
                        return
                    pr, prb = psum()
                    for hh in range(HB):
                        h = hb * HB + hh
                        mm(pr[:, hh * 128:(hh + 1) * 128], Sb[:, h, 128:256], qdT[:, h, :], True, False, [hbf("Sb", hb), hbf("qdT", hb)], [prb])
                        mm(pr[:, hh * 128:(hh + 1) * 128], vnew[:, h, 128:256], qkT[:, h, :], False, True, [hbf("vnew", hb), hbf("qkT", hb)], [prb])
                    cp(f2(so, hb), po[:, 0:W_], [pob], [sob], eng="act")
                    cp(f2(sr, hb), pr[:, 0:W_], [prb], [srb], eng="act")
                each(s_o)

            def s_st(hb):
                hs = HS[hb]
                for h2i in range(hb * HB, (hb + 1) * HB, 2):
                    p3, p3b = psum()
                    for hh in range(2):
                        h = h2i + hh
                        mm(p3[:, hh * 256:(hh + 1) * 256], kd[:, h, :], vnew[:, h, :], True, True, [hbf("kd", hb), hbf("vnew", hb)], [p3b])
                    for hh in range(2):
                        h = h2i + hh
                        stt(Sf[:, h, :], Sf[:, h, :], sm["gle"][:, h:h + 1], p3[:, hh * 256:(hh + 1) * 256], ALU.mult, ALU.add,
                            [p3b, hbf("Sf", hb), sB("gle")], [hbf("Sf", hb)])
                cp(Sb[:, hs, :], Sf[:, hs, :], [hbf("Sf", hb)], [hbf("Sb", hb)], eng="act")
            each(s_st)
            if not FUSED:
                dma("sp", os_d[gci], so[:].rearrange("p h f -> p (h f)"), [sob], [bf(f"os{gci}")], f"ost{gci % 2}")
                dma("sp", rs_d[gci], sr[:].rearrange("p h f -> p (h f)"), [srb], [bf(f"rs{gci}")], f"rst{gci % 2}")

        blocks = [(0, 4, True)] + [(4 + i * TB, TB, False) for i in range(NB)]
        for bi, (col0, N, is_halo) in enumerate(blocks if part in (0, 1) else []):
            xi = bi % 2
            xcur = xt[xi]
            xb = [bf("xt0")]
            dma("sp", xcur[:, :, 0:N], xT_d[:, :, col0:col0 + N], [], xb, f"xld{xi}")
            ffn(xcur, xb, N, 0, f1in_d, f1out_d)
            norm_mod(xcur, xb, N, 1, h2, [bf("h2")])
            if not is_halo:
                t0 = col0 - 4
                dma("sp", x1s_d[:, :, t0:t0 + N], xcur[:, :, 0:N], xb, [bf(f"x1s{bi}")], f"x1st{xi}")
                dma("sp", h2s_d[:, :, t0:t0 + N], h2[:, :, 0:N], [bf("h2")], [bf(f"h2s{bi}")], "h2st")
            E.barrier()
            qkv_conv(N, [bf("h2")], is_halo)
            if is_halo:
                chk(4)
            else:
                chk(5)
            if not is_halo:
                for ci in range(CPB):
                    chunk_solve(ci * 128, (bi - 1) * CPB + ci)
                    chk(6)
            E.barrier()

        chk(7)
        if part in (0, 1):
            dma("sp", st_in_t.ap(), Sf[:].rearrange("p h f -> p (h f)"), [bf("Sf")], [bf("st_in")], "stio")
        if part == 1:
            raise Stop()
        if G > 1 and part == 0:
            groups = [list(range(g0, g0 + G)) for g0 in range(0, c.NCORES, G)]
            E.op("pool", lambda e: e.collective_compute("AllGather", ALU.bypass, replica_groups=groups,
                                                        ins=[st_in_t.ap().opt()], outs=[st_all_t.ap().opt()]),
                 [bf("st_in")], [bf("st_all")], dma="ccsem")
        if not FUSED:
            E.op("dve", lambda e: e.memset(Sst[:], 0.0), [], [bf("Sst")])
        for i in (range(G - 1) if not FUSED else []):
            dma("sp", stg[i][:].rearrange("p h f -> p (h f)"), st_all_t.ap()[i * 128:(i + 1) * 128, :], [bf("st_all")],
                [bf("stg0")], f"stgl{i}")
            for hb in range(NHB):
                pT, pTb = psum()
                for hh in range(HB):
                    h = hb * HB + hh
                    tr(pT[:, hh * 128:(hh + 1) * 128], stg[i][:, h, 128:256], identf[:], [bf("stg0"), bf("identf")], [pTb])
                cp(PiT[:, hb * HB:(hb + 1) * HB, :].rearrange("p h f -> p (h f)"), pT[:, 0:HB * 128], [pTb], [bf("PiT")])
            for hb in range(NHB):
                hs = slice(hb * HB, (hb + 1) * HB)
                pc, pcb = psum()
                for hh in range(HB):
                    h = hb * HB + hh
                    mm(pc[:, hh * 128:(hh + 1) * 128], PiT[:, h, :], Sst[:, h, :], True, True, [bf("PiT"), bf("Sst")], [pcb])
                tt(cand[:, hs, :], pc[:, 0:HB * 128].rearrange("p (h f) -> p h f", h=HB), stg[i][:, hs, 0:128], ALU.add,
                   [pcb, bf("stg0")], [bf("cand")])
            tt(cand[:], cand[:], Sst[:], ALU.subtract, [bf("cand"), bf("Sst")], [bf("cand")])
            stt(Sst[:], cand[:], masks[:, 1 + i:2 + i], Sst[:], ALU.mult, ALU.add, [bf("cand"), bf("masks"), bf("Sst")],
                [bf("Sst")])
        if not FUSED:
            cp(Sstb[:], Sst[:], [bf("Sst")], [bf("Sstb")])
        E.barrier()

        chk(8)
        def phase2_tb(tbi):
            t0 = tbi * TB
            N = TB
            xi = tbi % 2
            xcur = xt[xi]
            xb = [bf("xt0")]
            if not FUSED:
                dma("sp", xcur[:], x1s_d[:, :, t0:t0 + N], [bf(f"x1s{tbi + 1}")], xb, f"xld{xi}")
                dma("sp", h2[:], h2s_d[:, :, t0:t0 + N], [bf(f"h2s{tbi + 1}")], [bf("h2")], "h2ld")
            dma("pool", wv[:], win_d[:, c.OFF_V:c.OFF_V + D].rearrange("(kt p) n -> p kt n", p=128), [bf("g_w_in")], [bf("wv")], "wvld")
            h2r = lambda kt: (h2[:, kt, 0:N], [bf("h2")])

            linear_fm(win_d, c.OFF_Z, KT, KT, h2r, N,
                      lambda ft, p_ap, p_b: act(zs[:, ft, :], p_ap, AF.Silu, [p_b], [bf("zs")]))
            linear_fm(win_d, c.OFF_U, KT, KT, h2r, N,
                      lambda ft, p_ap, p_b: act(ug[:, ft, :], p_ap, AF.Gelu, [p_b], [bf("ug")]))
            def o_part(ci):
                gci = tbi * CPB + ci
                csl = slice(ci * 128, (ci + 1) * 128)
                so, sr = ostage[gci % 2], rstage[gci % 2]
                sob, srb = bf(f"ostage{gci % 2}"), bf(f"rstage{gci % 2}")
                if FUSED:
                    cp(otrue[:], oTtb[:, :, csl], [bf("oTtb")], [bf("otrue")])
                else:
                    dma("sp", so[:].rearrange("p h f -> p (h f)"), os_d[gci], [bf(f"os{gci}")], [sob], f"old{gci % 2}")
                    dma("sp", sr[:].rearrange("p h f -> p (h f)"), rs_d[gci], [bf(f"rs{gci}")], [srb], f"rld{gci % 2}")
                for hb in (range(NHB) if not FUSED else []):
                    hs = slice(hb * HB, (hb + 1) * HB)
                    pc, pcb = psum()
                    for hh in range(HB):
                        h = hb * HB + hh
                        mm(pc[:, hh * 128:(hh + 1) * 128], Sstb[:, h, :], sr[:, h, :], True, True, [bf("Sstb"), srb], [pcb])
                    tt(otrue[:, hs, :], pc[:, 0:HB * 128].rearrange("p (h f) -> p h f", h=HB), so[:, hs, :], ALU.add,
                       [pcb, sob], [bf("otrue")])
                act(osq[:], otrue[:], AF.Square, [bf("otrue")], [bf("osq")])
                yield
                for hb in range(NHB):
                    hs = slice(hb * HB, (hb + 1) * HB)
                    pn, pnb = psum()
                    mm(pn[:, 0:HB * 128], onesV[:], osq[:, hs, :].rearrange("p h f -> p (h f)"), True, True,
                       [bf("onesV"), bf("osq")], [pnb])
                    act(ort[:, hs, :].rearrange("p h f -> p (h f)"), pn[:, 0:HB * 128], AF.Ln, [pnb], [bf("ort")], bias=EPS)
                yield
                act(ors[:], ort[:], AF.Exp, [bf("ort")], [bf("ort")], scale=-0.5)
                yield
                tt(otrue[:], otrue[:], ors[:], ALU.mult, [bf("otrue"), bf("ort")], [bf("otrue")])
                yield
                stt(obT[:, :, csl], otrue[:], vecs[:, c.V_DNG:c.V_DNG + 1], zs[:, :, csl], ALU.mult, ALU.mult,
                    [bf("otrue"), bf("vecs"), bf("zs")], [bf("obT")])
            def g_part(ci):
                csl = slice(ci * 128, (ci + 1) * 128)
                for o0 in range(0, D, 512):
                    n = min(512, D - o0)
                    p_t, p_b = psum()
                    for kt in range(KT):
                        mm(p_t[:, 0:n], h2[:, kt, csl], wv[:, kt, o0:o0 + n], kt == 0, kt == KT - 1, [bf("h2"), bf("wv")], [p_b])
                    act(vg[:, o0:o0 + n], p_t[:, 0:n], AF.Gelu, [p_b], [bf("vg")])
                    yield
                SS = lambda n: st[n][:]
                sBB = lambda n: bf("st_" + n)
                E.op("dve", lambda e: e.tensor_reduce(out=st["s1"][:], in_=vg[:], axis=AX.X, op=ALU.add), [bf("vg")], [sBB("s1")])
                tt(vsq[:], vg[:], vg[:], ALU.mult, [bf("vg")], [bf("mtmp")])
                yield
                E.op("dve", lambda e: e.tensor_reduce(out=st["s2"][:], in_=vsq[:], axis=AX.X, op=ALU.add), [bf("mtmp")], [sBB("s2")])
                ts(SS("mean"), SS("s1"), 1.0 / D, None, ALU.mult, None, [sBB("s1")], [sBB("mean")])
                yield
                tt(SS("msq"), SS("mean"), SS("mean"), ALU.mult, [sBB("mean")], [sBB("msq")])
                stt(SS("var"), SS("s2"), 1.0 / D, SS("msq"), ALU.mult, ALU.subtract, [sBB("s2"), sBB("msq")], [sBB("var")])
                yield
                act(SS("sd"), SS("var"), AF.Sqrt, [sBB("var")], [sBB("sd")], bias=EPS)
                recip(SS("rstd"), SS("sd"), [sBB("sd")], [sBB("rstd")])
                yield
                ts(nrm[:], vg[:], SS("mean"), SS("rstd"), ALU.subtract, ALU.mult, [bf("vg"), sBB("mean"), sBB("rstd")], [bf("nrm")])
                yield
                for gb in range(0, KT, 4):
                    ng = min(4, KT - gb)
                    pM, pMb = psum()
                    for gg in range(ng):
                        g = gb + gg
                        mm(pM[:, gg * 128:(gg + 1) * 128], nrm[:, g * 128:(g + 1) * 128], WmT[:, g, :], True, True,
                           [bf("nrm"), bf("WmT")], [pMb])
                    for gg in range(ng):
                        g = gb + gg
                        stt(mtmp[:, g, :], pM[:, gg * 128:(gg + 1) * 128], vecs[:, c.V_LNG + g:c.V_LNG + g + 1], BiasG[:, g, :],
                            ALU.mult, ALU.add, [pMb, bf("vecs"), bf("BiasG")], [bf("mtmp")])
                    yield
                tt(oaT[:, :, csl], mtmp[:], ug[:, :, csl], ALU.mult, [bf("mtmp"), bf("ug")], [bf("oaT")])
            for ci in range(CPB):
                gens = [o_part(ci), g_part(ci)]
                while gens:
                    for g_ in list(gens):
                        try:
                            next(g_)
                        except StopIteration:
                            gens.remove(g_)
            linear_fm(win_d, c.OFF_GATE, 2 * KT, KT, h2r, N,
                      lambda ft, p_ap, p_b: act(gt[:, ft, :], p_ap, AF.Sigmoid, [p_b], [bf("gt")]))
            linear_fm(wbr_d[0], 0, KT, KT, lambda kt: (oaT[:, kt, :], [bf("oaT")]), N,
                      lambda ft, p_ap, p_b: tt(mA[:, ft, :], p_ap, gt[:, ft, :], ALU.mult, [p_b, bf("gt")], [bf("mA")]))

            def ev_brB(ft, p_ap, p_b):
                tt(mB[:], p_ap, gt[:, KT + ft, :], ALU.mult, [p_b, bf("gt")], [bf("mB")])
                tt(mergedT[:, ft, :], mB[:], mA[:, ft, :], ALU.add, [bf("mB"), bf("mA")], [bf("mergedT")])

            linear_fm(wbr_d[1], 0, KT, KT, lambda kt: (obT[:, kt, :], [bf("obT")]), N, ev_brB)
            linear_fm(wout_d, 0, KT, KT, lambda kt: (mergedT[:, kt, :], [bf("mergedT")]), N,
                      lambda ft, p_ap, p_b: stt(xcur[:, ft, :], p_ap, hga[:, KT + ft:KT + ft + 1], xcur[:, ft, :], ALU.mult,
                                                ALU.add, [p_b, bf("hga")] + xb, xb))
            E.barrier()
            ffn(xcur, xb, N, 2, f3in_d, f3out_d)
            rms_stats(lambda kt: xcur[:, kt, 0:N], xb, N, onesD, bf("onesD"))
            og = ostg[tbi % 2]
            ogb = bf("ostg0")
            for kt in range(KT):
                stt(og[:, kt, :], xcur[:, kt, :], vecs[:, c.V_NF + kt:c.V_NF + kt + 1], rstd[:, 0:N], ALU.mult, ALU.mult,
                    xb + [bf("vecs"), bf("rstd")], [ogb])
            dma("sp", outT_d[:, :, t0:t0 + N], og[:], [ogb], [bf(f"out{tbi}")], f"outst{tbi % 2}")
            E.barrier()

        if not FUSED:
            for tbi in range(NB):
                phase2_tb(tbi)
        else:
            E.op("pool", lambda e: e.memset(tails[:], 0.0), [], [bf("tails")])
            blist = [(wi, tbi) for wi in range(G) for tbi in range(NB)]

            def stageA_gen(wi, tbi):
                col0 = wi * NT + tbi * TB
                xcur = xt[0]
                xb = [bf("xt0")]
                dma("sp", xcur[:, :, 0:TB], xT_d[:, :, col0:col0 + TB], [], xb, "xld0")
                yield from ffn_gen(xcur, xb, TB, 0, f1in_d, f1out_d)
                norm_mod(xcur, xb, TB, 1, h2, [bf("h2")])
                yield

            for _ in stageA_gen(*blist[0]):
                pass
            for bi_, (wi, tbi) in enumerate(blist):
                own = (wi == G - 1)
                vm = None if own else masks[:, wi:wi + 1]
                N = TB
                lastp = (wi == G - 2 and tbi == NB - 1)
                if own:
                    E.barrier()
                qkv_conv(N, [bf("h2")], False, vm=vm, own=(own or lastp), q_tails_only=lastp)
                for ci in range(CPB):
                    ab_proj(ci * 128, ci)
                small_chain_tb(vm)
                nxt_blk = blist[bi_ + 1] if bi_ + 1 < len(blist) else None
                if (not own) and nxt_blk is not None:
                    FILL[0] = stageA_gen(*nxt_blk)
                for ci in range(CPB):
                    chunk_solve(ci * 128, 0, own=own, vm=vm, ci=ci)
                if FILL[0] is not None:
                    for _ in FILL[0]:
                        pass
                    FILL[0] = None
                if own:
                    E.barrier()
                    phase2_tb(tbi)
                    if nxt_blk is not None:
                        for _ in stageA_gen(*nxt_blk):
                            pass
        return B

    Ep = Emit(nc, plan_only=True)
    try:
        program(Ep)
    except Stop:
        pass
    Ee = Emit(nc, plan_only=False, wplan=Ep.wreq)
    try:
        program(Ee)
    except Stop:
        pass

    keys = set()
    for eng, lst in Ee.ops.items():
        for waits, fn, inc in lst:
            keys.add(inc[0])
            for k, v in waits:
                keys.add(k)
    sem = {k: nc.alloc_semaphore(name="s_" + k) for k in sorted(keys)}
    final = dict(Ee.dcnt)

    with nc.Block() as block:
        def replay(e, name, last=False):
            for waits, fn, inc in Ee.ops[name]:
                for k, v in waits:
                    e.wait_ge(sem[k], v)
                ins = fn(e)
                ins.then_inc(sem[inc[0]], inc[1])
            if last:
                for k, v in final.items():
                    e.wait_ge(sem[k], v)
                for k in ("pe", "act", "dve", "pool"):
                    e.wait_ge(sem[k], Ee.cnt[k])

        @block.tensor
        def _(e):
            replay(e, "pe")

        @block.scalar
        def _(e):
            replay(e, "act")

        @block.vector
        def _(e):
            replay(e, "dve")

        @block.gpsimd
        def _(e):
            replay(e, "pool")

        @block.sync
        def _(e):
            replay(e, "sp", last=True)

    nc._n_ops = {k: len(v) for k, v in Ee.ops.items()}
    return nc


def host_inputs(cfg, inp, fused=False):
    c = cfg
    D, KT, H, NT, G = c.D, c.KT, c.H, c.NT, c.G
    f32 = np.float32
    x = np.asarray(inp["x"], f32)
    cc = np.asarray(inp["c"], f32)

    def pk(v):
        return np.asarray(v, f32).reshape(-1, 128).T

    vecs = np.zeros((128, c.NV), f32)
    vecs[:, c.V_N1:c.V_N1 + KT] = pk(inp["norm1_g"][0])
    vecs[:, c.V_N2:c.V_N2 + KT] = pk(inp["norm2_g"][0])
    vecs[:, c.V_N3:c.V_N3 + KT] = pk(inp["norm3_g"][0])
    vecs[:, c.V_NF:c.V_NF + KT] = pk(inp["final_g"])
    vecs[:, c.V_BADA:c.V_BADA + 9 * KT] = pk(inp["b_ada"][0])
    vecs[:, c.V_LNG:c.V_LNG + KT] = pk(inp["gm_ln_g"][0])
    vecs[:, c.V_DNG] = np.asarray(inp["dn_norm_g"][0], f32)
    cw = np.asarray(inp["conv_w"][0], f32)
    for j in range(4):
        vecs[:, c.V_CONV + j * 3 * KT: c.V_CONV + (j + 1) * 3 * KT] = pk(cw[j])
    rowv = np.concatenate([np.asarray(inp["gm_ln_b"][0], f32), np.asarray(inp["gm_b_s"][0], f32).reshape(-1)])[None, :]
    bc8 = np.tile(np.concatenate([np.asarray(inp["a_log"][0], f32), np.asarray(inp["dt_bias"][0], f32)])[None, :], (128, 1))
    w_sT = np.ascontiguousarray(np.transpose(np.asarray(inp["gm_w_s"][0], f32), (0, 2, 1)))
    shared = {
        "vecs": vecs, "rowv": np.ascontiguousarray(rowv), "bc8": np.ascontiguousarray(bc8),
        "w_sT": w_sT,
    }
    wfulls = {"w_ada": inp["w_ada"][0], "ffn1_w_in": inp["ffn1_w_in"][0], "ffn1_w_out": inp["ffn1_w_out"][0],
              "w_in": inp["w_in"][0], "w_branch": np.asarray(inp["w_branch"][0]).reshape(2 * D, D),
              "w_out": inp["w_out"][0], "ffn2_w_in": inp["ffn2_w_in"][0], "ffn2_w_out": inp["ffn2_w_out"][0]}
    wfulls = {k: np.asarray(v, f32) for k, v in wfulls.items()}
    ii = np.arange(128)
    bm = np.zeros((128, 4, 128), f32)
    bm[:, 0, :] = (ii[:, None] // 16 == ii[None, :] // 16)
    for i, sz in enumerate((16, 32, 64)):
        bm[:, 1 + i, :] = (ii[:, None] // (2 * sz) == ii[None, :] // (2 * sz)) & (ii[:, None] // sz != ii[None, :] // sz)
    maps = []
    for r in range(c.NCORES):
        b, j = r // G, r % G
        s0 = j * NT
        m = np.zeros((128, G), f32)
        if fused:
            xs = np.zeros((G * NT, D), f32)
            for sgi in range(G):
                seg = j - (G - 1) + sgi
                if seg >= 0:
                    xs[sgi * NT:(sgi + 1) * NT] = x[b, seg * NT:(seg + 1) * NT]
                    m[:, sgi] = 1.0
            xT = np.ascontiguousarray(xs.T.reshape(KT, 128, G * NT).transpose(1, 0, 2))
        else:
            xs = np.zeros((NT + 4, D), f32)
            xs[4:] = x[b, s0:s0 + NT]
            if j > 0:
                xs[:4] = x[b, s0 - 4:s0]
            xT = np.ascontiguousarray(xs.T.reshape(KT, 128, NT + 4).transpose(1, 0, 2))
            m[:, 0] = 1.0 if j > 0 else 0.0
            for i in range(G - 1):
                m[:, 1 + i] = 1.0 if i < j else 0.0
        d = dict(shared)
        for k, v in wfulls.items():
            rp = v.shape[0] // c.NCORES
            d[k] = np.ascontiguousarray(v[r * rp:(r + 1) * rp]) if c.WG else v
        d["xT"] = xT
        d["cT"] = np.ascontiguousarray(cc[b].reshape(KT, 128).T)
        d["masks"] = m
        d["bmask"] = bm
        maps.append(d)
    return maps


def host_output(cfg, results, B):
    c = cfg
    out = np.zeros((B, c.G * c.NT, c.D), np.float32)
    for r in range(c.NCORES):
        b, j = r // c.G, r % c.G
        oT = np.asarray(results[r]["outT"], np.float32).reshape(128, c.KT, c.NT)
        out[b, j * c.NT:(j + 1) * c.NT] = oT.transpose(2, 1, 0).reshape(c.NT, c.D)
    return out


_NC_CACHE = {}


def run_two_launch(cfg, inputs, nbatch):
    key = ("two", cfg.D, cfg.NT, cfg.NCORES)
    if key not in _NC_CACHE:
        _NC_CACHE[key] = (build(cfg, part=1), build(cfg, part=2))
    nc1, nc2 = _NC_CACHE[key]
    maps = host_inputs(cfg, inputs)
    res1 = run_bass_kernel_spmd(nc1, maps, core_ids=list(range(cfg.NCORES))).results
    G = cfg.G
    maps2 = []
    for r in range(cfg.NCORES):
        g0 = (r // G) * G
        d = dict(maps[r])
        for k in ("x1s", "h2s", "o_s", "r_s"):
            d[k] = np.ascontiguousarray(res1[r][k])
        d["st_all"] = np.ascontiguousarray(np.concatenate(
            [np.asarray(res1[g0 + i]["st_in"]).reshape(128, cfg.H * 256) for i in range(G)], axis=0))
        maps2.append(d)
    res2 = run_bass_kernel_spmd(nc2, maps2, core_ids=list(range(cfg.NCORES))).results
    return host_output(cfg, res2, nbatch)


def run_fused(cfg, inputs, nbatch):
    key = ("fused", cfg.D, cfg.NT, cfg.NCORES)
    if key not in _NC_CACHE:
        _NC_CACHE[key] = build(cfg, part=3)
    nc = _NC_CACHE[key]
    maps = host_inputs(cfg, inputs, fused=True)
    res = run_bass_kernel_spmd(nc, maps, core_ids=list(range(cfg.NCORES))).results
    return host_output(cfg, res, nbatch)


def kernel(**inputs):
    cfg = Cfg(WG=False)
    return run_fused(cfg, inputs, 2)
```

```python
import numpy as np
import ml_dtypes
from contextlib import ExitStack
import concourse.bass as bass
import concourse.mybir as mybir
from concourse.bass_utils import run_bass_kernel_spmd

F32 = mybir.dt.float32
BF16 = mybir.dt.bfloat16
AF = mybir.ActivationFunctionType
ALU = mybir.AluOpType
AX = mybir.AxisListType
EPS = 1e-6
NEGBIG = -1.0e30


class Cfg:
    def __init__(self, D=1024, DFF=2816, H=8, NT=2048, TB=512, G=4, NCORES=8, WG=True):
        self.WG = WG
        self.STOP = 0
        self.D, self.DFF, self.H, self.NT, self.TB, self.G, self.NCORES = D, DFF, H, NT, TB, G, NCORES
        self.KT = D // 128
        self.FT = DFF // 128
        self.NB = NT // TB
        self.CPB = TB // 128
        self.NCH = NT // 128
        assert H * 128 == D
        self.INW = 6 * D + 2 * H + 2 * D
        self.OFF_U, self.OFF_V, self.OFF_Q, self.OFF_Z = 0, D, 2 * D, 5 * D
        self.OFF_AB = 6 * D
        self.OFF_GATE = 6 * D + 2 * H
        KT = self.KT
        self.V_N1, self.V_N2, self.V_N3, self.V_NF = 0, KT, 2 * KT, 3 * KT
        self.V_BADA = 4 * KT
        self.V_LNG = 13 * KT
        self.V_DNG = 14 * KT
        self.V_CONV = 14 * KT + 1
        self.NV = self.V_CONV + 4 * 3 * KT
        self.WSLOT = max(self.FT * 128, KT * 256)


class Stop(Exception):
    pass


class Buf:
    __slots__ = ("w", "r", "name")

    def __init__(self, name=""):
        self.w = None
        self.r = {}
        self.name = name


class Emit:
    def __init__(self, nc, plan_only, wplan=None):
        self.nc = nc
        self.plan_only = plan_only
        self.ops = {e: [] for e in ("pe", "act", "dve", "pool", "sp")}
        self.cnt = {e: 0 for e in ("pe", "act", "dve", "pool")}
        self.waited = {e: {} for e in self.ops}
        self.dcnt = {}
        self.wreq = []
        self.wplan = wplan
        self.wissued = 0
        self.wconsumed = 0
        self.pend = {}

    def barrier(self):
        snap = dict(self.cnt)
        snap.update(self.dcnt)
        for e in self.ops:
            p = self.pend.get(e) or {}
            for k, v in snap.items():
                if v > p.get(k, 0):
                    p[k] = v
            self.pend[e] = p

    def op(self, eng, fn, reads=(), writes=(), dma=None):
        waits = {}

        def need(tok):
            if tok is None:
                return
            key, val = tok
            if key == eng and eng == "pe":
                return
            if self.waited[eng].get(key, 0) >= val:
                return
            if waits.get(key, 0) < val:
                waits[key] = val

        p = self.pend.get(eng)
        if p:
            for k, v in p.items():
                if v > 0:
                    need((k, v))
            self.pend[eng] = None
        for b in reads:
            need(b.w)
        for b in writes:
            need(b.w)
            for k, v in b.r.items():
                need((k, v))
        for k, v in waits.items():
            self.waited[eng][k] = v
        if dma is None:
            self.cnt[eng] += 1
            tok = (eng, self.cnt[eng])
            inc = (eng, 1)
        else:
            self.dcnt[dma] = self.dcnt.get(dma, 0) + 16
            tok = (dma, self.dcnt[dma])
            inc = (dma, 16)
        for b in reads:
            if b.r.get(tok[0], 0) < tok[1]:
                b.r[tok[0]] = tok[1]
        for b in writes:
            b.w = tok
            b.r = {}
        if not self.plan_only:
            self.ops[eng].append((list(waits.items()), fn, inc))
        return tok


def build(cfg, part=0):
    c = cfg
    D, KT, DFF, FT, H, NT, TB, NB, CPB, NCH, G = c.D, c.KT, c.DFF, c.FT, c.H, c.NT, c.TB, c.NB, c.CPB, c.NCH, c.G
    nc = bass.Bass("TRN2", target_bir_lowering=False)
    FUSED = (part == 3)

    def din(name, shape, dt=F32):
        return nc.dram_tensor(name, list(shape), dt, kind="ExternalInput").ap()

    xT_d = (din("xT", [128, KT, G * NT]) if FUSED else din("xT", [128, KT, NT + 4])) if part != 2 else None
    cT_d = din("cT", [128, KT])
    vecs_d = din("vecs", [128, c.NV])
    rowv_d = din("rowv", [1, 2 * D])
    bc8_d = din("bc8", [128, 2 * H])
    masks_d = din("masks", [128, G])
    bmask_d = din("bmask", [128, 4, 128])
    NCs = c.NCORES
    WSPEC = [("w_ada", D, 9 * D), ("ffn1_w_in", D, 2 * DFF), ("ffn1_w_out", DFF, D), ("w_in", D, c.INW),
             ("w_branch", 2 * D, D), ("w_out", D, D), ("ffn2_w_in", D, 2 * DFF), ("ffn2_w_out", DFF, D)]
    wext, wbnc, wfull = {}, {}, {}
    wap = {}
    if part == 1:
        WSPEC = [w for w in WSPEC if w[0] in ("w_ada", "ffn1_w_in", "ffn1_w_out", "w_in")]
    if part == 2:
        WSPEC = [w for w in WSPEC if w[0] not in ("ffn1_w_in", "ffn1_w_out")]
    for (wn, wr, wc) in WSPEC:
        if c.WG:
            wext[wn] = din(wn, [wr // NCs, wc])
            wbnc[wn] = nc.dram_tensor(wn + "_bnc", [wr // NCs, wc], F32)
            wfull[wn] = nc.dram_tensor(wn + "_full", [wr, wc], F32)
            wap[wn] = wfull[wn].ap()
        else:
            wap[wn] = din(wn, [wr, wc])
    w_ada_d = wap["w_ada"]
    f1in_d = wap.get("ffn1_w_in")
    f1out_d = wap.get("ffn1_w_out")
    win_d = wap["w_in"]
    wsT_d = din("w_sT", [KT, 128, 128])
    wbr_full = wap.get("w_branch")
    wbr_d = [wbr_full[0:D, :], wbr_full[D:2 * D, :]] if wbr_full is not None else None
    wout_d = wap.get("w_out")
    f3in_d = wap.get("ffn2_w_in")
    f3out_d = wap.get("ffn2_w_out")
    outT_d = nc.dram_tensor("outT", [128, KT, NT], F32, kind="ExternalOutput").ap() if part != 1 else None

    skw = {} if part in (0, 3) else {"kind": ("ExternalOutput" if part == 1 else "ExternalInput")}
    x1s_d = nc.dram_tensor("x1s", [128, KT, NT], F32, **skw).ap()
    h2s_d = nc.dram_tensor("h2s", [128, KT, NT], BF16, **skw).ap()
    os_d = nc.dram_tensor("o_s", [NCH, 128, H * 128], BF16, **skw).ap()
    rs_d = nc.dram_tensor("r_s", [NCH, 128, H * 128], BF16, **skw).ap()
    if part != 2:
        st_in_t = nc.dram_tensor("st_in", [128, H * 256], F32, **skw)
    akw = {} if part in (0, 3) else {"kind": "ExternalInput"}
    if part != 1:
        st_all_t = nc.dram_tensor("st_all", [G * 128, H * 256], F32, **akw)

    es = ExitStack()
    T = {}

    def sb(name, shape, dt):
        T[name] = nc.alloc_sbuf_tensor(name, list(shape), dt)
        return T[name]

    ARENA_E = 57 * 1024 - 512
    arena = nc.alloc_sbuf_tensor("arena", [128, ARENA_E], BF16)
    vptr = {}

    def cv(view, name, shape, dt):
        n = 1
        for d_ in shape[1:]:
            n *= d_
        ne = n * (2 if dt == F32 else 1)
        ne = (ne + 15) // 16 * 16
        off = vptr.get(view, 0)
        vptr[view] = off + ne
        assert off + ne <= ARENA_E, (view, name, off + ne)
        ap = arena[:, off:off + ne]
        if dt == F32:
            ap = ap.bitcast(F32)
        ap = ap[:, 0:n]
        if len(shape) == 3:
            ap = ap.rearrange("p (a b) -> p a b", a=shape[1])
        T[name] = ap
        return ap

    identb = sb("identb", [128, 128], BF16)
    identf = sb("identf", [128, 128], F32)
    onesb = sb("onesb", [128, 128], BF16)
    onesD = sb("onesD", [128, 128], BF16)
    onesV = sb("onesV", [128, 128], BF16)
    onesf = sb("onesf", [128, 128], F32)
    tri = sb("tri", [128, 128], F32)
    sel127 = sb("sel127", [128, 128], F32)
    neg1 = sb("neg1", [128, 128], F32)
    neg2 = sb("neg2", [128, 128], F32)
    vecs = sb("vecs_sb", [128, c.NV], F32)
    bc8 = sb("bc8_sb", [128, 2 * H], F32)
    masks = sb("masks_sb", [128, G], F32)
    bmask = sb("bmask_sb", [128, 4, 128], F32)
    ealog = sb("ealog", [128, H], F32)
    cTf = sb("cTf", [128, KT], F32)
    cact = sb("cact", [128, KT], BF16)
    modT = sb("modT", [128, 9 * KT], F32)
    gsc = sb("gsc", [128, 3 * KT], F32)
    hga = sb("hga", [128, 3 * KT], F32)
    wab = sb("wab", [128, KT, 2 * H], BF16)
    WmT = sb("WmT", [128, KT, 128], BF16)
    BiasG = sb("BiasG", [128, KT, 128], F32)
    NWS = 4
    wslot = [sb(f"wslot{i}", [128, c.WSLOT], BF16) for i in range(NWS)]
    xt = [sb("xt0", [128, KT, TB], F32)]
    xt.append(xt[0])
    h2 = sb("h2", [128, KT, TB], BF16)
    sq = [sb(f"sq{i}", [128, TB], BF16) for i in range(2)]
    rt = sb("rt", [128, TB], F32)
    rstd = sb("rstd", [128, TB], F32)
    ntmp = [sb(f"ntmp{i}", [128, TB], F32) for i in range(2)]
    tails = sb("tails", [128, 3 * KT, 4], BF16)
    upad = sb("upad", [128, H, 256], BF16)
    Sf = sb("Sf", [128, H, 256], F32)
    Sb = sb("Sb", [128, H, 256], BF16)
    ostage = [sb(f"ostage{i}", [128, H, 128], BF16) if not FUSED else None for i in range(2)]
    rstage = [sb(f"rstage{i}", [128, H, 128], BF16) if not FUSED else None for i in range(2)]
    Sstb = sb("Sstb", [128, H, 128], BF16) if not FUSED else None
    oTtb = sb("oTtb", [128, H, TB], BF16) if FUSED else None
    rowv = cv("S", "rowv_sb", [1, 2 * D], F32)[0:1, :]
    wsTf = cv("S", "wsTf", [128, KT, 128], F32)
    rsrow = cv("S", "rsrow", [1, KT * 128], F32)[0:1, :]
    pre = [cv("P1", f"pre{i}", [128, 4 + TB], BF16) for i in range(3)]
    dslot = [cv("P1", f"dslot{i}", [128, 4, 128], BF16) for i in range(2)]
    qraw = [cv("P1", f"qraw{i}", [128, TB], F32) for i in range(2)]
    knT = cv("P1", "knT", [128, H, TB], BF16)
    vT = cv("P1", "vT", [128, H, TB], BF16)
    ab = cv("P1", "ab", [128, 2 * H], F32)
    abT = cv("P1", "abT", [128, CPB, 2 * H], F32)
    sm = {n: cv("P1", "sm_" + n, [128, H], F32) for n in
          ("e", "beta", "x", "ex", "sp", "g", "CB", "eCB", "kbs", "dka", "dke", "gle")}
    smT = {n: cv("P1", "smT_" + n, [128, CPB * H], F32) for n in
           ("e", "beta", "x", "ex", "sp", "g", "CB", "eCB", "kbs", "dka", "dke", "gle")}
    dg = cv("P1", "tw", [128, H, 128], F32)
    t1 = dg
    t2 = dg
    RB = cv("P1", "RB", [128, H, 128], F32)
    E1 = cv("P1", "E1", [128, H, 128], BF16)
    E1b = E1
    Nb = [cv("P1", f"Nb{i}", [128, H, 128], BF16) for i in range(2)]
    NTb = [cv("P1", f"NTb{i}", [128, H, 128], BF16) for i in range(2)]
    Ub = [cv("P1", f"Ub{i}", [128, H, 128], BF16) for i in range(2)]
    Afull = cv("P1", "Afull", [128, H, 128], BF16)
    ATfull = cv("P1", "ATfull", [128, H, 128], BF16)
    As1 = cv("P1", "As", [128, H, 128], BF16)
    Ms1 = cv("P1", "Ms", [128, H, 128], BF16)
    Tb = [cv("P1", f"Tb{i}", [128, H, 128], BF16) for i in range(2)]
    Wb = cv("P1", "Wb", [128, H, 128], BF16)
    Vb = cv("P1", "Vb", [128, H, 128], BF16)
    kd = cv("P1", "kd", [128, H, 128], BF16)
    kbg = cv("P1", "kbg", [128, H, 128], BF16)
    vb = cv("P1", "vb", [128, H, 128], BF16)
    wT = cv("P1", "wT", [128, H, 128], BF16)
    vnew = cv("P1", "vnew", [128, H, 256], BF16)
    vptr["F"] = vptr["P1"]
    hT = cv("F", "hT", [128, KT, TB], BF16)
    gT = cv("F", "gT", [128, FT, TB], BF16)
    sa = [cv("F", f"sa{i}", [128, TB], BF16) for i in range(4)]
    ostg = [cv("FO", "ostg0", [128, KT, TB], F32)]
    ostg.append(ostg[0])
    qnT = cv("P1", "qnT", [128, H, TB], BF16)
    E2 = cv("P1", "E2", [128, H, 128], BF16)
    Eg = cv("P1", "Eg", [128, H, 128], BF16)
    qkT = cv("P1", "qkT", [128, H, 128], BF16)
    qdT = cv("P1", "qdT", [128, H, 128], BF16)
    stg = [cv("X", "stg0", [128, H, 256], F32)] * max(G - 1, 1)
    PiT = cv("X", "PiT", [128, H, 128], F32)
    Sst = cv("X", "Sst", [128, H, 128], F32)
    cand = cv("X", "cand", [128, H, 128], F32)
    otrue = cv("P2", "otrue", [128, H, 128], F32)
    osq = cv("P2", "osq", [128, H, 128], BF16)
    ort = cv("P2", "ort", [128, H, 128], F32)
    ors = ort
    obT = cv("P2", "obT", [128, KT, TB], BF16)
    zs = cv("P2", "zs", [128, KT, TB], BF16)
    ug = cv("P2", "ug", [128, KT, TB], BF16)
    vg = cv("P2", "vg", [128, D], F32)
    nrm = cv("P2", "nrm", [128, D], BF16)
    st = {n: cv("P2", "st_" + n, [128, 1], F32) for n in ("s1", "s2", "mean", "msq", "var", "sd", "rstd")}
    mtmp = cv("P2", "mtmp", [128, KT, 128], F32)
    vsq = mtmp.rearrange("p a b -> p (a b)")
    oaT = cv("P2", "oaT", [128, KT, TB], BF16)
    gt = cv("P2", "gt", [128, 2 * KT, TB], BF16)
    mA = cv("P2", "mA", [128, KT, TB], BF16)
    mB = cv("P2", "mB", [128, TB], F32)
    mergedT = cv("P2", "mergedT", [128, KT, TB], BF16)
    wv = cv("P2", "wv", [128, KT, D], BF16)

    ps = [nc.alloc_psum_tensor(f"ps{i}", [128, 512], F32) for i in range(8)]

    def program(E):
        B = {}

        def chk(k):
            if c.STOP == k:
                raise Stop()

        def bf(name):
            if name not in B:
                B[name] = Buf(name)
            return B[name]

        psb = [Buf(f"ps{i}") for i in range(8)]
        pstate = [0]

        def psum():
            i = pstate[0] % 8
            pstate[0] += 1
            return ps[i], psb[i]

        def dma(q, out, in_, reads, writes, sem):
            E.op(q, lambda e, o=out, i=in_: e.dma_start(out=o, in_=i), reads, writes, dma=sem)

        def act(out, in_, func, reads, writes, bias=None, scale=None):
            kw = {}
            if bias is not None:
                kw["bias"] = bias
            if scale is not None:
                kw["scale"] = scale
            E.op("act", lambda e, o=out, i=in_, f=func, k=kw: e.activation(out=o, in_=i, func=f, **k), reads, writes)

        def tt(out, in0, in1, op, reads, writes, eng="dve"):
            E.op(eng, lambda e, o=out, a=in0, b=in1, p=op: e.tensor_tensor(out=o, in0=a, in1=b, op=p), reads, writes)

        def ts(out, in0, s1, s2, op0, op1, reads, writes, eng="dve"):
            if s2 is None:
                E.op(eng, lambda e, o=out, a=in0, x=s1, p=op0: e.tensor_scalar(out=o, in0=a, scalar1=x, scalar2=None, op0=p),
                     reads, writes)
            else:
                E.op(eng, lambda e, o=out, a=in0, x=s1, y=s2, p=op0, q=op1: e.tensor_scalar(
                    out=o, in0=a, scalar1=x, scalar2=y, op0=p, op1=q), reads, writes)

        def stt(out, in0, scalar, in1, op0, op1, reads, writes, eng="dve"):
            E.op(eng, lambda e, o=out, a=in0, s=scalar, b=in1, p=op0, q=op1: e.scalar_tensor_tensor(
                out=o, in0=a, scalar=s, in1=b, op0=p, op1=q), reads, writes)

        def cp(out, in_, reads, writes, eng="dve"):
            if eng == "act":
                E.op("act", lambda e, o=out, i=in_: e.activation(out=o, in_=i, func=AF.Copy), reads, writes)
            else:
                E.op(eng, lambda e, o=out, i=in_: e.tensor_copy(out=o, in_=i), reads, writes)

        def recip(out, in_, reads, writes):
            E.op("dve", lambda e, o=out, i=in_: e.reciprocal(out=o, in_=i), reads, writes)

        def mm(out, lhsT, rhs, start, stop, reads, writes):
            E.op("pe", lambda e, o=out, l=lhsT, r=rhs, s=start, p=stop: e.matmul(o, lhsT=l, rhs=r, start=s, stop=p),
                 reads, writes)

        def tr(out, in_, ident, reads, writes):
            E.op("pe", lambda e, o=out, i=in_, d=ident: e.transpose(o, i, d), reads, writes)

        def bc_mid(ap2d, n):
            return ap2d.unsqueeze(1).broadcast_to([128, n, ap2d.shape[1]])

        def bc_last(ap2d, n):
            return ap2d.unsqueeze(2).broadcast_to([128, ap2d.shape[1], n])

        wslot_b = [Buf(f"wslot{i}") for i in range(NWS)]

        def w_issue(upto):
            plan = E.wplan
            while E.wissued < min(upto, len(plan)):
                j = E.wissued
                s = j % NWS
                for (o_fn, src, dep) in plan[j]:
                    dma("pool", o_fn(wslot[s]), src, [bf(dep)] if dep else [], [wslot_b[s]], f"w{s}")
                E.wissued += 1

        def w_get(loads):
            j = E.wconsumed
            E.wconsumed += 1
            if E.plan_only:
                E.wreq.append(loads)
                return wslot[j % NWS], wslot_b[j % NWS]
            w_issue(j + NWS - 1)
            return wslot[j % NWS], wslot_b[j % NWS]

        def linear_fm(Wd, col0, nft, KTin, rhs, N, evac, CW=2):
            for _ in linear_fm_gen(Wd, col0, nft, KTin, rhs, N, evac, CW):
                pass

        def linear_fm_gen(Wd, col0, nft, KTin, rhs, N, evac, CW=2):
            nchunk = (nft + CW - 1) // CW
            for ci in range(nchunk):
                f0 = ci * CW
                nf = min(CW, nft - f0)
                ncols = nf * 128
                src = Wd[:, col0 + f0 * 128: col0 + f0 * 128 + ncols].rearrange("(kt p) n -> p kt n", p=128)
                nm = Wd.name
                dep = ("g_" + nm[:-5]) if nm.endswith("_full") else None
                slot, sbuf_ = w_get([(lambda s, k=KTin, n=ncols: s[:, 0:k * n].rearrange("p (k n) -> p k n", k=k), src, dep)])
                wv_ = slot[:, 0:KTin * ncols].rearrange("p (k n) -> p k n", k=KTin)
                for fi in range(nf):
                    p_t, p_b = psum()
                    for kt in range(KTin):
                        r_ap, r_bufs = rhs(kt)
                        mm(p_t[:, 0:N], wv_[:, kt, fi * 128:(fi + 1) * 128], r_ap, kt == 0, kt == KTin - 1,
                           [sbuf_] + r_bufs, [p_b])
                    evac(f0 + fi, p_t[:, 0:N], p_b)
                    yield

        allc = [list(range(NCs))]
        for (wn, wr, wc) in (WSPEC if c.WG else []):
            dma("sp", wbnc[wn].ap(), wext[wn], [], [bf("b_" + wn)], "wb_" + wn)
            E.op("pool", lambda e, a=wbnc[wn], o=wfull[wn]: e.collective_compute(
                "AllGather", ALU.bypass, replica_groups=allc, ins=[a.ap().opt()], outs=[o.ap().opt()]),
                [bf("b_" + wn)], [bf("g_" + wn)], dma="cc_" + wn)
        E.op("pool", lambda e: e.memset(identf[:], 0.0), [], [bf("identf")])
        E.op("pool", lambda e: e.affine_select(out=identf[:], in_=identf[:], pattern=[[-1, 128]], compare_op=ALU.not_equal,
                                               fill=1.0, base=0, channel_multiplier=1), [bf("identf")], [bf("identf")])
        E.op("pool", lambda e: e.tensor_copy(out=identb[:], in_=identf[:]), [bf("identf")], [bf("identb")])
        E.op("pool", lambda e: e.memset(onesb[:], 1.0), [], [bf("onesb")])
        E.op("pool", lambda e: e.memset(onesD[:], 1.0 / D), [], [bf("onesD")])
        E.op("pool", lambda e: e.memset(onesV[:], 1.0 / 128), [], [bf("onesV")])
        E.op("pool", lambda e: e.memset(onesf[:], 1.0), [], [bf("onesf")])
        E.op("pool", lambda e: e.memset(tri[:], 1.0), [], [bf("tri")])
        E.op("pool", lambda e: e.affine_select(out=tri[:], in_=tri[:], pattern=[[1, 128]], compare_op=ALU.is_ge,
                                               fill=0.0, base=0, channel_multiplier=-1), [bf("tri")], [bf("tri")])
        E.op("pool", lambda e: e.memset(sel127[:], 1.0), [], [bf("sel127")])
        E.op("pool", lambda e: e.affine_select(out=sel127[:], in_=sel127[:], pattern=[[0, 128]], compare_op=ALU.is_ge,
                                               fill=0.0, base=-127, channel_multiplier=1), [bf("sel127")], [bf("sel127")])
        E.op("pool", lambda e: e.memset(neg1[:], 0.0), [], [bf("neg1")])
        E.op("pool", lambda e: e.affine_select(out=neg1[:], in_=neg1[:], pattern=[[-1, 128]], compare_op=ALU.is_gt,
                                               fill=NEGBIG, base=0, channel_multiplier=1), [bf("neg1")], [bf("neg1")])
        E.op("pool", lambda e: e.memset(neg2[:], 0.0), [], [bf("neg2")])
        E.op("pool", lambda e: e.affine_select(out=neg2[:], in_=neg2[:], pattern=[[1, 128]], compare_op=ALU.is_ge,
                                               fill=NEGBIG, base=0, channel_multiplier=-1), [bf("neg2")], [bf("neg2")])
        E.op("pool", lambda e: e.memset(upad[:], 0.0), [], [bf(f"upad#{hb}") for hb in range(2 if H >= 8 else 1)])

        dma("sp", vecs[:], vecs_d, [], [bf("vecs")], "c_vecs")
        dma("sp", rowv[:], rowv_d, [], [bf("rowv")], "c_rowv")
        dma("sp", bc8[:], bc8_d, [], [bf("bc8")], "c_bc8")
        dma("sp", masks[:], masks_d, [], [bf("masks")], "c_masks")
        dma("sp", bmask[:], bmask_d, [], [bf("bmask")], "c_bmask")
        dma("sp", cTf[:], cT_d, [], [bf("cTf")], "c_cT")
        dma("sp", wsTf[:], wsT_d.rearrange("g s t -> s g t"), [], [bf("wsTf")], "c_wsT")
        dma("pool", wab[:], win_d[:, c.OFF_AB:c.OFF_AB + 2 * H].rearrange("(kt p) n -> p kt n", p=128), [bf("g_w_in")], [bf("wab")],
            "c_wab")

        act(ealog[:], bc8[:, 0:H], AF.Exp, [bf("bc8")], [bf("ealog")])
        act(cact[:], cTf[:], AF.Silu, [bf("cTf")], [bf("cact")])

        chk(1)
        def mod_evac(ft, p_ap, p_b):
            tt(modT[:, ft:ft + 1], p_ap, vecs[:, c.V_BADA + ft:c.V_BADA + ft + 1], ALU.add, [p_b, bf("vecs")], [bf("modT")])

        linear_fm(w_ada_d, 0, 9 * KT, KT, lambda kt: (cact[:, kt:kt + 1], [bf("cact")]), 1, mod_evac)
        for i, (vn, half) in enumerate(((c.V_N1, 0.5), (c.V_N2, 1.0), (c.V_N3, 0.5))):
            stt(gsc[:, i * KT:(i + 1) * KT], modT[:, (3 * i + 1) * KT:(3 * i + 2) * KT], 1.0, vecs[:, vn:vn + KT],
                ALU.add, ALU.mult, [bf("modT"), bf("vecs")], [bf("gsc")])
            ts(hga[:, i * KT:(i + 1) * KT], modT[:, (3 * i + 2) * KT:(3 * i + 3) * KT], half, None, ALU.mult, None,
               [bf("modT")], [bf("hga")])

        tt(WmT[:], wsTf[:], bc_mid(tri[:], KT), ALU.mult, [bf("wsTf"), bf("tri")], [bf("WmT")])
        WmT_flat = WmT[:].rearrange("p g t -> p (g t)")
        for o0 in range(0, KT * 128, 512):
            n = min(512, KT * 128 - o0)
            p_t, p_b = psum()
            mm(p_t[0:1, 0:n], onesb[:, 0:1], WmT_flat[:, o0:o0 + n], True, True, [bf("onesb"), bf("WmT")], [p_b])
            cp(rsrow[0:1, o0:o0 + n], p_t[0:1, 0:n], [p_b], [bf("rsrow")])
        for g in range(KT):
            p_t, p_b = psum()
            mm(p_t[:, 0:128], rowv[0:1, g * 128:(g + 1) * 128], rsrow[0:1, g * 128:(g + 1) * 128], True, False,
               [bf("rowv"), bf("rsrow")], [p_b])
            mm(p_t[:, 0:128], onesf[0:1, 0:128], rowv[0:1, D + g * 128:D + (g + 1) * 128], False, True,
               [bf("rowv"), bf("onesf")], [p_b])
            cp(BiasG[:, g, :], p_t[:, 0:128], [p_b], [bf("BiasG")])

        chk(2)
        E.barrier()

        def rms_stats(xin, xb, N, ones_t, ones_b, out_rstd=rstd, kts=None):
            kts = range(KT) if kts is None else kts
            p_t, p_b = psum()
            kl = list(kts)
            for i, kt in enumerate(kl):
                s = sq[i % 2]
                act(s[:, 0:N], xin(kt), AF.Square, xb, [bf(f"sq{i % 2}")])
                mm(p_t[:, 0:N], ones_t[:], s[:, 0:N], i == 0, i == len(kl) - 1, [bf(f"sq{i % 2}"), ones_b], [p_b])
            act(rt[:, 0:N], p_t[:, 0:N], AF.Ln, [p_b], [bf("rt")], bias=EPS)
            act(out_rstd[:, 0:N], rt[:, 0:N], AF.Exp, [bf("rt")], [bf("rstd")], scale=-0.5)

        def norm_mod(xcur, xb, N, which, dst, dstb):
            rms_stats(lambda kt: xcur[:, kt, 0:N], xb, N, onesD, bf("onesD"))
            for kt in range(KT):
                tmp = ntmp[kt % 2]
                stt(tmp[:, 0:N], xcur[:, kt, 0:N], gsc[:, which * KT + kt:which * KT + kt + 1], rstd[:, 0:N], ALU.mult,
                    ALU.mult, xb + [bf("gsc"), bf("rstd")], [bf(f"ntmp{kt % 2}")])
                shc = (3 * which) * KT + kt
                act(dst[:, kt, 0:N], tmp[:, 0:N], AF.Identity, [bf(f"ntmp{kt % 2}"), bf("modT")], dstb,
                    bias=modT[:, shc:shc + 1])

        def ffn(xcur, xb, N, which, win_d_, wout_d_):
            for _ in ffn_gen(xcur, xb, N, which, win_d_, wout_d_):
                pass

        def ffn_gen(xcur, xb, N, which, win_d_, wout_d_):
            norm_mod(xcur, xb, N, which, hT, [bf("hT")])
            yield

            for j0 in range(0, FT, 2):
                nj = min(2, FT - j0)

                base = (j0 // 2 % 2) * 2

                def ev_a2(ft, p_ap, p_b, base=base):
                    act(sa[base + ft][:, 0:N], p_ap, AF.Silu, [p_b], [bf(f"sa{base + ft}")])

                def ev_b2(ft, p_ap, p_b, base=base, j0=j0):
                    tt(gT[:, j0 + ft, 0:N], sa[base + ft][:, 0:N], p_ap, ALU.mult, [bf(f"sa{base + ft}"), p_b], [bf("gT")])

                yield from linear_fm_gen(win_d_, j0 * 128, nj, KT, lambda kt: (hT[:, kt, 0:N], [bf("hT")]), N, ev_a2)
                yield from linear_fm_gen(win_d_, DFF + j0 * 128, nj, KT, lambda kt: (hT[:, kt, 0:N], [bf("hT")]), N, ev_b2)

            def ev_out(ft, p_ap, p_b):
                stt(xcur[:, ft, 0:N], p_ap, hga[:, which * KT + ft:which * KT + ft + 1], xcur[:, ft, 0:N], ALU.mult, ALU.add,
                    [p_b, bf("hga")] + xb, xb)

            yield from linear_fm_gen(wout_d_, 0, KT, FT, lambda kt: (gT[:, kt, 0:N], [bf("gT")]), N, ev_out, CW=1)

        _sfb = [bf(f"Sf#{hb}") for hb in range(2 if H >= 8 else 1)]
        _sbb = [bf(f"Sb#{hb}") for hb in range(2 if H >= 8 else 1)]
        E.op("dve", lambda e: e.memset(Sf[:], 0.0), [], _sfb)
        cp(Sf[:, :, 128:256], bc_mid(identf[:], H), [bf("identf")] + _sfb, _sfb)
        cp(Sb[:], Sf[:], _sfb, _sbb, eng="act")

        def qkv_conv(N, xb_h2, is_halo, vm=None, own=True, q_tails_only=False):
            ft_off = 0 if own else KT
            pending = []

            def flush():
                bufsets = [(rt, "rt", rstd, "rstd"), (ntmp[0], "ntmp0", ntmp[1], "ntmp1")]
                banks = []
                for i_, (ft_, h_, isq_, qr_, qrb_) in enumerate(pending):
                    p2, p2b = psum()
                    s_ = sq[i_]
                    act(s_[:, 0:N], qr_[:, 0:N], AF.Square, [qrb_], [bf(f"sq{i_}")])
                    mm(p2[:, 0:N], onesb[:], s_[:, 0:N], True, True, [bf(f"sq{i_}"), bf("onesb")], [p2b])
                    banks.append((p2, p2b))
                for i_ in range(len(pending)):
                    a_, an_, b_, bn_ = bufsets[i_]
                    act(a_[:, 0:N], banks[i_][0][:, 0:N], AF.Ln, [banks[i_][1]], [bf(an_)], bias=EPS)
                for i_ in range(len(pending)):
                    a_, an_, b_, bn_ = bufsets[i_]
                    act(b_[:, 0:N], a_[:, 0:N], AF.Exp, [bf(an_)], [bf(bn_)], scale=-0.5)
                for i_, (ft_, h_, isq_, qr_, qrb_) in enumerate(pending):
                    a_, an_, b_, bn_ = bufsets[i_]
                    dst = qnT if isq_ else knT
                    stt(dst[:, h_, 0:N], qr_[:, 0:N], (128.0 ** -0.5) if isq_ else 1.0, b_[:, 0:N], ALU.mult, ALU.mult,
                        [qrb_, bf(bn_)], [bf("qnT" if isq_ else "knT")])
                pending.clear()

            def ev_pre(ft, p_ap, p_b):
                ft = ft + ft_off
                if is_halo:
                    ts(tails[:, ft, :], p_ap, masks[:, 0:1], None, ALU.mult, None, [p_b, bf("masks")], [bf("tails")])
                    return
                slot = pre[ft % 3]
                slb = bf(f"pre{ft % 3}")
                cp(slot[:, 0:4], tails[:, ft, :], [bf("tails")], [slb], eng="act")
                if vm is not None:
                    ts(slot[:, 4:4 + N], p_ap, vm, None, ALU.mult, None, [p_b, bf("masks")], [slb])
                else:
                    cp(slot[:, 4:4 + N], p_ap, [p_b], [slb], eng=("act" if ft % 2 else "dve"))
                cp(tails[:, ft, :], slot[:, N:N + 4], [slb], [bf("tails")])
                if q_tails_only and ft < KT:
                    return
                ds = dslot[ft % 2]
                dsb = bf(f"dslot{ft % 2}")
                for j in range(4):
                    col = c.V_CONV + j * 3 * KT + ft
                    act(ds[:, j, :], identb[:], AF.Copy, [bf("identb"), bf("vecs")], [dsb], scale=vecs[:, col:col + 1])
                p_t, p_b2 = psum()
                for j in range(4):
                    mm(p_t[:, 0:N], ds[:, j, :], slot[:, 1 + j:1 + j + N], j == 0, j == 3, [dsb, slb], [p_b2])
                if ft < 2 * KT:
                    h = ft % KT
                    isq = ft < KT
                    qr = qraw[ft % 2]
                    qrb = bf(f"qraw{ft % 2}")
                    act(qr[:, 0:N], p_t[:, 0:N], AF.Silu, [p_b2], [qrb])
                    pending.append((ft, h, isq, qr, qrb))
                    if len(pending) == 2:
                        flush()
                else:
                    act(vT[:, ft - 2 * KT, 0:N], p_t[:, 0:N], AF.Silu, [p_b2], [bf("vT")])

            linear_fm(win_d, c.OFF_Q + ft_off * 128, 3 * KT - ft_off, KT, lambda kt: (h2[:, kt, 0:N], xb_h2), N, ev_pre)
            if pending:
                flush()

        HB = 4 if H >= 4 else H
        NHB = H // HB

        FILL = [None]

        def fill(k=1):
            g_ = FILL[0]
            if g_ is None:
                return
            for _ in range(k):
                try:
                    next(g_)
                except StopIteration:
                    FILL[0] = None
                    return

        def ab_proj(c0, ci):
            csl = slice(c0, c0 + 128)
            p_t, p_b = psum()
            for kt in range(KT):
                mm(p_t[:, 0:2 * H], h2[:, kt, csl], wab[:, kt, :], kt == 0, kt == KT - 1, [bf("h2"), bf("wab")], [p_b])
            cp(abT[:, ci, :], p_t[:, 0:2 * H], [p_b], [bf("abT")])

        def small_chain_tb(vm):
            Z = lambda n: smT[n][:]
            zB = lambda n: bf("smT_" + n)
            abv = abT[:]
            a_ap = abv[:, :, 0:H]
            b_ap = abv[:, :, H:2 * H]
            v3 = lambda n: smT[n][:].rearrange("p (c h) -> p c h", c=CPB)
            act(v3("e"), b_ap, AF.Exp, [bf("abT")], [zB("e")], scale=-1.0)
            ts(Z("e"), Z("e"), 1.0, None, ALU.add, None, [zB("e")], [zB("e")])
            recip(Z("beta"), Z("e"), [zB("e")], [zB("beta")])
            if vm is not None:
                ts(Z("beta"), Z("beta"), vm, None, ALU.mult, None, [zB("beta"), bf("masks")], [zB("beta")])
            tt(v3("x"), a_ap, bc_mid(bc8[:, H:2 * H], CPB), ALU.add, [bf("abT"), bf("bc8")], [zB("x")])
            act(Z("ex"), Z("x"), AF.Exp, [zB("x")], [zB("ex")])
            act(Z("sp"), Z("ex"), AF.Ln, [zB("ex")], [zB("sp")], bias=1.0)
            stt(v3("g"), v3("sp"), -1.0, bc_mid(ealog[:], CPB), ALU.mult, ALU.mult, [zB("sp"), bf("ealog")], [zB("g")])
            p_t, p_b = psum()
            mm(p_t[:, 0:CPB * H], tri[:], Z("g"), True, True, [bf("tri"), zB("g")], [p_b])
            cp(Z("CB"), p_t[:, 0:CPB * H], [p_b], [zB("CB")])
            act(Z("eCB"), Z("CB"), AF.Exp, [zB("CB")], [zB("eCB")])
            tt(Z("kbs"), Z("eCB"), Z("beta"), ALU.mult, [zB("eCB"), zB("beta")], [zB("kbs")])
            p_t, p_b = psum()
            mm(p_t[:, 0:CPB * H], sel127[:], Z("CB"), True, True, [bf("sel127"), zB("CB")], [p_b])
            tt(Z("dka"), p_t[:, 0:CPB * H], Z("CB"), ALU.subtract, [p_b, zB("CB")], [zB("dka")])
            act(Z("gle"), p_t[:, 0:CPB * H], AF.Exp, [p_b], [zB("gle")])
            act(Z("dke"), Z("dka"), AF.Exp, [zB("dka")], [zB("dke")])

        def chunk_solve(c0, gci, own=True, vm=None, ci=None):
            csl = slice(c0, c0 + 128)
            assert ci is not None
            sm = {n: smT[n][:, ci * H:(ci + 1) * H] for n in smT}
            S = lambda n: sm[n]
            sB = lambda n: bf("smT_" + n)
            fill()

            HS = [slice(hb * HB, (hb + 1) * HB) for hb in range(NHB)]
            hbf = lambda name, hb: bf(f"{name}#{hb}")
            f2 = lambda t, hb: t[:, HS[hb], :].rearrange("p h f -> p (h f)")
            W_ = HB * 128

            def each(fn):
                for hb in range(NHB):
                    fn(hb)
                fill()

            def s_rb(hb):
                hs = HS[hb]
                tt(dg[:, hs, :], bc_mid(identf[:], HB), bc_last(sm["CB"][:, hs], 128), ALU.mult, [bf("identf"), sB("CB")], [hbf("tw", hb)])
                p_t, p_b = psum()
                mm(p_t[:, 0:W_], onesf[:], f2(dg, hb), True, True, [bf("onesf"), hbf("tw", hb)], [p_b])
                cp(f2(RB, hb), p_t[:, 0:W_], [p_b], [hbf("RB", hb)], eng="act")
            each(s_rb)

            def s_e1(hb):
                hs = HS[hb]
                tt(t1[:, hs, :], bc_mid(neg1[:], HB), RB[:, hs, :], ALU.subtract, [bf("neg1"), hbf("RB", hb)], [hbf("tw", hb)])
                tt(t1[:, hs, :], t1[:, hs, :], bc_last(sm["CB"][:, hs], 128), ALU.add, [hbf("tw", hb), sB("CB")], [hbf("tw", hb)])
                act(E1[:, hs, :], t1[:, hs, :], AF.Exp, [hbf("tw", hb)], [hbf("E1", hb)])
                tt(E1[:, hs, :], E1[:, hs, :], bc_last(sm["beta"][:, hs], 128), ALU.mult, [hbf("E1", hb), sB("beta")], [hbf("E1", hb)])
            each(s_e1)

            def s_gram(hb):
                pG, pGb = psum()
                for hh in range(HB):
                    h = hb * HB + hh
                    mm(pG[:, hh * 128:(hh + 1) * 128], knT[:, h, csl], knT[:, h, csl], True, True, [bf("knT")], [pGb])
                tt(f2(Afull, hb), pG[:, 0:W_], f2(E1, hb), ALU.mult, [pGb, hbf("E1", hb)], [hbf("Afull", hb)])
            each(s_gram)
            if own:
                def s_e2(hb):
                    hs = HS[hb]
                    tt(t2[:, hs, :], RB[:, hs, :], bc_mid(neg2[:], HB), ALU.add, [bf("neg2"), hbf("RB", hb)], [hbf("tw", hb)])
                    tt(t2[:, hs, :], t2[:, hs, :], bc_last(sm["CB"][:, hs], 128), ALU.subtract, [hbf("tw", hb), sB("CB")], [hbf("tw", hb)])
                    act(E2[:, hs, :], t2[:, hs, :], AF.Exp, [hbf("tw", hb)], [hbf("E2", hb)])
                    act(Eg[:, hs, :], RB[:, hs, :], AF.Exp, [hbf("RB", hb)], [hbf("Eg", hb)])
                    pQ, pQb = psum()
                    for hh in range(HB):
                        h = hb * HB + hh
                        mm(pQ[:, hh * 128:(hh + 1) * 128], knT[:, h, csl], qnT[:, h, csl], True, True, [bf("knT"), bf("qnT")], [pQb])
                    tt(f2(qkT, hb), pQ[:, 0:W_], f2(E2, hb), ALU.mult, [pQb, hbf("E2", hb)], [hbf("qkT", hb)])
                    tt(qdT[:, hs, :], qnT[:, hs, csl], Eg[:, hs, :], ALU.mult, [bf("qnT"), hbf("Eg", hb)], [hbf("qdT", hb)])
                each(s_e2)

            mk = lambda i: bc_mid(bmask[:, i, :], HB)

            def transp(src, srcb_fn, hb):
                pT, pTb = psum()
                pTv = pT[:].bitcast(BF16)
                for hh in range(HB):
                    h = hb * HB + hh
                    tr(pTv[:, hh * 128:(hh + 1) * 128], src(h), identb[:], [srcb_fn(hb), bf("identb")], [pTb])
                return pTv, pTb

            def s_at(hb):
                hs = HS[hb]
                pTv, pTb = transp(lambda h: Afull[:, h, :], lambda hb_: hbf("Afull", hb_), hb)
                cp(f2(ATfull, hb), pTv[:, 0:W_], [pTb], [hbf("ATfull", hb)], eng="act")
                tt(Nb[0][:, hs, :], Afull[:, hs, :], mk(0), ALU.mult, [hbf("Afull", hb), bf("bmask")], [hbf("Nb0", hb)])
                tt(NTb[0][:, hs, :], ATfull[:, hs, :], mk(0), ALU.mult, [hbf("ATfull", hb), bf("bmask")], [hbf("NTb0", hb)])
                tt(Ub[0][:, hs, :], bc_mid(identb[:], HB), NTb[0][:, hs, :], ALU.subtract, [hbf("NTb0", hb), bf("identb")], [hbf("Ub0", hb)])
            each(s_at)
            cur = 0
            NLEV = 3
            for k in range(1, NLEV + 1):
                nxt = 1 - cur

                def s_lev(hb, cur=cur, nxt=nxt, k=k):
                    pN, pNb = psum()
                    for hh in range(HB):
                        h = hb * HB + hh
                        mm(pN[:, hh * 128:(hh + 1) * 128], NTb[cur][:, h, :], Nb[cur][:, h, :], True, True,
                           [hbf(f"NTb{cur}", hb), hbf(f"Nb{cur}", hb)], [pNb])
                    cp(f2(Nb[nxt], hb), pN[:, 0:W_], [pNb], [hbf(f"Nb{nxt}", hb)], eng="act")
                    if k < NLEV:
                        pM, pMb = psum()
                        for hh in range(HB):
                            h = hb * HB + hh
                            mm(pM[:, hh * 128:(hh + 1) * 128], Nb[cur][:, h, :], NTb[cur][:, h, :], True, True,
                               [hbf(f"NTb{cur}", hb), hbf(f"Nb{cur}", hb)], [pMb])
                        cp(f2(NTb[nxt], hb), pM[:, 0:W_], [pMb], [hbf(f"NTb{nxt}", hb)])
                    pU, pUb = psum()
                    for hh in range(HB):
                        h = hb * HB + hh
                        mm(pU[:, hh * 128:(hh + 1) * 128], Nb[nxt][:, h, :], Ub[cur][:, h, :], True, True,
                           [hbf(f"Nb{nxt}", hb), hbf(f"Ub{cur}", hb)], [pUb])
                    tt(f2(Ub[nxt], hb), f2(Ub[cur], hb), pU[:, 0:W_], ALU.add, [pUb, hbf(f"Ub{cur}", hb)], [hbf(f"Ub{nxt}", hb)])
                each(s_lev)
                cur = nxt

            def s_td(hb, cur=cur):
                pTv, pTb = transp(lambda h: Ub[cur][:, h, :], lambda hb_: hbf(f"Ub{cur}", hb_), hb)
                cp(f2(Tb[0], hb), pTv[:, 0:W_], [pTb], [hbf("Tb0", hb)], eng="act")
            each(s_td)
            tcur = 0
            for i in range(3):
                nxt = 1 - cur
                tnx = 1 - tcur
                lastm = (i == 2)

                def s_mrg(hb, i=i, cur=cur, nxt=nxt, tcur=tcur, tnx=tnx, lastm=lastm):
                    hs = HS[hb]
                    tt(As1[:, hs, :], Afull[:, hs, :], mk(1 + i), ALU.mult, [hbf("Afull", hb), bf("bmask")], [hbf("As", hb)])
                    pW, pWb = psum()
                    for hh in range(HB):
                        h = hb * HB + hh
                        mm(pW[:, hh * 128:(hh + 1) * 128], As1[:, h, :], Ub[cur][:, h, :], True, True,
                           [hbf("As", hb), hbf(f"Ub{cur}", hb)], [pWb])
                    cp(f2(Wb, hb), pW[:, 0:W_], [pWb], [hbf("Wb", hb)], eng="act")
                    if not lastm:
                        tt(Ms1[:, hs, :], ATfull[:, hs, :], mk(1 + i), ALU.mult, [hbf("ATfull", hb), bf("bmask")], [hbf("Ms", hb)])
                        pV, pVb = psum()
                        for hh in range(HB):
                            h = hb * HB + hh
                            mm(pV[:, hh * 128:(hh + 1) * 128], Ms1[:, h, :], Tb[tcur][:, h, :], True, True,
                               [hbf("Ms", hb), hbf(f"Tb{tcur}", hb)], [pVb])
                        cp(f2(Vb, hb), pV[:, 0:W_], [pVb], [hbf("Vb", hb)])
                    pU, pUb = psum()
                    for hh in range(HB):
                        h = hb * HB + hh
                        mm(pU[:, hh * 128:(hh + 1) * 128], Tb[tcur][:, h, :], Wb[:, h, :], True, True,
                           [hbf(f"Tb{tcur}", hb), hbf("Wb", hb)], [pUb])
                    tt(f2(Ub[nxt], hb), f2(Ub[cur], hb), pU[:, 0:W_], ALU.subtract, [pUb, hbf(f"Ub{cur}", hb)], [hbf(f"Ub{nxt}", hb)])
                    if not lastm:
                        pX, pXb = psum()
                        for hh in range(HB):
                            h = hb * HB + hh
                            mm(pX[:, hh * 128:(hh + 1) * 128], Ub[cur][:, h, :], Vb[:, h, :], True, True,
                               [hbf(f"Ub{cur}", hb), hbf("Vb", hb)], [pXb])
                        tt(f2(Tb[tnx], hb), f2(Tb[tcur], hb), pX[:, 0:W_], ALU.subtract, [pXb, hbf(f"Tb{tcur}", hb)], [hbf(f"Tb{tnx}", hb)])
                each(s_mrg)
                cur = nxt
                tcur = tnx
            U = Ub[cur]
            Ubn = f"Ub{cur}"

            def s_kv(hb):
                hs = HS[hb]
                pTv, pTb = transp(lambda h: knT[:, h, csl], lambda hb_: bf("knT"), hb)
                pT3 = pTv[:, 0:W_].rearrange("p (h f) -> p h f", h=HB)
                tt(kd[:, hs, :], pT3, bc_last(sm["dke"][:, hs], 128), ALU.mult, [pTb, sB("dke")], [hbf("kd", hb)])
                tt(kbg[:, hs, :], pT3, bc_last(sm["kbs"][:, hs], 128), ALU.mult, [pTb, sB("kbs")], [hbf("kbg", hb)])
                pTv, pTb = transp(lambda h: vT[:, h, csl], lambda hb_: bf("vT"), hb)
                pT3 = pTv[:, 0:W_].rearrange("p (h f) -> p h f", h=HB)
                tt(vb[:, hs, :], pT3, bc_last(sm["beta"][:, hs], 128), ALU.mult, [pTb, sB("beta")], [hbf("vb", hb)])
            each(s_kv)

            def s_uw(hb):
                hs = HS[hb]
                pu, pub = psum()
                pw, pwb = psum()
                for hh in range(HB):
                    h = hb * HB + hh
                    mm(pu[:, hh * 128:(hh + 1) * 128], U[:, h, :], vb[:, h, :], True, True, [hbf(Ubn, hb), hbf("vb", hb)], [pub])
                    mm(pw[:, hh * 128:(hh + 1) * 128], kbg[:, h, :], U[:, h, :], True, True, [hbf(Ubn, hb), hbf("kbg", hb)], [pwb])
                cp(upad[:, hs, 0:128], pu[:, 0:W_].rearrange("p (h f) -> p h f", h=HB), [pub], [hbf("upad", hb)], eng="act")
                cp(f2(wT, hb), pw[:, 0:W_], [pwb], [hbf("wT", hb)])
            each(s_uw)

            so = ostage[gci % 2]
            sr = rstage[gci % 2]
            sob, srb = bf(f"ostage{gci % 2}"), bf(f"rstage{gci % 2}")

            def s_vn(hb):
                for h2i in range(hb * HB, (hb + 1) * HB, 2):
                    p1, p1b = psum()
                    for hh in range(2):
                        h = h2i + hh
                        mm(p1[:, hh * 256:(hh + 1) * 256], wT[:, h, :], Sb[:, h, :], True, True, [hbf("wT", hb), hbf("Sb", hb)], [p1b])
                    tt(vnew[:, h2i:h2i + 2, :].rearrange("p h f -> p (h f)"), upad[:, h2i:h2i + 2, :].rearrange("p h f -> p (h f)"),
                       p1[:, 0:512], ALU.subtract, [p1b, hbf("upad", hb)], [hbf("vnew", hb)])
            each(s_vn)
            if own:
                def s_o(hb):
                    hs = HS[hb]
                    po, pob = psum()
                    for hh in range(HB):
                        h = hb * HB + hh
                        mm(po[:, hh * 128:(hh + 1) * 128], Sb[:, h, 0:128], qdT[:, h, :], True, False, [hbf("Sb", hb), hbf("qdT", hb)], [pob])
                        mm(po[:, hh * 128:(hh + 1) * 128], vnew[:, h, 0:128], qkT[:, h, :], False, True, [hbf("vnew", hb), hbf("qkT", hb)], [pob])
                    if FUSED:
                        cp(oTtb[:, hs, csl], po[:, 0:W_].rearrange("p (h f) -> p h f", h=HB), [pob], [bf("oTtb")], eng="act")
                        return
                    pr, prb = psum()
                    for hh in range(HB):
                        h = hb * HB + hh
                        mm(pr[:, hh * 128:(hh + 1) * 128], Sb[:, h, 128:256], qdT[:, h, :], True, False, [hbf("Sb", hb), hbf("qdT", hb)], [prb])
                        mm(pr[:, hh * 128:(hh + 1) * 128], vnew[:, h, 128:256], qkT[:, h, :], False, True, [hbf("vnew", hb), hbf("qkT", hb)], [prb])
                    cp(f2(so, hb), po[:, 0:W_], [pob], [sob], eng="act")
                    cp(f2(sr, hb), pr[:, 0:W_], [prb], [srb], eng="act")
                each(s_o)

            def s_st(hb):
                hs = HS[hb]
                for h2i in range(hb * HB, (hb + 1) * HB, 2):
                    p3, p3b = psum()
                    for hh in range(2):
                        h = h2i + hh
                        mm(p3[:, hh * 256:(hh + 1) * 256], kd[:, h, :], vnew[:, h, :], True, True, [hbf("kd", hb), hbf("vnew", hb)], [p3b])
                    for hh in range(2):
                        h = h2i + hh
                        stt(Sf[:, h, :], Sf[:, h, :], sm["gle"][:, h:h + 1], p3[:, hh * 256:(hh + 1) * 256], ALU.mult, ALU.add,
                            [p3b, hbf("Sf", hb), sB("gle")], [hbf("Sf", hb)])
                cp(Sb[:, hs, :], Sf[:, hs, :], [hbf("Sf", hb)], [hbf("Sb", hb)], eng="act")
            each(s_st)
            if not FUSED:
                dma("sp", os_d[gci], so[:].rearrange("p h f -> p (h f)"), [sob], [bf(f"os{gci}")], f"ost{gci % 2}")
                dma("sp", rs_d[gci], sr[:].rearrange("p h f -> p (h f)"), [srb], [bf(f"rs{gci}")], f"rst{gci % 2}")

        blocks = [(0, 4, True)] + [(4 + i * TB, TB, False) for i in range(NB)]
        for bi, (col0, N, is_halo) in enumerate(blocks if part in (0, 1) else []):
            xi = bi % 2
            xcur = xt[xi]
            xb = [bf("xt0")]
            dma("sp", xcur[:, :, 0:N], xT_d[:, :, col0:col0 + N], [], xb, f"xld{xi}")
            ffn(xcur, xb, N, 0, f1in_d, f1out_d)
            norm_mod(xcur, xb, N, 1, h2, [bf("h2")])
            if not is_halo:
                t0 = col0 - 4
                dma("sp", x1s_d[:, :, t0:t0 + N], xcur[:, :, 0:N], xb, [bf(f"x1s{bi}")], f"x1st{xi}")
                dma("sp", h2s_d[:, :, t0:t0 + N], h2[:, :, 0:N], [bf("h2")], [bf(f"h2s{bi}")], "h2st")
            E.barrier()
            qkv_conv(N, [bf("h2")], is_halo)
            if is_halo:
                chk(4)
            else:
                chk(5)
            if not is_halo:
                for ci in range(CPB):
                    chunk_solve(ci * 128, (bi - 1) * CPB + ci)
                    chk(6)
            E.barrier()

        chk(7)
        if part in (0, 1):
            dma("sp", st_in_t.ap(), Sf[:].rearrange("p h f -> p (h f)"), [bf("Sf")], [bf("st_in")], "stio")
        if part == 1:
            raise Stop()
        if G > 1 and part == 0:
            groups = [list(range(g0, g0 + G)) for g0 in range(0, c.NCORES, G)]
            E.op("pool", lambda e: e.collective_compute("AllGather", ALU.bypass, replica_groups=groups,
                                                        ins=[st_in_t.ap().opt()], outs=[st_all_t.ap().opt()]),
                 [bf("st_in")], [bf("st_all")], dma="ccsem")
        if not FUSED:
            E.op("dve", lambda e: e.memset(Sst[:], 0.0), [], [bf("Sst")])
        for i in (range(G - 1) if not FUSED else []):
            dma("sp", stg[i][:].rearrange("p h f -> p (h f)"), st_all_t.ap()[i * 128:(i + 1) * 128, :], [bf("st_all")],
                [bf("stg0")], f"stgl{i}")
            for hb in range(NHB):
                pT, pTb = psum()
                for hh in range(HB):
                    h = hb * HB + hh
                    tr(pT[:, hh * 128:(hh + 1) * 128], stg[i][:, h, 128:256], identf[:], [bf("stg0"), bf("identf")], [pTb])
                cp(PiT[:, hb * HB:(hb + 1) * HB, :].rearrange("p h f -> p (h f)"), pT[:, 0:HB * 128], [pTb], [bf("PiT")])
            for hb in range(NHB):
                hs = slice(hb * HB, (hb + 1) * HB)
                pc, pcb = psum()
                for hh in range(HB):
                    h = hb * HB + hh
                    mm(pc[:, hh * 128:(hh + 1) * 128], PiT[:, h, :], Sst[:, h, :], True, True, [bf("PiT"), bf("Sst")], [pcb])
                tt(cand[:, hs, :], pc[:, 0:HB * 128].rearrange("p (h f) -> p h f", h=HB), stg[i][:, hs, 0:128], ALU.add,
                   [pcb, bf("stg0")], [bf("cand")])
            tt(cand[:], cand[:], Sst[:], ALU.subtract, [bf("cand"), bf("Sst")], [bf("cand")])
            stt(Sst[:], cand[:], masks[:, 1 + i:2 + i], Sst[:], ALU.mult, ALU.add, [bf("cand"), bf("masks"), bf("Sst")],
                [bf("Sst")])
        if not FUSED:
            cp(Sstb[:], Sst[:], [bf("Sst")], [bf("Sstb")])
        E.barrier()

        chk(8)
        def phase2_tb(tbi):
            t0 = tbi * TB
            N = TB
            xi = tbi % 2
            xcur = xt[xi]
            xb = [bf("xt0")]
            if not FUSED:
                dma("sp", xcur[:], x1s_d[:, :, t0:t0 + N], [bf(f"x1s{tbi + 1}")], xb, f"xld{xi}")
                dma("sp", h2[:], h2s_d[:, :, t0:t0 + N], [bf(f"h2s{tbi + 1}")], [bf("h2")], "h2ld")
            dma("pool", wv[:], win_d[:, c.OFF_V:c.OFF_V + D].rearrange("(kt p) n -> p kt n", p=128), [bf("g_w_in")], [bf("wv")], "wvld")
            h2r = lambda kt: (h2[:, kt, 0:N], [bf("h2")])

            linear_fm(win_d, c.OFF_Z, KT, KT, h2r, N,
                      lambda ft, p_ap, p_b: act(zs[:, ft, :], p_ap, AF.Silu, [p_b], [bf("zs")]))
            linear_fm(win_d, c.OFF_U, KT, KT, h2r, N,
                      lambda ft, p_ap, p_b: act(ug[:, ft, :], p_ap, AF.Gelu, [p_b], [bf("ug")]))
            def o_part(ci):
                gci = tbi * CPB + ci
                csl = slice(ci * 128, (ci + 1) * 128)
                so, sr = ostage[gci % 2], rstage[gci % 2]
                sob, srb = bf(f"ostage{gci % 2}"), bf(f"rstage{gci % 2}")
                if FUSED:
                    cp(otrue[:], oTtb[:, :, csl], [bf("oTtb")], [bf("otrue")])
                else:
                    dma("sp", so[:].rearrange("p h f -> p (h f)"), os_d[gci], [bf(f"os{gci}")], [sob], f"old{gci % 2}")
                    dma("sp", sr[:].rearrange("p h f -> p (h f)"), rs_d[gci], [bf(f"rs{gci}")], [srb], f"rld{gci % 2}")
                for hb in (range(NHB) if not FUSED else []):
                    hs = slice(hb * HB, (hb + 1) * HB)
                    pc, pcb = psum()
                    for hh in range(HB):
                        h = hb * HB + hh
                        mm(pc[:, hh * 128:(hh + 1) * 128], Sstb[:, h, :], sr[:, h, :], True, True, [bf("Sstb"), srb], [pcb])
                    tt(otrue[:, hs, :], pc[:, 0:HB * 128].rearrange("p (h f) -> p h f", h=HB), so[:, hs, :], ALU.add,
                       [pcb, sob], [bf("otrue")])
                act(osq[:], otrue[:], AF.Square, [bf("otrue")], [bf("osq")])
                yield
                for hb in range(NHB):
                    hs = slice(hb * HB, (hb + 1) * HB)
                    pn, pnb = psum()
                    mm(pn[:, 0:HB * 128], onesV[:], osq[:, hs, :].rearrange("p h f -> p (h f)"), True, True,
                       [bf("onesV"), bf("osq")], [pnb])
                    act(ort[:, hs, :].rearrange("p h f -> p (h f)"), pn[:, 0:HB * 128], AF.Ln, [pnb], [bf("ort")], bias=EPS)
                yield
                act(ors[:], ort[:], AF.Exp, [bf("ort")], [bf("ort")], scale=-0.5)
                yield
                tt(otrue[:], otrue[:], ors[:], ALU.mult, [bf("otrue"), bf("ort")], [bf("otrue")])
                yield
                stt(obT[:, :, csl], otrue[:], vecs[:, c.V_DNG:c.V_DNG + 1], zs[:, :, csl], ALU.mult, ALU.mult,
                    [bf("otrue"), bf("vecs"), bf("zs")], [bf("obT")])
            def g_part(ci):
                csl = slice(ci * 128, (ci + 1) * 128)
                for o0 in range(0, D, 512):
                    n = min(512, D - o0)
                    p_t, p_b = psum()
                    for kt in range(KT):
                        mm(p_t[:, 0:n], h2[:, kt, csl], wv[:, kt, o0:o0 + n], kt == 0, kt == KT - 1, [bf("h2"), bf("wv")], [p_b])
                    act(vg[:, o0:o0 + n], p_t[:, 0:n], AF.Gelu, [p_b], [bf("vg")])
                    yield
                SS = lambda n: st[n][:]
                sBB = lambda n: bf("st_" + n)
                E.op("dve", lambda e: e.tensor_reduce(out=st["s1"][:], in_=vg[:], axis=AX.X, op=ALU.add), [bf("vg")], [sBB("s1")])
                tt(vsq[:], vg[:], vg[:], ALU.mult, [bf("vg")], [bf("mtmp")])
                yield
                E.op("dve", lambda e: e.tensor_reduce(out=st["s2"][:], in_=vsq[:], axis=AX.X, op=ALU.add), [bf("mtmp")], [sBB("s2")])
                ts(SS("mean"), SS("s1"), 1.0 / D, None, ALU.mult, None, [sBB("s1")], [sBB("mean")])
                yield
                tt(SS("msq"), SS("mean"), SS("mean"), ALU.mult, [sBB("mean")], [sBB("msq")])
                stt(SS("var"), SS("s2"), 1.0 / D, SS("msq"), ALU.mult, ALU.subtract, [sBB("s2"), sBB("msq")], [sBB("var")])
                yield
                act(SS("sd"), SS("var"), AF.Sqrt, [sBB("var")], [sBB("sd")], bias=EPS)
                recip(SS("rstd"), SS("sd"), [sBB("sd")], [sBB("rstd")])
                yield
                ts(nrm[:], vg[:], SS("mean"), SS("rstd"), ALU.subtract, ALU.mult, [bf("vg"), sBB("mean"), sBB("rstd")], [bf("nrm")])
                yield
                for gb in range(0, KT, 4):
                    ng = min(4, KT - gb)
                    pM, pMb = psum()
                    for gg in range(ng):
                        g = gb + gg
                        mm(pM[:, gg * 128:(gg + 1) * 128], nrm[:, g * 128:(g + 1) * 128], WmT[:, g, :], True, True,
                           [bf("nrm"), bf("WmT")], [pMb])
                    for gg in range(ng):
                        g = gb + gg
                        stt(mtmp[:, g, :], pM[:, gg * 128:(gg + 1) * 128], vecs[:, c.V_LNG + g:c.V_LNG + g + 1], BiasG[:, g, :],
                            ALU.mult, ALU.add, [pMb, bf("vecs"), bf("BiasG")], [bf("mtmp")])
                    yield
                tt(oaT[:, :, csl], mtmp[:], ug[:, :, csl], ALU.mult, [bf("mtmp"), bf("ug")], [bf("oaT")])
            for ci in range(CPB):
                gens = [o_part(ci), g_part(ci)]
                while gens:
                    for g_ in list(gens):
                        try:
                            next(g_)
                        except StopIteration:
                            gens.remove(g_)
            linear_fm(win_d, c.OFF_GATE, 2 * KT, KT, h2r, N,
                      lambda ft, p_ap, p_b: act(gt[:, ft, :], p_ap, AF.Sigmoid, [p_b], [bf("gt")]))
            linear_fm(wbr_d[0], 0, KT, KT, lambda kt: (oaT[:, kt, :], [bf("oaT")]), N,
                      lambda ft, p_ap, p_b: tt(mA[:, ft, :], p_ap, gt[:, ft, :], ALU.mult, [p_b, bf("gt")], [bf("mA")]))

            def ev_brB(ft, p_ap, p_b):
                tt(mB[:], p_ap, gt[:, KT + ft, :], ALU.mult, [p_b, bf("gt")], [bf("mB")])
                tt(mergedT[:, ft, :], mB[:], mA[:, ft, :], ALU.add, [bf("mB"), bf("mA")], [bf("mergedT")])

            linear_fm(wbr_d[1], 0, KT, KT, lambda kt: (obT[:, kt, :], [bf("obT")]), N, ev_brB)
            linear_fm(wout_d, 0, KT, KT, lambda kt: (mergedT[:, kt, :], [bf("mergedT")]), N,
                      lambda ft, p_ap, p_b: stt(xcur[:, ft, :], p_ap, hga[:, KT + ft:KT + ft + 1], xcur[:, ft, :], ALU.mult,
                                                ALU.add, [p_b, bf("hga")] + xb, xb))
            E.barrier()
            ffn(xcur, xb, N, 2, f3in_d, f3out_d)
            rms_stats(lambda kt: xcur[:, kt, 0:N], xb, N, onesD, bf("onesD"))
            og = ostg[tbi % 2]
            ogb = bf("ostg0")
            for kt in range(KT):
                stt(og[:, kt, :], xcur[:, kt, :], vecs[:, c.V_NF + kt:c.V_NF + kt + 1], rstd[:, 0:N], ALU.mult, ALU.mult,
                    xb + [bf("vecs"), bf("rstd")], [ogb])
            dma("sp", outT_d[:, :, t0:t0 + N], og[:], [ogb], [bf(f"out{tbi}")], f"outst{tbi % 2}")
            E.barrier()

        if not FUSED:
            for tbi in range(NB):
                phase2_tb(tbi)
        else:
            E.op("pool", lambda e: e.memset(tails[:], 0.0), [], [bf("tails")])
            blist = [(wi, tbi) for wi in range(G) for tbi in range(NB)]

            def stageA_gen(wi, tbi):
                col0 = wi * NT + tbi * TB
                xcur = xt[0]
                xb = [bf("xt0")]
                dma("sp", xcur[:, :, 0:TB], xT_d[:, :, col0:col0 + TB], [], xb, "xld0")
                yield from ffn_gen(xcur, xb, TB, 0, f1in_d, f1out_d)
                norm_mod(xcur, xb, TB, 1, h2, [bf("h2")])
                yield

            for _ in stageA_gen(*blist[0]):
                pass
            for bi_, (wi, tbi) in enumerate(blist):
                own = (wi == G - 1)
                vm = None if own else masks[:, wi:wi + 1]
                N = TB
                lastp = (wi == G - 2 and tbi == NB - 1)
                if own:
                    E.barrier()
                qkv_conv(N, [bf("h2")], False, vm=vm, own=(own or lastp), q_tails_only=lastp)
                for ci in range(CPB):
                    ab_proj(ci * 128, ci)
                small_chain_tb(vm)
                nxt_blk = blist[bi_ + 1] if bi_ + 1 < len(blist) else None
                if (not own) and nxt_blk is not None:
                    FILL[0] = stageA_gen(*nxt_blk)
                for ci in range(CPB):
                    chunk_solve(ci * 128, 0, own=own, vm=vm, ci=ci)
                if FILL[0] is not None:
                    for _ in FILL[0]:
                        pass
                    FILL[0] = None
                if own:
                    E.barrier()
                    phase2_tb(tbi)
                    if nxt_blk is not None:
                        for _ in stageA_gen(*nxt_blk):
                            pass
        return B

    Ep = Emit(nc, plan_only=True)
    try:
        program(Ep)
    except Stop:
        pass
    Ee = Emit(nc, plan_only=False, wplan=Ep.wreq)
    try:
        program(Ee)
    except Stop:
        pass

    keys = set()
    for eng, lst in Ee.ops.items():
        for waits, fn, inc in lst:
            keys.add(inc[0])
            for k, v in waits:
                keys.add(k)
    sem = {k: nc.alloc_semaphore(name="s_" + k) for k in sorted(keys)}
    final = dict(Ee.dcnt)

    with nc.Block() as block:
        def replay(e, name, last=False):
            for waits, fn, inc in Ee.ops[name]:
                for k, v in waits:
                    e.wait_ge(sem[k], v)
                ins = fn(e)
                ins.then_inc(sem[inc[0]], inc[1])
            if last:
                for k, v in final.items():
                    e.wait_ge(sem[k], v)
                for k in ("pe", "act", "dve", "pool"):
                    e.wait_ge(sem[k], Ee.cnt[k])

        @block.tensor
        def _(e):
            replay(e, "pe")

        @block.scalar
        def _(e):
            replay(e, "act")

        @block.vector
        def _(e):
            replay(e, "dve")

        @block.gpsimd
        def _(e):
            replay(e, "pool")

        @block.sync
        def _(e):
            replay(e, "sp", last=True)

    nc._n_ops = {k: len(v) for k, v in Ee.ops.items()}
    return nc


def host_inputs(cfg, inp, fused=False):
    c = cfg
    D, KT, H, NT, G = c.D, c.KT, c.H, c.NT, c.G
    f32 = np.float32
    x = np.asarray(inp["x"], f32)
    cc = np.asarray(inp["c"], f32)

    def pk(v):
        return np.asarray(v, f32).reshape(-1, 128).T

    vecs = np.zeros((128, c.NV), f32)
    vecs[:, c.V_N1:c.V_N1 + KT] = pk(inp["norm1_g"][0])
    vecs[:, c.V_N2:c.V_N2 + KT] = pk(inp["norm2_g"][0])
    vecs[:, c.V_N3:c.V_N3 + KT] = pk(inp["norm3_g"][0])
    vecs[:, c.V_NF:c.V_NF + KT] = pk(inp["final_g"])
    vecs[:, c.V_BADA:c.V_BADA + 9 * KT] = pk(inp["b_ada"][0])
    vecs[:, c.V_LNG:c.V_LNG + KT] = pk(inp["gm_ln_g"][0])
    vecs[:, c.V_DNG] = np.asarray(inp["dn_norm_g"][0], f32)
    cw = np.asarray(inp["conv_w"][0], f32)
    for j in range(4):
        vecs[:, c.V_CONV + j * 3 * KT: c.V_CONV + (j + 1) * 3 * KT] = pk(cw[j])
    rowv = np.concatenate([np.asarray(inp["gm_ln_b"][0], f32), np.asarray(inp["gm_b_s"][0], f32).reshape(-1)])[None, :]
    bc8 = np.tile(np.concatenate([np.asarray(inp["a_log"][0], f32), np.asarray(inp["dt_bias"][0], f32)])[None, :], (128, 1))
    w_sT = np.ascontiguousarray(np.transpose(np.asarray(inp["gm_w_s"][0], f32), (0, 2, 1)))
    shared = {
        "vecs": vecs, "rowv": np.ascontiguousarray(rowv), "bc8": np.ascontiguousarray(bc8),
        "w_sT": w_sT,
    }
    wfulls = {"w_ada": inp["w_ada"][0], "ffn1_w_in": inp["ffn1_w_in"][0], "ffn1_w_out": inp["ffn1_w_out"][0],
              "w_in": inp["w_in"][0], "w_branch": np.asarray(inp["w_branch"][0]).reshape(2 * D, D),
              "w_out": inp["w_out"][0], "ffn2_w_in": inp["ffn2_w_in"][0], "ffn2_w_out": inp["ffn2_w_out"][0]}
    wfulls = {k: np.asarray(v, f32) for k, v in wfulls.items()}
    ii = np.arange(128)
    bm = np.zeros((128, 4, 128), f32)
    bm[:, 0, :] = (ii[:, None] // 16 == ii[None, :] // 16)
    for i, sz in enumerate((16, 32, 64)):
        bm[:, 1 + i, :] = (ii[:, None] // (2 * sz) == ii[None, :] // (2 * sz)) & (ii[:, None] // sz != ii[None, :] // sz)
    maps = []
    for r in range(c.NCORES):
        b, j = r // G, r % G
        s0 = j * NT
        m = np.zeros((128, G), f32)
        if fused:
            xs = np.zeros((G * NT, D), f32)
            for sgi in range(G):
                seg = j - (G - 1) + sgi
                if seg >= 0:
                    xs[sgi * NT:(sgi + 1) * NT] = x[b, seg * NT:(seg + 1) * NT]
                    m[:, sgi] = 1.0
            xT = np.ascontiguousarray(xs.T.reshape(KT, 128, G * NT).transpose(1, 0, 2))
        else:
            xs = np.zeros((NT + 4, D), f32)
            xs[4:] = x[b, s0:s0 + NT]
            if j > 0:
                xs[:4] = x[b, s0 - 4:s0]
            xT = np.ascontiguousarray(xs.T.reshape(KT, 128, NT + 4).transpose(1, 0, 2))
            m[:, 0] = 1.0 if j > 0 else 0.0
            for i in range(G - 1):
                m[:, 1 + i] = 1.0 if i < j else 0.0
        d = dict(shared)
        for k, v in wfulls.items():
            rp = v.shape[0] // c.NCORES
            d[k] = np.ascontiguousarray(v[r * rp:(r + 1) * rp]) if c.WG else v
        d["xT"] = xT
        d["cT"] = np.ascontiguousarray(cc[b].reshape(KT, 128).T)
        d["masks"] = m
        d["bmask"] = bm
        maps.append(d)
    return maps


def host_output(cfg, results, B):
    c = cfg
    out = np.zeros((B, c.G * c.NT, c.D), np.float32)
    for r in range(c.NCORES):
        b, j = r // c.G, r % c.G
        oT = np.asarray(results[r]["outT"], np.float32).reshape(128, c.KT, c.NT)
        out[b, j * c.NT:(j + 1) * c.NT] = oT.transpose(2, 1, 0).reshape(c.NT, c.D)
    return out


_NC_CACHE = {}


def run_two_launch(cfg, inputs, nbatch):
    key = ("two", cfg.D, cfg.NT, cfg.NCORES)
    if key not in _NC_CACHE:
        _NC_CACHE[key] = (build(cfg, part=1), build(cfg, part=2))
    nc1, nc2 = _NC_CACHE[key]
    maps = host_inputs(cfg, inputs)
    res1 = run_bass_kernel_spmd(nc1, maps, core_ids=list(range(cfg.NCORES))).results
    G = cfg.G
    maps2 = []
    for r in range(cfg.NCORES):
        g0 = (r // G) * G
        d = dict(maps[r])
        for k in ("x1s", "h2s", "o_s", "r_s"):
            d[k] = np.ascontiguousarray(res1[r][k])
        d["st_all"] = np.ascontiguousarray(np.concatenate(
            [np.asarray(res1[g0 + i]["st_in"]).reshape(128, cfg.H * 256) for i in range(G)], axis=0))
        maps2.append(d)
    res2 = run_bass_kernel_spmd(nc2, maps2, core_ids=list(range(cfg.NCORES))).results
    return host_output(cfg, res2, nbatch)


def run_fused(cfg, inputs, nbatch):
    key = ("fused", cfg.D, cfg.NT, cfg.NCORES)
    if key not in _NC_CACHE:
        _NC_CACHE[key] = build(cfg, part=3)
    nc = _NC_CACHE[key]
    maps = host_inputs(cfg, inputs, fused=True)
    res = run_bass_kernel_spmd(nc, maps, core_ids=list(range(cfg.NCORES))).results
    return host_output(cfg, res, nbatch)


def kernel(**inputs):
    cfg = Cfg(WG=False)
    return run_fused(cfg, inputs, 2)
```

```python
import numpy as np
import ml_dtypes
from contextlib import ExitStack
import concourse.bass as bass
import concourse.mybir as mybir
from concourse.bass_utils import run_bass_kernel_spmd

F32 = mybir.dt.float32
BF16 = mybir.dt.bfloat16
AF = mybir.ActivationFunctionType
ALU = mybir.AluOpType
AX = mybir.AxisListType
EPS = 1e-6
NEGBIG = -1.0e30


class Cfg:
    def __init__(self, D=1024, DFF=2816, H=8, NT=2048, TB=512, G=4, NCORES=8, WG=True):
        self.WG = WG
        self.STOP = 0
        self.D, self.DFF, self.H, self.NT, self.TB, self.G, self.NCORES = D, DFF, H, NT, TB, G, NCORES
        self.KT = D // 128
        self.FT = DFF // 128
        self.NB = NT // TB
        self.CPB = TB // 128
        self.NCH = NT // 128
        assert H * 128 == D
        self.INW = 6 * D + 2 * H + 2 * D
        self.OFF_U, self.OFF_V, self.OFF_Q, self.OFF_Z = 0, D, 2 * D, 5 * D
        self.OFF_AB = 6 * D
        self.OFF_GATE = 6 * D + 2 * H
        KT = self.KT
        self.V_N1, self.V_N2, self.V_N3, self.V_NF = 0, KT, 2 * KT, 3 * KT
        self.V_BADA = 4 * KT
        self.V_LNG = 13 * KT
        self.V_DNG = 14 * KT
        self.V_CONV = 14 * KT + 1
        self.NV = self.V_CONV + 4 * 3 * KT
        self.WSLOT = max(self.FT * 128, KT * 256)


class Stop(Exception):
    pass


class Buf:
    __slots__ = ("w", "r", "name")

    def __init__(self, name=""):
        self.w = None
        self.r = {}
        self.name = name


class Emit:
    def __init__(self, nc, plan_only, wplan=None):
        self.nc = nc
        self.plan_only = plan_only
        self.ops = {e: [] for e in ("pe", "act", "dve", "pool", "sp")}
        self.cnt = {e: 0 for e in ("pe", "act", "dve", "pool")}
        self.waited = {e: {} for e in self.ops}
        self.dcnt = {}
        self.wreq = []
        self.wplan = wplan
        self.wissued = 0
        self.wconsumed = 0
        self.pend = {}

    def barrier(self):
        snap = dict(self.cnt)
        snap.update(self.dcnt)
        for e in self.ops:
            p = self.pend.get(e) or {}
            for k, v in snap.items():
                if v > p.get(k, 0):
                    p[k] = v
            self.pend[e] = p

    def op(self, eng, fn, reads=(), writes=(), dma=None):
        waits = {}

        def need(tok):
            if tok is None:
                return
            key, val = tok
            if key == eng and eng == "pe":
                return
            if self.waited[eng].get(key, 0) >= val:
                return
            if waits.get(key, 0) < val:
                waits[key] = val

        p = self.pend.get(eng)
        if p:
            for k, v in p.items():
                if v > 0:
                    need((k, v))
            self.pend[eng] = None
        for b in reads:
            need(b.w)
        for b in writes:
            need(b.w)
            for k, v in b.r.items():
                need((k, v))
        for k, v in waits.items():
            self.waited[eng][k] = v
        if dma is None:
            self.cnt[eng] += 1
            tok = (eng, self.cnt[eng])
            inc = (eng, 1)
        else:
            self.dcnt[dma] = self.dcnt.get(dma, 0) + 16
            tok = (dma, self.dcnt[dma])
            inc = (dma, 16)
        for b in reads:
            if b.r.get(tok[0], 0) < tok[1]:
                b.r[tok[0]] = tok[1]
        for b in writes:
            b.w = tok
            b.r = {}
        if not self.plan_only:
            self.ops[eng].append((list(waits.items()), fn, inc))
        return tok


def build(cfg, part=0):
    c = cfg
    D, KT, DFF, FT, H, NT, TB, NB, CPB, NCH, G = c.D, c.KT, c.DFF, c.FT, c.H, c.NT, c.TB, c.NB, c.CPB, c.NCH, c.G
    nc = bass.Bass("TRN2", target_bir_lowering=False)
    FUSED = (part == 3)

    def din(name, shape, dt=F32):
        return nc.dram_tensor(name, list(shape), dt, kind="ExternalInput").ap()

    xT_d = (din("xT", [128, KT, G * NT]) if FUSED else din("xT", [128, KT, NT + 4])) if part != 2 else None
    cT_d = din("cT", [128, KT])
    vecs_d = din("vecs", [128, c.NV])
    rowv_d = din("rowv", [1, 2 * D])
    bc8_d = din("bc8", [128, 2 * H])
    masks_d = din("masks", [128, G])
    bmask_d = din("bmask", [128, 4, 128])
    NCs = c.NCORES
    WSPEC = [("w_ada", D, 9 * D), ("ffn1_w_in", D, 2 * DFF), ("ffn1_w_out", DFF, D), ("w_in", D, c.INW),
             ("w_branch", 2 * D, D), ("w_out", D, D), ("ffn2_w_in", D, 2 * DFF), ("ffn2_w_out", DFF, D)]
    wext, wbnc, wfull = {}, {}, {}
    wap = {}
    if part == 1:
        WSPEC = [w for w in WSPEC if w[0] in ("w_ada", "ffn1_w_in", "ffn1_w_out", "w_in")]
    if part == 2:
        WSPEC = [w for w in WSPEC if w[0] not in ("ffn1_w_in", "ffn1_w_out")]
    for (wn, wr, wc) in WSPEC:
        if c.WG:
            wext[wn] = din(wn, [wr // NCs, wc])
            wbnc[wn] = nc.dram_tensor(wn + "_bnc", [wr // NCs, wc], F32)
            wfull[wn] = nc.dram_tensor(wn + "_full", [wr, wc], F32)
            wap[wn] = wfull[wn].ap()
        else:
            wap[wn] = din(wn, [wr, wc])
    w_ada_d = wap["w_ada"]
    f1in_d = wap.get("ffn1_w_in")
    f1out_d = wap.get("ffn1_w_out")
    win_d = wap["w_in"]
    wsT_d = din("w_sT", [KT, 128, 128])
    wbr_full = wap.get("w_branch")
    wbr_d = [wbr_full[0:D, :], wbr_full[D:2 * D, :]] if wbr_full is not None else None
    wout_d = wap.get("w_out")
    f3in_d = wap.get("ffn2_w_in")
    f3out_d = wap.get("ffn2_w_out")
    outT_d = nc.dram_tensor("outT", [128, KT, NT], F32, kind="ExternalOutput").ap() if part != 1 else None

    skw = {} if part in (0, 3) else {"kind": ("ExternalOutput" if part == 1 else "ExternalInput")}
    x1s_d = nc.dram_tensor("x1s", [128, KT, NT], F32, **skw).ap()
    h2s_d = nc.dram_tensor("h2s", [128, KT, NT], BF16, **skw).ap()
    os_d = nc.dram_tensor("o_s", [NCH, 128, H * 128], BF16, **skw).ap()
    rs_d = nc.dram_tensor("r_s", [NCH, 128, H * 128], BF16, **skw).ap()
    if part != 2:
        st_in_t = nc.dram_tensor("st_in", [128, H * 256], F32, **skw)
    akw = {} if part in (0, 3) else {"kind": "ExternalInput"}
    if part != 1:
        st_all_t = nc.dram_tensor("st_all", [G * 128, H * 256], F32, **akw)

    es = ExitStack()
    T = {}

    def sb(name, shape, dt):
        T[name] = nc.alloc_sbuf_tensor(name, list(shape), dt)
        return T[name]

    ARENA_E = 57 * 1024 - 512
    arena = nc.alloc_sbuf_tensor("arena", [128, ARENA_E], BF16)
    vptr = {}

    def cv(view, name, shape, dt):
        n = 1
        for d_ in shape[1:]:
            n *= d_
        ne = n * (2 if dt == F32 else 1)
        ne = (ne + 15) // 16 * 16
        off = vptr.get(view, 0)
        vptr[view] = off + ne
        assert off + ne <= ARENA_E, (view, name, off + ne)
        ap = arena[:, off:off + ne]
        if dt == F32:
            ap = ap.bitcast(F32)
        ap = ap[:, 0:n]
        if len(shape) == 3:
            ap = ap.rearrange("p (a b) -> p a b", a=shape[1])
        T[name] = ap
        return ap

    identb = sb("identb", [128, 128], BF16)
    identf = sb("identf", [128, 128], F32)
    onesb = sb("onesb", [128, 128], BF16)
    onesD = sb("onesD", [128, 128], BF16)
    onesV = sb("onesV", [128, 128], BF16)
    onesf = sb("onesf", [128, 128], F32)
    tri = sb("tri", [128, 128], F32)
    sel127 = sb("sel127", [128, 128], F32)
    neg1 = sb("neg1", [128, 128], F32)
    neg2 = sb("neg2", [128, 128], F32)
    vecs = sb("vecs_sb", [128, c.NV], F32)
    bc8 = sb("bc8_sb", [128, 2 * H], F32)
    masks = sb("masks_sb", [128, G], F32)
    bmask = sb("bmask_sb", [128, 4, 128], F32)
    ealog = sb("ealog", [128, H], F32)
    cTf = sb("cTf", [128, KT], F32)
    cact = sb("cact", [128, KT], BF16)
    modT = sb("modT", [128, 9 * KT], F32)
    gsc = sb("gsc", [128, 3 * KT], F32)
    hga = sb("hga", [128, 3 * KT], F32)
    wab = sb("wab", [128, KT, 2 * H], BF16)
    WmT = sb("WmT", [128, KT, 128], BF16)
    BiasG = sb("BiasG", [128, KT, 128], F32)
    NWS = 5
    wslot = [sb(f"wslot{i}", [128, c.WSLOT], BF16) for i in range(NWS)]
    xt = [sb("xt0", [128, KT, TB], F32)]
    xt.append(xt[0])
    h2 = sb("h2", [128, KT, TB], BF16)
    sq = [sb(f"sq{i}", [128, TB], BF16) for i in range(2)]
    rt = sb("rt", [128, TB], F32)
    rstd = sb("rstd", [128, TB], F32)
    ntmp = [sb(f"ntmp{i}", [128, TB], F32) for i in range(2)]
    tails = sb("tails", [128, 3 * KT, 4], BF16)
    SW = 128 if FUSED else 256
    upad = sb("upad", [128, H, SW], BF16)
    Sf = sb("Sf", [128, H, SW], F32)
    Sb = sb("Sb", [128, H, SW], BF16)
    ostage = [sb(f"ostage{i}", [128, H, 128], BF16) if not FUSED else None for i in range(2)]
    rstage = [sb(f"rstage{i}", [128, H, 128], BF16) if not FUSED else None for i in range(2)]
    Sstb = sb("Sstb", [128, H, 128], BF16) if not FUSED else None
    oTtb = sb("oTtb", [128, H, TB], BF16) if FUSED else None
    rowv = cv("S", "rowv_sb", [1, 2 * D], F32)[0:1, :]
    wsTf = cv("S", "wsTf", [128, KT, 128], F32)
    rsrow = cv("S", "rsrow", [1, KT * 128], F32)[0:1, :]
    pre = [cv("P1", f"pre{i}", [128, 4 + TB], BF16) for i in range(3)]
    dslot = [cv("P1", f"dslot{i}", [128, 4, 128], BF16) for i in range(2)]
    qraw = [cv("P1", f"qraw{i}", [128, TB], F32) for i in range(2)]
    knT = cv("P1", "knT", [128, H, TB], BF16)
    vT = cv("P1", "vT", [128, H, TB], BF16)
    ab = cv("P1", "ab", [128, 2 * H], F32)
    abT = cv("P1", "abT", [128, CPB, 2 * H], F32)
    sm = {n: cv("P1", "sm_" + n, [128, H], F32) for n in
          ("e", "beta", "x", "ex", "sp", "g", "CB", "eCB", "kbs", "dka", "dke", "gle")}
    smT = {n: cv("P1", "smT_" + n, [128, CPB * H], F32) for n in
           ("e", "beta", "x", "ex", "sp", "g", "CB", "eCB", "kbs", "dka", "dke", "gle")}
    dg = cv("P1", "tw", [128, H, 128], F32)
    t1 = dg
    t2 = dg
    RB = cv("P1", "RB", [128, H, 128], F32)
    E1 = cv("P1", "E1", [128, H, 128], BF16)
    E1b = E1
    Nb = [cv("P1", f"Nb{i}", [128, H, 128], BF16) for i in range(2)]
    NTb = [cv("P1", f"NTb{i}", [128, H, 128], BF16) for i in range(2)]
    Ub = [cv("P1", f"Ub{i}", [128, H, 128], BF16) for i in range(2)]
    Afull = cv("P1", "Afull", [128, H, 128], BF16)
    ATfull = cv("P1", "ATfull", [128, H, 128], BF16)
    As1 = cv("P1", "As", [128, H, 128], BF16)
    Ms1 = cv("P1", "Ms", [128, H, 128], BF16)
    Tb = [cv("P1", f"Tb{i}", [128, H, 128], BF16) for i in range(2)]
    Wb = cv("P1", "Wb", [128, H, 128], BF16)
    Vb = cv("P1", "Vb", [128, H, 128], BF16)
    kd = cv("P1", "kd", [128, H, 128], BF16)
    kbg = cv("P1", "kbg", [128, H, 128], BF16)
    vb = cv("P1", "vb", [128, H, 128], BF16)
    wT = cv("P1", "wT", [128, H, 128], BF16)
    vnew = cv("P1", "vnew", [128, H, SW], BF16)
    vptr["F"] = vptr["P1"]
    hT = cv("F", "hT", [128, KT, TB], BF16)
    gT = cv("F", "gT", [128, FT, TB], BF16)
    sa = [cv("F", f"sa{i}", [128, TB], BF16) for i in range(4)]
    ostg = [cv("FO", "ostg0", [128, KT, TB], F32)]
    ostg.append(ostg[0])
    qnT = cv("P1", "qnT", [128, H, TB], BF16)
    E2 = cv("P1", "E2", [128, H, 128], BF16)
    Eg = cv("P1", "Eg", [128, H, 128], BF16)
    qkT = cv("P1", "qkT", [128, H, 128], BF16)
    qdT = cv("P1", "qdT", [128, H, 128], BF16)
    stg = [cv("X", "stg0", [128, H, 256], F32)] * max(G - 1, 1)
    PiT = cv("X", "PiT", [128, H, 128], F32)
    Sst = cv("X", "Sst", [128, H, 128], F32)
    cand = cv("X", "cand", [128, H, 128], F32)
    otrue = cv("P2", "otrue", [128, H, 128], F32)
    osq = cv("P2", "osq", [128, H, 128], BF16)
    ort = cv("P2", "ort", [128, H, 128], F32)
    ors = ort
    obT = cv("P2", "obT", [128, KT, TB], BF16)
    zs = cv("P2", "zs", [128, KT, TB], BF16)
    ug = cv("P2", "ug", [128, KT, TB], BF16)
    vg = cv("P2", "vg", [128, D], F32)
    nrm = cv("P2", "nrm", [128, D], BF16)
    st = {n: cv("P2", "st_" + n, [128, 1], F32) for n in ("s1", "s2", "mean", "msq", "var", "sd", "rstd")}
    mtmp = cv("P2", "mtmp", [128, KT, 128], F32)
    vsq = mtmp.rearrange("p a b -> p (a b)")
    oaT = cv("P2", "oaT", [128, KT, TB], BF16)
    gt = cv("P2", "gt", [128, 2 * KT, TB], BF16)
    mA = cv("P2", "mA", [128, KT, TB], BF16)
    mB = cv("P2", "mB", [128, TB], F32)
    mergedT = cv("P2", "mergedT", [128, KT, TB], BF16)
    wv = cv("P2", "wv", [128, KT, D], BF16)

    ps = [nc.alloc_psum_tensor(f"ps{i}", [128, 512], F32) for i in range(8)]

    def program(E):
        B = {}

        def chk(k):
            if c.STOP == k:
                raise Stop()

        def bf(name):
            if name not in B:
                B[name] = Buf(name)
            return B[name]

        psb = [Buf(f"ps{i}") for i in range(8)]
        pstate = [0]

        def psum():
            i = pstate[0] % 8
            pstate[0] += 1
            return ps[i], psb[i]

        def dma(q, out, in_, reads, writes, sem):
            E.op(q, lambda e, o=out, i=in_: e.dma_start(out=o, in_=i), reads, writes, dma=sem)

        def act(out, in_, func, reads, writes, bias=None, scale=None):
            kw = {}
            if bias is not None:
                kw["bias"] = bias
            if scale is not None:
                kw["scale"] = scale
            E.op("act", lambda e, o=out, i=in_, f=func, k=kw: e.activation(out=o, in_=i, func=f, **k), reads, writes)

        def tt(out, in0, in1, op, reads, writes, eng="dve"):
            E.op(eng, lambda e, o=out, a=in0, b=in1, p=op: e.tensor_tensor(out=o, in0=a, in1=b, op=p), reads, writes)

        def ts(out, in0, s1, s2, op0, op1, reads, writes, eng="dve"):
            if s2 is None:
                E.op(eng, lambda e, o=out, a=in0, x=s1, p=op0: e.tensor_scalar(out=o, in0=a, scalar1=x, scalar2=None, op0=p),
                     reads, writes)
            else:
                E.op(eng, lambda e, o=out, a=in0, x=s1, y=s2, p=op0, q=op1: e.tensor_scalar(
                    out=o, in0=a, scalar1=x, scalar2=y, op0=p, op1=q), reads, writes)

        def stt(out, in0, scalar, in1, op0, op1, reads, writes, eng="dve"):
            E.op(eng, lambda e, o=out, a=in0, s=scalar, b=in1, p=op0, q=op1: e.scalar_tensor_tensor(
                out=o, in0=a, scalar=s, in1=b, op0=p, op1=q), reads, writes)

        def cp(out, in_, reads, writes, eng="dve"):
            if eng == "act":
                E.op("act", lambda e, o=out, i=in_: e.activation(out=o, in_=i, func=AF.Copy), reads, writes)
            else:
                E.op(eng, lambda e, o=out, i=in_: e.tensor_copy(out=o, in_=i), reads, writes)

        def recip(out, in_, reads, writes):
            E.op("dve", lambda e, o=out, i=in_: e.reciprocal(out=o, in_=i), reads, writes)

        def mm(out, lhsT, rhs, start, stop, reads, writes):
            E.op("pe", lambda e, o=out, l=lhsT, r=rhs, s=start, p=stop: e.matmul(o, lhsT=l, rhs=r, start=s, stop=p),
                 reads, writes)

        def tr(out, in_, ident, reads, writes):
            E.op("pe", lambda e, o=out, i=in_, d=ident: e.transpose(o, i, d), reads, writes)

        def bc_mid(ap2d, n):
            return ap2d.unsqueeze(1).broadcast_to([128, n, ap2d.shape[1]])

        def bc_last(ap2d, n):
            return ap2d.unsqueeze(2).broadcast_to([128, ap2d.shape[1], n])

        wslot_b = [Buf(f"wslot{i}") for i in range(NWS)]

        def w_issue(upto):
            plan = E.wplan
            while E.wissued < min(upto, len(plan)):
                j = E.wissued
                s = j % NWS
                for (o_fn, src, dep) in plan[j]:
                    dma("pool", o_fn(wslot[s]), src, [bf(dep)] if dep else [], [wslot_b[s]], f"w{s}")
                E.wissued += 1

        def w_get(loads):
            j = E.wconsumed
            E.wconsumed += 1
            if E.plan_only:
                E.wreq.append(loads)
                return wslot[j % NWS], wslot_b[j % NWS]
            w_issue(j + NWS - 1)
            return wslot[j % NWS], wslot_b[j % NWS]

        def linear_fm(Wd, col0, nft, KTin, rhs, N, evac, CW=2):
            for _ in linear_fm_gen(Wd, col0, nft, KTin, rhs, N, evac, CW):
                pass

        def linear_fm_gen(Wd, col0, nft, KTin, rhs, N, evac, CW=2):
            nchunk = (nft + CW - 1) // CW
            for ci in range(nchunk):
                f0 = ci * CW
                nf = min(CW, nft - f0)
                ncols = nf * 128
                src = Wd[:, col0 + f0 * 128: col0 + f0 * 128 + ncols].rearrange("(kt p) n -> p kt n", p=128)
                nm = Wd.name
                dep = ("g_" + nm[:-5]) if nm.endswith("_full") else None
                slot, sbuf_ = w_get([(lambda s, k=KTin, n=ncols: s[:, 0:k * n].rearrange("p (k n) -> p k n", k=k), src, dep)])
                wv_ = slot[:, 0:KTin * ncols].rearrange("p (k n) -> p k n", k=KTin)
                for fi in range(nf):
                    p_t, p_b = psum()
                    for kt in range(KTin):
                        r_ap, r_bufs = rhs(kt)
                        mm(p_t[:, 0:N], wv_[:, kt, fi * 128:(fi + 1) * 128], r_ap, kt == 0, kt == KTin - 1,
                           [sbuf_] + r_bufs, [p_b])
                    evac(f0 + fi, p_t[:, 0:N], p_b)
                    yield

        allc = [list(range(NCs))]
        for (wn, wr, wc) in (WSPEC if c.WG else []):
            dma("sp", wbnc[wn].ap(), wext[wn], [], [bf("b_" + wn)], "wb_" + wn)
            E.op("pool", lambda e, a=wbnc[wn], o=wfull[wn]: e.collective_compute(
                "AllGather", ALU.bypass, replica_groups=allc, ins=[a.ap().opt()], outs=[o.ap().opt()]),
                [bf("b_" + wn)], [bf("g_" + wn)], dma="cc_" + wn)
        E.op("pool", lambda e: e.memset(identf[:], 0.0), [], [bf("identf")])
        E.op("pool", lambda e: e.affine_select(out=identf[:], in_=identf[:], pattern=[[-1, 128]], compare_op=ALU.not_equal,
                                               fill=1.0, base=0, channel_multiplier=1), [bf("identf")], [bf("identf")])
        E.op("pool", lambda e: e.tensor_copy(out=identb[:], in_=identf[:]), [bf("identf")], [bf("identb")])
        E.op("pool", lambda e: e.memset(onesb[:], 1.0), [], [bf("onesb")])
        E.op("pool", lambda e: e.memset(onesD[:], 1.0 / D), [], [bf("onesD")])
        E.op("pool", lambda e: e.memset(onesV[:], 1.0 / 128), [], [bf("onesV")])
        E.op("pool", lambda e: e.memset(onesf[:], 1.0), [], [bf("onesf")])
        E.op("pool", lambda e: e.memset(tri[:], 1.0), [], [bf("tri")])
        E.op("pool", lambda e: e.affine_select(out=tri[:], in_=tri[:], pattern=[[1, 128]], compare_op=ALU.is_ge,
                                               fill=0.0, base=0, channel_multiplier=-1), [bf("tri")], [bf("tri")])
        E.op("pool", lambda e: e.memset(sel127[:], 1.0), [], [bf("sel127")])
        E.op("pool", lambda e: e.affine_select(out=sel127[:], in_=sel127[:], pattern=[[0, 128]], compare_op=ALU.is_ge,
                                               fill=0.0, base=-127, channel_multiplier=1), [bf("sel127")], [bf("sel127")])
        E.op("pool", lambda e: e.memset(neg1[:], 0.0), [], [bf("neg1")])
        E.op("pool", lambda e: e.affine_select(out=neg1[:], in_=neg1[:], pattern=[[-1, 128]], compare_op=ALU.is_gt,
                                               fill=NEGBIG, base=0, channel_multiplier=1), [bf("neg1")], [bf("neg1")])
        E.op("pool", lambda e: e.memset(neg2[:], 0.0), [], [bf("neg2")])
        E.op("pool", lambda e: e.affine_select(out=neg2[:], in_=neg2[:], pattern=[[1, 128]], compare_op=ALU.is_ge,
                                               fill=NEGBIG, base=0, channel_multiplier=-1), [bf("neg2")], [bf("neg2")])
        E.op("pool", lambda e: e.memset(upad[:], 0.0), [], [bf(f"upad#{hb}") for hb in range(2 if H >= 8 else 1)])

        dma("sp", vecs[:], vecs_d, [], [bf("vecs")], "c_vecs")
        dma("sp", rowv[:], rowv_d, [], [bf("rowv")], "c_rowv")
        dma("sp", bc8[:], bc8_d, [], [bf("bc8")], "c_bc8")
        dma("sp", masks[:], masks_d, [], [bf("masks")], "c_masks")
        dma("sp", bmask[:], bmask_d, [], [bf("bmask")], "c_bmask")
        dma("sp", cTf[:], cT_d, [], [bf("cTf")], "c_cT")
        dma("sp", wsTf[:], wsT_d.rearrange("g s t -> s g t"), [], [bf("wsTf")], "c_wsT")
        dma("pool", wab[:], win_d[:, c.OFF_AB:c.OFF_AB + 2 * H].rearrange("(kt p) n -> p kt n", p=128), [bf("g_w_in")], [bf("wab")],
            "c_wab")

        act(ealog[:], bc8[:, 0:H], AF.Exp, [bf("bc8")], [bf("ealog")])
        act(cact[:], cTf[:], AF.Silu, [bf("cTf")], [bf("cact")])

        chk(1)
        def mod_evac(ft, p_ap, p_b):
            tt(modT[:, ft:ft + 1], p_ap, vecs[:, c.V_BADA + ft:c.V_BADA + ft + 1], ALU.add, [p_b, bf("vecs")], [bf("modT")])

        linear_fm(w_ada_d, 0, 9 * KT, KT, lambda kt: (cact[:, kt:kt + 1], [bf("cact")]), 1, mod_evac)
        for i, (vn, half) in enumerate(((c.V_N1, 0.5), (c.V_N2, 1.0), (c.V_N3, 0.5))):
            stt(gsc[:, i * KT:(i + 1) * KT], modT[:, (3 * i + 1) * KT:(3 * i + 2) * KT], 1.0, vecs[:, vn:vn + KT],
                ALU.add, ALU.mult, [bf("modT"), bf("vecs")], [bf("gsc")])
            ts(hga[:, i * KT:(i + 1) * KT], modT[:, (3 * i + 2) * KT:(3 * i + 3) * KT], half, None, ALU.mult, None,
               [bf("modT")], [bf("hga")])

        tt(WmT[:], wsTf[:], bc_mid(tri[:], KT), ALU.mult, [bf("wsTf"), bf("tri")], [bf("WmT")])
        WmT_flat = WmT[:].rearrange("p g t -> p (g t)")
        for o0 in range(0, KT * 128, 512):
            n = min(512, KT * 128 - o0)
            p_t, p_b = psum()
            mm(p_t[0:1, 0:n], onesb[:, 0:1], WmT_flat[:, o0:o0 + n], True, True, [bf("onesb"), bf("WmT")], [p_b])
            cp(rsrow[0:1, o0:o0 + n], p_t[0:1, 0:n], [p_b], [bf("rsrow")])
        for g in range(KT):
            p_t, p_b = psum()
            mm(p_t[:, 0:128], rowv[0:1, g * 128:(g + 1) * 128], rsrow[0:1, g * 128:(g + 1) * 128], True, False,
               [bf("rowv"), bf("rsrow")], [p_b])
            mm(p_t[:, 0:128], onesf[0:1, 0:128], rowv[0:1, D + g * 128:D + (g + 1) * 128], False, True,
               [bf("rowv"), bf("onesf")], [p_b])
            cp(BiasG[:, g, :], p_t[:, 0:128], [p_b], [bf("BiasG")])

        chk(2)
        E.barrier()

        def rms_stats(xin, xb, N, ones_t, ones_b, out_rstd=rstd, kts=None):
            kts = range(KT) if kts is None else kts
            p_t, p_b = psum()
            kl = list(kts)
            for i, kt in enumerate(kl):
                s = sq[i % 2]
                act(s[:, 0:N], xin(kt), AF.Square, xb, [bf(f"sq{i % 2}")])
                mm(p_t[:, 0:N], ones_t[:], s[:, 0:N], i == 0, i == len(kl) - 1, [bf(f"sq{i % 2}"), ones_b], [p_b])
            act(rt[:, 0:N], p_t[:, 0:N], AF.Ln, [p_b], [bf("rt")], bias=EPS)
            act(out_rstd[:, 0:N], rt[:, 0:N], AF.Exp, [bf("rt")], [bf("rstd")], scale=-0.5)

        def norm_mod(xcur, xb, N, which, dst, dstb):
            rms_stats(lambda kt: xcur[:, kt, 0:N], xb, N, onesD, bf("onesD"))
            for kt in range(KT):
                tmp = ntmp[kt % 2]
                stt(tmp[:, 0:N], xcur[:, kt, 0:N], gsc[:, which * KT + kt:which * KT + kt + 1], rstd[:, 0:N], ALU.mult,
                    ALU.mult, xb + [bf("gsc"), bf("rstd")], [bf(f"ntmp{kt % 2}")])
                shc = (3 * which) * KT + kt
                act(dst[:, kt, 0:N], tmp[:, 0:N], AF.Identity, [bf(f"ntmp{kt % 2}"), bf("modT")], dstb,
                    bias=modT[:, shc:shc + 1])

        def ffn(xcur, xb, N, which, win_d_, wout_d_):
            for _ in ffn_gen(xcur, xb, N, which, win_d_, wout_d_):
                pass

        def ffn_gen(xcur, xb, N, which, win_d_, wout_d_):
            norm_mod(xcur, xb, N, which, hT, [bf("hT")])
            yield

            for j0 in range(0, FT, 2):
                nj = min(2, FT - j0)

                base = (j0 // 2 % 2) * 2

                def ev_a2(ft, p_ap, p_b, base=base):
                    act(sa[base + ft][:, 0:N], p_ap, AF.Silu, [p_b], [bf(f"sa{base + ft}")])

                def ev_b2(ft, p_ap, p_b, base=base, j0=j0):
                    tt(gT[:, j0 + ft, 0:N], sa[base + ft][:, 0:N], p_ap, ALU.mult, [bf(f"sa{base + ft}"), p_b], [bf("gT")])

                yield from linear_fm_gen(win_d_, j0 * 128, nj, KT, lambda kt: (hT[:, kt, 0:N], [bf("hT")]), N, ev_a2)
                yield from linear_fm_gen(win_d_, DFF + j0 * 128, nj, KT, lambda kt: (hT[:, kt, 0:N], [bf("hT")]), N, ev_b2)

            def ev_out(ft, p_ap, p_b):
                stt(xcur[:, ft, 0:N], p_ap, hga[:, which * KT + ft:which * KT + ft + 1], xcur[:, ft, 0:N], ALU.mult, ALU.add,
                    [p_b, bf("hga")] + xb, xb)

            yield from linear_fm_gen(wout_d_, 0, KT, FT, lambda kt: (gT[:, kt, 0:N], [bf("gT")]), N, ev_out, CW=1)

        _sfb = [bf(f"Sf#{hb}") for hb in range(2 if H >= 8 else 1)]
        _sbb = [bf(f"Sb#{hb}") for hb in range(2 if H >= 8 else 1)]
        E.op("dve", lambda e: e.memset(Sf[:], 0.0), [], _sfb)
        if not FUSED:
            cp(Sf[:, :, 128:256], bc_mid(identf[:], H), [bf("identf")] + _sfb, _sfb)
        cp(Sb[:], Sf[:], _sfb, _sbb, eng="act")

        def qkv_conv(N, xb_h2, is_halo, vm=None, own=True, q_tails_only=False):
            ft_off = 0 if own else KT
            pending = []

            def flush():
                bufsets = [(rt, "rt", rstd, "rstd"), (ntmp[0], "ntmp0", ntmp[1], "ntmp1")]
                banks = []
                for i_, (ft_, h_, isq_, qr_, qrb_) in enumerate(pending):
                    p2, p2b = psum()
                    s_ = sq[i_]
                    act(s_[:, 0:N], qr_[:, 0:N], AF.Square, [qrb_], [bf(f"sq{i_}")])
                    mm(p2[:, 0:N], onesb[:], s_[:, 0:N], True, True, [bf(f"sq{i_}"), bf("onesb")], [p2b])
                    banks.append((p2, p2b))
                for i_ in range(len(pending)):
                    a_, an_, b_, bn_ = bufsets[i_]
                    act(a_[:, 0:N], banks[i_][0][:, 0:N], AF.Ln, [banks[i_][1]], [bf(an_)], bias=EPS)
                for i_ in range(len(pending)):
                    a_, an_, b_, bn_ = bufsets[i_]
                    act(b_[:, 0:N], a_[:, 0:N], AF.Exp, [bf(an_)], [bf(bn_)], scale=-0.5)
                for i_, (ft_, h_, isq_, qr_, qrb_) in enumerate(pending):
                    a_, an_, b_, bn_ = bufsets[i_]
                    dst = qnT if isq_ else knT
                    stt(dst[:, h_, 0:N], qr_[:, 0:N], (128.0 ** -0.5) if isq_ else 1.0, b_[:, 0:N], ALU.mult, ALU.mult,
                        [qrb_, bf(bn_)], [bf("qnT" if isq_ else "knT")])
                pending.clear()

            def ev_pre(ft, p_ap, p_b):
                ft = ft + ft_off
                if is_halo:
                    ts(tails[:, ft, :], p_ap, masks[:, 0:1], None, ALU.mult, None, [p_b, bf("masks")], [bf("tails")])
                    return
                slot = pre[ft % 3]
                slb = bf(f"pre{ft % 3}")
                cp(slot[:, 0:4], tails[:, ft, :], [bf("tails")], [slb], eng="act")
                if vm is not None:
                    ts(slot[:, 4:4 + N], p_ap, vm, None, ALU.mult, None, [p_b, bf("masks")], [slb])
                else:
                    cp(slot[:, 4:4 + N], p_ap, [p_b], [slb], eng=("act" if ft % 2 else "dve"))
                cp(tails[:, ft, :], slot[:, N:N + 4], [slb], [bf("tails")])
                if q_tails_only and ft < KT:
                    return
                ds = dslot[ft % 2]
                dsb = bf(f"dslot{ft % 2}")
                for j in range(4):
                    col = c.V_CONV + j * 3 * KT + ft
                    act(ds[:, j, :], identb[:], AF.Copy, [bf("identb"), bf("vecs")], [dsb], scale=vecs[:, col:col + 1])
                p_t, p_b2 = psum()
                for j in range(4):
                    mm(p_t[:, 0:N], ds[:, j, :], slot[:, 1 + j:1 + j + N], j == 0, j == 3, [dsb, slb], [p_b2])
                if ft < 2 * KT:
                    h = ft % KT
                    isq = ft < KT
                    qr = qraw[ft % 2]
                    qrb = bf(f"qraw{ft % 2}")
                    act(qr[:, 0:N], p_t[:, 0:N], AF.Silu, [p_b2], [qrb])
                    pending.append((ft, h, isq, qr, qrb))
                    if len(pending) == 2:
                        flush()
                else:
                    act(vT[:, ft - 2 * KT, 0:N], p_t[:, 0:N], AF.Silu, [p_b2], [bf("vT")])

            linear_fm(win_d, c.OFF_Q + ft_off * 128, 3 * KT - ft_off, KT, lambda kt: (h2[:, kt, 0:N], xb_h2), N, ev_pre)
            if pending:
                flush()

        HB = 4 if H >= 4 else H
        NHB = H // HB

        FILL = [None]

        def fill(k=1):
            g_ = FILL[0]
            if g_ is None:
                return
            for _ in range(k):
                try:
                    next(g_)
                except StopIteration:
                    FILL[0] = None
                    return

        def ab_proj(c0, ci):
            csl = slice(c0, c0 + 128)
            p_t, p_b = psum()
            for kt in range(KT):
                mm(p_t[:, 0:2 * H], h2[:, kt, csl], wab[:, kt, :], kt == 0, kt == KT - 1, [bf("h2"), bf("wab")], [p_b])
            cp(abT[:, ci, :], p_t[:, 0:2 * H], [p_b], [bf("abT")])

        def small_chain_tb(vm):
            Z = lambda n: smT[n][:]
            zB = lambda n: bf("smT_" + n)
            abv = abT[:]
            a_ap = abv[:, :, 0:H]
            b_ap = abv[:, :, H:2 * H]
            v3 = lambda n: smT[n][:].rearrange("p (c h) -> p c h", c=CPB)
            act(v3("e"), b_ap, AF.Exp, [bf("abT")], [zB("e")], scale=-1.0)
            ts(Z("e"), Z("e"), 1.0, None, ALU.add, None, [zB("e")], [zB("e")])
            recip(Z("beta"), Z("e"), [zB("e")], [zB("beta")])
            if vm is not None:
                ts(Z("beta"), Z("beta"), vm, None, ALU.mult, None, [zB("beta"), bf("masks")], [zB("beta")])
            tt(v3("x"), a_ap, bc_mid(bc8[:, H:2 * H], CPB), ALU.add, [bf("abT"), bf("bc8")], [zB("x")])
            act(Z("ex"), Z("x"), AF.Exp, [zB("x")], [zB("ex")])
            act(Z("sp"), Z("ex"), AF.Ln, [zB("ex")], [zB("sp")], bias=1.0)
            stt(v3("g"), v3("sp"), -1.0, bc_mid(ealog[:], CPB), ALU.mult, ALU.mult, [zB("sp"), bf("ealog")], [zB("g")])
            p_t, p_b = psum()
            mm(p_t[:, 0:CPB * H], tri[:], Z("g"), True, True, [bf("tri"), zB("g")], [p_b])
            cp(Z("CB"), p_t[:, 0:CPB * H], [p_b], [zB("CB")])
            act(Z("eCB"), Z("CB"), AF.Exp, [zB("CB")], [zB("eCB")])
            tt(Z("kbs"), Z("eCB"), Z("beta"), ALU.mult, [zB("eCB"), zB("beta")], [zB("kbs")])
            p_t, p_b = psum()
            mm(p_t[:, 0:CPB * H], sel127[:], Z("CB"), True, True, [bf("sel127"), zB("CB")], [p_b])
            tt(Z("dka"), p_t[:, 0:CPB * H], Z("CB"), ALU.subtract, [p_b, zB("CB")], [zB("dka")])
            act(Z("gle"), p_t[:, 0:CPB * H], AF.Exp, [p_b], [zB("gle")])
            act(Z("dke"), Z("dka"), AF.Exp, [zB("dka")], [zB("dke")])

        def chunk_solve(c0, gci, own=True, vm=None, ci=None):
            csl = slice(c0, c0 + 128)
            assert ci is not None
            sm = {n: smT[n][:, ci * H:(ci + 1) * H] for n in smT}
            S = lambda n: sm[n]
            sB = lambda n: bf("smT_" + n)
            fill()

            HS = [slice(hb * HB, (hb + 1) * HB) for hb in range(NHB)]
            hbf = lambda name, hb: bf(f"{name}#{hb}")
            f2 = lambda t, hb: t[:, HS[hb], :].rearrange("p h f -> p (h f)")
            W_ = HB * 128

            def each(fn):
                for hb in range(NHB):
                    fn(hb)
                fill()

            def s_rb(hb):
                hs = HS[hb]
                tt(dg[:, hs, :], bc_mid(identf[:], HB), bc_last(sm["CB"][:, hs], 128), ALU.mult, [bf("identf"), sB("CB")], [hbf("tw", hb)])
                p_t, p_b = psum()
                mm(p_t[:, 0:W_], onesf[:], f2(dg, hb), True, True, [bf("onesf"), hbf("tw", hb)], [p_b])
                cp(f2(RB, hb), p_t[:, 0:W_], [p_b], [hbf("RB", hb)], eng="act")
            each(s_rb)

            def s_e1(hb):
                hs = HS[hb]
                tt(t1[:, hs, :], bc_mid(neg1[:], HB), RB[:, hs, :], ALU.subtract, [bf("neg1"), hbf("RB", hb)], [hbf("tw", hb)])
                tt(t1[:, hs, :], t1[:, hs, :], bc_last(sm["CB"][:, hs], 128), ALU.add, [hbf("tw", hb), sB("CB")], [hbf("tw", hb)])
                act(E1[:, hs, :], t1[:, hs, :], AF.Exp, [hbf("tw", hb)], [hbf("E1", hb)])
                tt(E1[:, hs, :], E1[:, hs, :], bc_last(sm["beta"][:, hs], 128), ALU.mult, [hbf("E1", hb), sB("beta")], [hbf("E1", hb)])
            each(s_e1)

            def s_gram(hb):
                pG, pGb = psum()
                for hh in range(HB):
                    h = hb * HB + hh
                    mm(pG[:, hh * 128:(hh + 1) * 128], knT[:, h, csl], knT[:, h, csl], True, True, [bf("knT")], [pGb])
                tt(f2(Afull, hb), pG[:, 0:W_], f2(E1, hb), ALU.mult, [pGb, hbf("E1", hb)], [hbf("Afull", hb)])
            each(s_gram)
            if own:
                def s_e2(hb):
                    hs = HS[hb]
                    tt(t2[:, hs, :], RB[:, hs, :], bc_mid(neg2[:], HB), ALU.add, [bf("neg2"), hbf("RB", hb)], [hbf("tw", hb)])
                    tt(t2[:, hs, :], t2[:, hs, :], bc_last(sm["CB"][:, hs], 128), ALU.subtract, [hbf("tw", hb), sB("CB")], [hbf("tw", hb)])
                    act(E2[:, hs, :], t2[:, hs, :], AF.Exp, [hbf("tw", hb)], [hbf("E2", hb)])
                    act(Eg[:, hs, :], RB[:, hs, :], AF.Exp, [hbf("RB", hb)], [hbf("Eg", hb)])
                    pQ, pQb = psum()
                    for hh in range(HB):
                        h = hb * HB + hh
                        mm(pQ[:, hh * 128:(hh + 1) * 128], knT[:, h, csl], qnT[:, h, csl], True, True, [bf("knT"), bf("qnT")], [pQb])
                    tt(f2(qkT, hb), pQ[:, 0:W_], f2(E2, hb), ALU.mult, [pQb, hbf("E2", hb)], [hbf("qkT", hb)])
                    tt(qdT[:, hs, :], qnT[:, hs, csl], Eg[:, hs, :], ALU.mult, [bf("qnT"), hbf("Eg", hb)], [hbf("qdT", hb)])
                each(s_e2)

            mk = lambda i: bc_mid(bmask[:, i, :], HB)

            def transp(src, srcb_fn, hb):
                pT, pTb = psum()
                pTv = pT[:].bitcast(BF16)
                for hh in range(HB):
                    h = hb * HB + hh
                    tr(pTv[:, hh * 128:(hh + 1) * 128], src(h), identb[:], [srcb_fn(hb), bf("identb")], [pTb])
                return pTv, pTb

            def s_at(hb):
                hs = HS[hb]
                pTv, pTb = transp(lambda h: Afull[:, h, :], lambda hb_: hbf("Afull", hb_), hb)
                cp(f2(ATfull, hb), pTv[:, 0:W_], [pTb], [hbf("ATfull", hb)], eng="act")
                tt(Nb[0][:, hs, :], Afull[:, hs, :], mk(0), ALU.mult, [hbf("Afull", hb), bf("bmask")], [hbf("Nb0", hb)])
                tt(NTb[0][:, hs, :], ATfull[:, hs, :], mk(0), ALU.mult, [hbf("ATfull", hb), bf("bmask")], [hbf("NTb0", hb)])
                tt(Ub[0][:, hs, :], bc_mid(identb[:], HB), NTb[0][:, hs, :], ALU.subtract, [hbf("NTb0", hb), bf("identb")], [hbf("Ub0", hb)])
            each(s_at)
            cur = 0
            NLEV = 3
            for k in range(1, NLEV + 1):
                nxt = 1 - cur

                def s_lev(hb, cur=cur, nxt=nxt, k=k):
                    pN, pNb = psum()
                    for hh in range(HB):
                        h = hb * HB + hh
                        mm(pN[:, hh * 128:(hh + 1) * 128], NTb[cur][:, h, :], Nb[cur][:, h, :], True, True,
                           [hbf(f"NTb{cur}", hb), hbf(f"Nb{cur}", hb)], [pNb])
                    cp(f2(Nb[nxt], hb), pN[:, 0:W_], [pNb], [hbf(f"Nb{nxt}", hb)], eng="act")
                    if k < NLEV:
                        pM, pMb = psum()
                        for hh in range(HB):
                            h = hb * HB + hh
                            mm(pM[:, hh * 128:(hh + 1) * 128], Nb[cur][:, h, :], NTb[cur][:, h, :], True, True,
                               [hbf(f"NTb{cur}", hb), hbf(f"Nb{cur}", hb)], [pMb])
                        cp(f2(NTb[nxt], hb), pM[:, 0:W_], [pMb], [hbf(f"NTb{nxt}", hb)])
                    pU, pUb = psum()
                    for hh in range(HB):
                        h = hb * HB + hh
                        mm(pU[:, hh * 128:(hh + 1) * 128], Nb[nxt][:, h, :], Ub[cur][:, h, :], True, True,
                           [hbf(f"Nb{nxt}", hb), hbf(f"Ub{cur}", hb)], [pUb])
                    tt(f2(Ub[nxt], hb), f2(Ub[cur], hb), pU[:, 0:W_], ALU.add, [pUb, hbf(f"Ub{cur}", hb)], [hbf(f"Ub{nxt}", hb)])
                each(s_lev)
                cur = nxt

            def s_td(hb, cur=cur):
                pTv, pTb = transp(lambda h: Ub[cur][:, h, :], lambda hb_: hbf(f"Ub{cur}", hb_), hb)
                cp(f2(Tb[0], hb), pTv[:, 0:W_], [pTb], [hbf("Tb0", hb)], eng="act")
            each(s_td)
            tcur = 0
            for i in range(3):
                nxt = 1 - cur
                tnx = 1 - tcur
                lastm = (i == 2)

                def s_mrg(hb, i=i, cur=cur, nxt=nxt, tcur=tcur, tnx=tnx, lastm=lastm):
                    hs = HS[hb]
                    tt(As1[:, hs, :], Afull[:, hs, :], mk(1 + i), ALU.mult, [hbf("Afull", hb), bf("bmask")], [hbf("As", hb)])
                    pW, pWb = psum()
                    for hh in range(HB):
                        h = hb * HB + hh
                        mm(pW[:, hh * 128:(hh + 1) * 128], As1[:, h, :], Ub[cur][:, h, :], True, True,
                           [hbf("As", hb), hbf(f"Ub{cur}", hb)], [pWb])
                    cp(f2(Wb, hb), pW[:, 0:W_], [pWb], [hbf("Wb", hb)], eng="act")
                    if not lastm:
                        tt(Ms1[:, hs, :], ATfull[:, hs, :], mk(1 + i), ALU.mult, [hbf("ATfull", hb), bf("bmask")], [hbf("Ms", hb)])
                        pV, pVb = psum()
                        for hh in range(HB):
                            h = hb * HB + hh
                            mm(pV[:, hh * 128:(hh + 1) * 128], Ms1[:, h, :], Tb[tcur][:, h, :], True, True,
                               [hbf("Ms", hb), hbf(f"Tb{tcur}", hb)], [pVb])
                        cp(f2(Vb, hb), pV[:, 0:W_], [pVb], [hbf("Vb", hb)])
                    pU, pUb = psum()
                    for hh in range(HB):
                        h = hb * HB + hh
                        mm(pU[:, hh * 128:(hh + 1) * 128], Tb[tcur][:, h, :], Wb[:, h, :], True, True,
                           [hbf(f"Tb{tcur}", hb), hbf("Wb", hb)], [pUb])
                    tt(f2(Ub[nxt], hb), f2(Ub[cur], hb), pU[:, 0:W_], ALU.subtract, [pUb, hbf(f"Ub{cur}", hb)], [hbf(f"Ub{nxt}", hb)])
                    if not lastm:
                        pX, pXb = psum()
                        for hh in range(HB):
                            h = hb * HB + hh
                            mm(pX[:, hh * 128:(hh + 1) * 128], Ub[cur][:, h, :], Vb[:, h, :], True, True,
                               [hbf(f"Ub{cur}", hb), hbf("Vb", hb)], [pXb])
                        tt(f2(Tb[tnx], hb), f2(Tb[tcur], hb), pX[:, 0:W_], ALU.subtract, [pXb, hbf(f"Tb{tcur}", hb)], [hbf(f"Tb{tnx}", hb)])
                each(s_mrg)
                cur = nxt
                tcur = tnx
            U = Ub[cur]
            Ubn = f"Ub{cur}"

            def s_kv(hb):
                hs = HS[hb]
                pTv, pTb = transp(lambda h: knT[:, h, csl], lambda hb_: bf("knT"), hb)
                pT3 = pTv[:, 0:W_].rearrange("p (h f) -> p h f", h=HB)
                tt(kd[:, hs, :], pT3, bc_last(sm["dke"][:, hs], 128), ALU.mult, [pTb, sB("dke")], [hbf("kd", hb)])
                tt(kbg[:, hs, :], pT3, bc_last(sm["kbs"][:, hs], 128), ALU.mult, [pTb, sB("kbs")], [hbf("kbg", hb)])
                pTv, pTb = transp(lambda h: vT[:, h, csl], lambda hb_: bf("vT"), hb)
                pT3 = pTv[:, 0:W_].rearrange("p (h f) -> p h f", h=HB)
                tt(vb[:, hs, :], pT3, bc_last(sm["beta"][:, hs], 128), ALU.mult, [pTb, sB("beta")], [hbf("vb", hb)])
            each(s_kv)

            def s_uw(hb):
                hs = HS[hb]
                pu, pub = psum()
                pw, pwb = psum()
                for hh in range(HB):
                    h = hb * HB + hh
                    mm(pu[:, hh * 128:(hh + 1) * 128], U[:, h, :], vb[:, h, :], True, True, [hbf(Ubn, hb), hbf("vb", hb)], [pub])
                    mm(pw[:, hh * 128:(hh + 1) * 128], kbg[:, h, :], U[:, h, :], True, True, [hbf(Ubn, hb), hbf("kbg", hb)], [pwb])
                cp(upad[:, hs, 0:128], pu[:, 0:W_].rearrange("p (h f) -> p h f", h=HB), [pub], [hbf("upad", hb)], eng="act")
                cp(f2(wT, hb), pw[:, 0:W_], [pwb], [hbf("wT", hb)])
            each(s_uw)

            so = ostage[gci % 2]
            sr = rstage[gci % 2]
            sob, srb = bf(f"ostage{gci % 2}"), bf(f"rstage{gci % 2}")

            def s_vn(hb):
                for h2i in range(hb * HB, (hb + 1) * HB, 2):
                    p1, p1b = psum()
                    for hh in range(2):
                        h = h2i + hh
                        mm(p1[:, hh * SW:(hh + 1) * SW], wT[:, h, :], Sb[:, h, :], True, True, [hbf("wT", hb), hbf("Sb", hb)], [p1b])
                    tt(vnew[:, h2i:h2i + 2, :].rearrange("p h f -> p (h f)"), upad[:, h2i:h2i + 2, :].rearrange("p h f -> p (h f)"),
                       p1[:, 0:2 * SW], ALU.subtract, [p1b, hbf("upad", hb)], [hbf("vnew", hb)])
            each(s_vn)
            if own:
                def s_o(hb):
                    hs = HS[hb]
                    po, pob = psum()
                    for hh in range(HB):
                        h = hb * HB + hh
                        mm(po[:, hh * 128:(hh + 1) * 128], Sb[:, h, 0:128], qdT[:, h, :], True, False, [hbf("Sb", hb), hbf("qdT", hb)], [pob])
                        mm(po[:, hh * 128:(hh + 1) * 128], vnew[:, h, 0:128], qkT[:, h, :], False, True, [hbf("vnew", hb), hbf("qkT", hb)], [pob])
                    if FUSED:
                        cp(oTtb[:, hs, csl], po[:, 0:W_].rearrange("p (h f) -> p h f", h=HB), [pob], [bf("oTtb")], eng="act")
                        return
                    pr, prb = psum()
                    for hh in range(HB):
                        h = hb * HB + hh
                        mm(pr[:, hh * 128:(hh + 1) * 128], Sb[:, h, 128:256], qdT[:, h, :], True, False, [hbf("Sb", hb), hbf("qdT", hb)], [prb])
                        mm(pr[:, hh * 128:(hh + 1) * 128], vnew[:, h, 128:256], qkT[:, h, :], False, True, [hbf("vnew", hb), hbf("qkT", hb)], [prb])
                    cp(f2(so, hb), po[:, 0:W_], [pob], [sob], eng="act")
                    cp(f2(sr, hb), pr[:, 0:W_], [prb], [srb], eng="act")
                each(s_o)

            def s_st(hb):
                hs = HS[hb]
                for h2i in range(hb * HB, (hb + 1) * HB, 2):
                    p3, p3b = psum()
                    for hh in range(2):
                        h = h2i + hh
                        mm(p3[:, hh * SW:(hh + 1) * SW], kd[:, h, :], vnew[:, h, :], True, True, [hbf("kd", hb), hbf("vnew", hb)], [p3b])
                    for hh in range(2):
                        h = h2i + hh
                        stt(Sf[:, h, :], Sf[:, h, :], sm["gle"][:, h:h + 1], p3[:, hh * SW:(hh + 1) * SW], ALU.mult, ALU.add,
                            [p3b, hbf("Sf", hb), sB("gle")], [hbf("Sf", hb)])
                cp(Sb[:, hs, :], Sf[:, hs, :], [hbf("Sf", hb)], [hbf("Sb", hb)], eng="act")
            each(s_st)
            if not FUSED:
                dma("sp", os_d[gci], so[:].rearrange("p h f -> p (h f)"), [sob], [bf(f"os{gci}")], f"ost{gci % 2}")
                dma("sp", rs_d[gci], sr[:].rearrange("p h f -> p (h f)"), [srb], [bf(f"rs{gci}")], f"rst{gci % 2}")

        blocks = [(0, 4, True)] + [(4 + i * TB, TB, False) for i in range(NB)]
        for bi, (col0, N, is_halo) in enumerate(blocks if part in (0, 1) else []):
            xi = bi % 2
            xcur = xt[xi]
            xb = [bf("xt0")]
            dma("sp", xcur[:, :, 0:N], xT_d[:, :, col0:col0 + N], [], xb, f"xld{xi}")
            ffn(xcur, xb, N, 0, f1in_d, f1out_d)
            norm_mod(xcur, xb, N, 1, h2, [bf("h2")])
            if not is_halo:
                t0 = col0 - 4
                dma("sp", x1s_d[:, :, t0:t0 + N], xcur[:, :, 0:N], xb, [bf(f"x1s{bi}")], f"x1st{xi}")
                dma("sp", h2s_d[:, :, t0:t0 + N], h2[:, :, 0:N], [bf("h2")], [bf(f"h2s{bi}")], "h2st")
            E.barrier()
            qkv_conv(N, [bf("h2")], is_halo)
            if is_halo:
                chk(4)
            else:
                chk(5)
            if not is_halo:
                for ci in range(CPB):
                    chunk_solve(ci * 128, (bi - 1) * CPB + ci)
                    chk(6)
            E.barrier()

        chk(7)
        if part in (0, 1):
            dma("sp", st_in_t.ap(), Sf[:].rearrange("p h f -> p (h f)"), [bf("Sf")], [bf("st_in")], "stio")
        if part == 1:
            raise Stop()
        if G > 1 and part == 0:
            groups = [list(range(g0, g0 + G)) for g0 in range(0, c.NCORES, G)]
            E.op("pool", lambda e: e.collective_compute("AllGather", ALU.bypass, replica_groups=groups,
                                                        ins=[st_in_t.ap().opt()], outs=[st_all_t.ap().opt()]),
                 [bf("st_in")], [bf("st_all")], dma="ccsem")
        if not FUSED:
            E.op("dve", lambda e: e.memset(Sst[:], 0.0), [], [bf("Sst")])
        for i in (range(G - 1) if not FUSED else []):
            dma("sp", stg[i][:].rearrange("p h f -> p (h f)"), st_all_t.ap()[i * 128:(i + 1) * 128, :], [bf("st_all")],
                [bf("stg0")], f"stgl{i}")
            for hb in range(NHB):
                pT, pTb = psum()
                for hh in range(HB):
                    h = hb * HB + hh
                    tr(pT[:, hh * 128:(hh + 1) * 128], stg[i][:, h, 128:256], identf[:], [bf("stg0"), bf("identf")], [pTb])
                cp(PiT[:, hb * HB:(hb + 1) * HB, :].rearrange("p h f -> p (h f)"), pT[:, 0:HB * 128], [pTb], [bf("PiT")])
            for hb in range(NHB):
                hs = slice(hb * HB, (hb + 1) * HB)
                pc, pcb = psum()
                for hh in range(HB):
                    h = hb * HB + hh
                    mm(pc[:, hh * 128:(hh + 1) * 128], PiT[:, h, :], Sst[:, h, :], True, True, [bf("PiT"), bf("Sst")], [pcb])
                tt(cand[:, hs, :], pc[:, 0:HB * 128].rearrange("p (h f) -> p h f", h=HB), stg[i][:, hs, 0:128], ALU.add,
                   [pcb, bf("stg0")], [bf("cand")])
            tt(cand[:], cand[:], Sst[:], ALU.subtract, [bf("cand"), bf("Sst")], [bf("cand")])
            stt(Sst[:], cand[:], masks[:, 1 + i:2 + i], Sst[:], ALU.mult, ALU.add, [bf("cand"), bf("masks"), bf("Sst")],
                [bf("Sst")])
        if not FUSED:
            cp(Sstb[:], Sst[:], [bf("Sst")], [bf("Sstb")])
        E.barrier()

        chk(8)
        def phase2_tb(tbi):
            t0 = tbi * TB
            N = TB
            xi = tbi % 2
            xcur = xt[xi]
            xb = [bf("xt0")]
            if not FUSED:
                dma("sp", xcur[:], x1s_d[:, :, t0:t0 + N], [bf(f"x1s{tbi + 1}")], xb, f"xld{xi}")
                dma("sp", h2[:], h2s_d[:, :, t0:t0 + N], [bf(f"h2s{tbi + 1}")], [bf("h2")], "h2ld")
            dma("pool", wv[:], win_d[:, c.OFF_V:c.OFF_V + D].rearrange("(kt p) n -> p kt n", p=128), [bf("g_w_in")], [bf("wv")], "wvld")
            h2r = lambda kt: (h2[:, kt, 0:N], [bf("h2")])

            linear_fm(win_d, c.OFF_Z, KT, KT, h2r, N,
                      lambda ft, p_ap, p_b: act(zs[:, ft, :], p_ap, AF.Silu, [p_b], [bf("zs")]))
            linear_fm(win_d, c.OFF_U, KT, KT, h2r, N,
                      lambda ft, p_ap, p_b: act(ug[:, ft, :], p_ap, AF.Gelu, [p_b], [bf("ug")]))
            def o_part(ci):
                gci = tbi * CPB + ci
                csl = slice(ci * 128, (ci + 1) * 128)
                so, sr = ostage[gci % 2], rstage[gci % 2]
                sob, srb = bf(f"ostage{gci % 2}"), bf(f"rstage{gci % 2}")
                if FUSED:
                    cp(otrue[:], oTtb[:, :, csl], [bf("oTtb")], [bf("otrue")])
                else:
                    dma("sp", so[:].rearrange("p h f -> p (h f)"), os_d[gci], [bf(f"os{gci}")], [sob], f"old{gci % 2}")
                    dma("sp", sr[:].rearrange("p h f -> p (h f)"), rs_d[gci], [bf(f"rs{gci}")], [srb], f"rld{gci % 2}")
                for hb in (range(NHB) if not FUSED else []):
                    hs = slice(hb * HB, (hb + 1) * HB)
                    pc, pcb = psum()
                    for hh in range(HB):
                        h = hb * HB + hh
                        mm(pc[:, hh * 128:(hh + 1) * 128], Sstb[:, h, :], sr[:, h, :], True, True, [bf("Sstb"), srb], [pcb])
                    tt(otrue[:, hs, :], pc[:, 0:HB * 128].rearrange("p (h f) -> p h f", h=HB), so[:, hs, :], ALU.add,
                       [pcb, sob], [bf("otrue")])
                act(osq[:], otrue[:], AF.Square, [bf("otrue")], [bf("osq")])
                yield
                for hb in range(NHB):
                    hs = slice(hb * HB, (hb + 1) * HB)
                    pn, pnb = psum()
                    mm(pn[:, 0:HB * 128], onesV[:], osq[:, hs, :].rearrange("p h f -> p (h f)"), True, True,
                       [bf("onesV"), bf("osq")], [pnb])
                    act(ort[:, hs, :].rearrange("p h f -> p (h f)"), pn[:, 0:HB * 128], AF.Ln, [pnb], [bf("ort")], bias=EPS)
                yield
                act(ors[:], ort[:], AF.Exp, [bf("ort")], [bf("ort")], scale=-0.5)
                yield
                tt(otrue[:], otrue[:], ors[:], ALU.mult, [bf("otrue"), bf("ort")], [bf("otrue")])
                yield
                stt(obT[:, :, csl], otrue[:], vecs[:, c.V_DNG:c.V_DNG + 1], zs[:, :, csl], ALU.mult, ALU.mult,
                    [bf("otrue"), bf("vecs"), bf("zs")], [bf("obT")])
            def g_part(ci):
                csl = slice(ci * 128, (ci + 1) * 128)
                for o0 in range(0, D, 512):
                    n = min(512, D - o0)
                    p_t, p_b = psum()
                    for kt in range(KT):
                        mm(p_t[:, 0:n], h2[:, kt, csl], wv[:, kt, o0:o0 + n], kt == 0, kt == KT - 1, [bf("h2"), bf("wv")], [p_b])
                    act(vg[:, o0:o0 + n], p_t[:, 0:n], AF.Gelu, [p_b], [bf("vg")])
                    yield
                SS = lambda n: st[n][:]
                sBB = lambda n: bf("st_" + n)
                E.op("dve", lambda e: e.tensor_reduce(out=st["s1"][:], in_=vg[:], axis=AX.X, op=ALU.add), [bf("vg")], [sBB("s1")])
                tt(vsq[:], vg[:], vg[:], ALU.mult, [bf("vg")], [bf("mtmp")])
                yield
                E.op("dve", lambda e: e.tensor_reduce(out=st["s2"][:], in_=vsq[:], axis=AX.X, op=ALU.add), [bf("mtmp")], [sBB("s2")])
                ts(SS("mean"), SS("s1"), 1.0 / D, None, ALU.mult, None, [sBB("s1")], [sBB("mean")])
                yield
                tt(SS("msq"), SS("mean"), SS("mean"), ALU.mult, [sBB("mean")], [sBB("msq")])
                stt(SS("var"), SS("s2"), 1.0 / D, SS("msq"), ALU.mult, ALU.subtract, [sBB("s2"), sBB("msq")], [sBB("var")])
                yield
                act(SS("sd"), SS("var"), AF.Sqrt, [sBB("var")], [sBB("sd")], bias=EPS)
                recip(SS("rstd"), SS("sd"), [sBB("sd")], [sBB("rstd")])
                yield
                ts(nrm[:], vg[:], SS("mean"), SS("rstd"), ALU.subtract, ALU.mult, [bf("vg"), sBB("mean"), sBB("rstd")], [bf("nrm")])
                yield
                for gb in range(0, KT, 4):
                    ng = min(4, KT - gb)
                    pM, pMb = psum()
                    for gg in range(ng):
                        g = gb + gg
                        mm(pM[:, gg * 128:(gg + 1) * 128], nrm[:, g * 128:(g + 1) * 128], WmT[:, g, :], True, True,
                           [bf("nrm"), bf("WmT")], [pMb])
                    for gg in range(ng):
                        g = gb + gg
                        stt(mtmp[:, g, :], pM[:, gg * 128:(gg + 1) * 128], vecs[:, c.V_LNG + g:c.V_LNG + g + 1], BiasG[:, g, :],
                            ALU.mult, ALU.add, [pMb, bf("vecs"), bf("BiasG")], [bf("mtmp")])
                    yield
                tt(oaT[:, :, csl], mtmp[:], ug[:, :, csl], ALU.mult, [bf("mtmp"), bf("ug")], [bf("oaT")])
            for ci in range(CPB):
                gens = [o_part(ci), g_part(ci)]
                while gens:
                    for g_ in list(gens):
                        try:
                            next(g_)
                        except StopIteration:
                            gens.remove(g_)
            linear_fm(win_d, c.OFF_GATE, 2 * KT, KT, h2r, N,
                      lambda ft, p_ap, p_b: act(gt[:, ft, :], p_ap, AF.Sigmoid, [p_b], [bf("gt")]))
            linear_fm(wbr_d[0], 0, KT, KT, lambda kt: (oaT[:, kt, :], [bf("oaT")]), N,
                      lambda ft, p_ap, p_b: tt(mA[:, ft, :], p_ap, gt[:, ft, :], ALU.mult, [p_b, bf("gt")], [bf("mA")]))

            def ev_brB(ft, p_ap, p_b):
                tt(mB[:], p_ap, gt[:, KT + ft, :], ALU.mult, [p_b, bf("gt")], [bf("mB")])
                tt(mergedT[:, ft, :], mB[:], mA[:, ft, :], ALU.add, [bf("mB"), bf("mA")], [bf("mergedT")])

            linear_fm(wbr_d[1], 0, KT, KT, lambda kt: (obT[:, kt, :], [bf("obT")]), N, ev_brB)
            linear_fm(wout_d, 0, KT, KT, lambda kt: (mergedT[:, kt, :], [bf("mergedT")]), N,
                      lambda ft, p_ap, p_b: stt(xcur[:, ft, :], p_ap, hga[:, KT + ft:KT + ft + 1], xcur[:, ft, :], ALU.mult,
                                                ALU.add, [p_b, bf("hga")] + xb, xb))
            E.barrier()
            ffn(xcur, xb, N, 2, f3in_d, f3out_d)
            rms_stats(lambda kt: xcur[:, kt, 0:N], xb, N, onesD, bf("onesD"))
            og = ostg[tbi % 2]
            ogb = bf("ostg0")
            for kt in range(KT):
                stt(og[:, kt, :], xcur[:, kt, :], vecs[:, c.V_NF + kt:c.V_NF + kt + 1], rstd[:, 0:N], ALU.mult, ALU.mult,
                    xb + [bf("vecs"), bf("rstd")], [ogb])
            dma("sp", outT_d[:, :, t0:t0 + N], og[:], [ogb], [bf(f"out{tbi}")], f"outst{tbi % 2}")
            E.barrier()

        if not FUSED:
            for tbi in range(NB):
                phase2_tb(tbi)
        else:
            E.op("pool", lambda e: e.memset(tails[:], 0.0), [], [bf("tails")])
            blist = [(wi, tbi) for wi in range(G) for tbi in range(NB)]

            def stageA_gen(wi, tbi):
                col0 = wi * NT + tbi * TB
                xcur = xt[0]
                xb = [bf("xt0")]
                dma("sp", xcur[:, :, 0:TB], xT_d[:, :, col0:col0 + TB], [], xb, "xld0")
                yield from ffn_gen(xcur, xb, TB, 0, f1in_d, f1out_d)
                norm_mod(xcur, xb, TB, 1, h2, [bf("h2")])
                yield

            for _ in stageA_gen(*blist[0]):
                pass
            for bi_, (wi, tbi) in enumerate(blist):
                own = (wi == G - 1)
                vm = None if own else masks[:, wi:wi + 1]
                N = TB
                lastp = (wi == G - 2 and tbi == NB - 1)
                if own:
                    E.barrier()
                qkv_conv(N, [bf("h2")], False, vm=vm, own=(own or lastp), q_tails_only=lastp)
                for ci in range(CPB):
                    ab_proj(ci * 128, ci)
                small_chain_tb(vm)
                nxt_blk = blist[bi_ + 1] if bi_ + 1 < len(blist) else None
                if (not own) and nxt_blk is not None:
                    FILL[0] = stageA_gen(*nxt_blk)
                for ci in range(CPB):
                    chunk_solve(ci * 128, 0, own=own, vm=vm, ci=ci)
                if FILL[0] is not None:
                    for _ in FILL[0]:
                        pass
                    FILL[0] = None
                if own:
                    E.barrier()
                    phase2_tb(tbi)
                    if nxt_blk is not None:
                        for _ in stageA_gen(*nxt_blk):
                            pass
        return B

    Ep = Emit(nc, plan_only=True)
    try:
        program(Ep)
    except Stop:
        pass
    Ee = Emit(nc, plan_only=False, wplan=Ep.wreq)
    try:
        program(Ee)
    except Stop:
        pass

    keys = set()
    for eng, lst in Ee.ops.items():
        for waits, fn, inc in lst:
            keys.add(inc[0])
            for k, v in waits:
                keys.add(k)
    sem = {k: nc.alloc_semaphore(name="s_" + k) for k in sorted(keys)}
    final = dict(Ee.dcnt)

    with nc.Block() as block:
        def replay(e, name, last=False):
            for waits, fn, inc in Ee.ops[name]:
                for k, v in waits:
                    e.wait_ge(sem[k], v)
                ins = fn(e)
                ins.then_inc(sem[inc[0]], inc[1])
            if last:
                for k, v in final.items():
                    e.wait_ge(sem[k], v)
                for k in ("pe", "act", "dve", "pool"):
                    e.wait_ge(sem[k], Ee.cnt[k])

        @block.tensor
        def _(e):
            replay(e, "pe")

        @block.scalar
        def _(e):
            replay(e, "act")

        @block.vector
        def _(e):
            replay(e, "dve")

        @block.gpsimd
        def _(e):
            replay(e, "pool")

        @block.sync
        def _(e):
            replay(e, "sp", last=True)

    nc._n_ops = {k: len(v) for k, v in Ee.ops.items()}
    return nc


def host_inputs(cfg, inp, fused=False):
    c = cfg
    D, KT, H, NT, G = c.D, c.KT, c.H, c.NT, c.G
    f32 = np.float32
    x = np.asarray(inp["x"], f32)
    cc = np.asarray(inp["c"], f32)

    def pk(v):
        return np.asarray(v, f32).reshape(-1, 128).T

    vecs = np.zeros((128, c.NV), f32)
    vecs[:, c.V_N1:c.V_N1 + KT] = pk(inp["norm1_g"][0])
    vecs[:, c.V_N2:c.V_N2 + KT] = pk(inp["norm2_g"][0])
    vecs[:, c.V_N3:c.V_N3 + KT] = pk(inp["norm3_g"][0])
    vecs[:, c.V_NF:c.V_NF + KT] = pk(inp["final_g"])
    vecs[:, c.V_BADA:c.V_BADA + 9 * KT] = pk(inp["b_ada"][0])
    vecs[:, c.V_LNG:c.V_LNG + KT] = pk(inp["gm_ln_g"][0])
    vecs[:, c.V_DNG] = np.asarray(inp["dn_norm_g"][0], f32)
    cw = np.asarray(inp["conv_w"][0], f32)
    for j in range(4):
        vecs[:, c.V_CONV + j * 3 * KT: c.V_CONV + (j + 1) * 3 * KT] = pk(cw[j])
    rowv = np.concatenate([np.asarray(inp["gm_ln_b"][0], f32), np.asarray(inp["gm_b_s"][0], f32).reshape(-1)])[None, :]
    bc8 = np.tile(np.concatenate([np.asarray(inp["a_log"][0], f32), np.asarray(inp["dt_bias"][0], f32)])[None, :], (128, 1))
    w_sT = np.ascontiguousarray(np.transpose(np.asarray(inp["gm_w_s"][0], f32), (0, 2, 1)))
    shared = {
        "vecs": vecs, "rowv": np.ascontiguousarray(rowv), "bc8": np.ascontiguousarray(bc8),
        "w_sT": w_sT,
    }
    wfulls = {"w_ada": inp["w_ada"][0], "ffn1_w_in": inp["ffn1_w_in"][0], "ffn1_w_out": inp["ffn1_w_out"][0],
              "w_in": inp["w_in"][0], "w_branch": np.asarray(inp["w_branch"][0]).reshape(2 * D, D),
              "w_out": inp["w_out"][0], "ffn2_w_in": inp["ffn2_w_in"][0], "ffn2_w_out": inp["ffn2_w_out"][0]}
    wfulls = {k: np.asarray(v, f32) for k, v in wfulls.items()}
    ii = np.arange(128)
    bm = np.zeros((128, 4, 128), f32)
    bm[:, 0, :] = (ii[:, None] // 16 == ii[None, :] // 16)
    for i, sz in enumerate((16, 32, 64)):
        bm[:, 1 + i, :] = (ii[:, None] // (2 * sz) == ii[None, :] // (2 * sz)) & (ii[:, None] // sz != ii[None, :] // sz)
    maps = []
    for r in range(c.NCORES):
        b, j = r // G, r % G
        s0 = j * NT
        m = np.zeros((128, G), f32)
        if fused:
            xs = np.zeros((G * NT, D), f32)
            for sgi in range(G):
                seg = j - (G - 1) + sgi
                if seg >= 0:
                    xs[sgi * NT:(sgi + 1) * NT] = x[b, seg * NT:(seg + 1) * NT]
                    m[:, sgi] = 1.0
            xT = np.ascontiguousarray(xs.T.reshape(KT, 128, G * NT).transpose(1, 0, 2))
        else:
            xs = np.zeros((NT + 4, D), f32)
            xs[4:] = x[b, s0:s0 + NT]
            if j > 0:
                xs[:4] = x[b, s0 - 4:s0]
            xT = np.ascontiguousarray(xs.T.reshape(KT, 128, NT + 4).transpose(1, 0, 2))
            m[:, 0] = 1.0 if j > 0 else 0.0
            for i in range(G - 1):
                m[:, 1 + i] = 1.0 if i < j else 0.0
        d = dict(shared)
        for k, v in wfulls.items():
            rp = v.shape[0] // c.NCORES
            d[k] = np.ascontiguousarray(v[r * rp:(r + 1) * rp]) if c.WG else v
        d["xT"] = xT
        d["cT"] = np.ascontiguousarray(cc[b].reshape(KT, 128).T)
        d["masks"] = m
        d["bmask"] = bm
        maps.append(d)
    return maps


def host_output(cfg, results, B):
    c = cfg
    out = np.zeros((B, c.G * c.NT, c.D), np.float32)
    for r in range(c.NCORES):
        b, j = r // c.G, r % c.G
        oT = np.asarray(results[r]["outT"], np.float32).reshape(128, c.KT, c.NT)
        out[b, j * c.NT:(j + 1) * c.NT] = oT.transpose(2, 1, 0).reshape(c.NT, c.D)
    return out


_NC_CACHE = {}


def run_two_launch(cfg, inputs, nbatch):
    key = ("two", cfg.D, cfg.NT, cfg.NCORES)
    if key not in _NC_CACHE:
        _NC_CACHE[key] = (build(cfg, part=1), build(cfg, part=2))
    nc1, nc2 = _NC_CACHE[key]
    maps = host_inputs(cfg, inputs)
    res1 = run_bass_kernel_spmd(nc1, maps, core_ids=list(range(cfg.NCORES))).results
    G = cfg.G
    maps2 = []
    for r in range(cfg.NCORES):
        g0 = (r // G) * G
        d = dict(maps[r])
        for k in ("x1s", "h2s", "o_s", "r_s"):
            d[k] = np.ascontiguousarray(res1[r][k])
        d["st_all"] = np.ascontiguousarray(np.concatenate(
            [np.asarray(res1[g0 + i]["st_in"]).reshape(128, cfg.H * 256) for i in range(G)], axis=0))
        maps2.append(d)
    res2 = run_bass_kernel_spmd(nc2, maps2, core_ids=list(range(cfg.NCORES))).results
    return host_output(cfg, res2, nbatch)


def run_fused(cfg, inputs, nbatch):
    key = ("fused", cfg.D, cfg.NT, cfg.NCORES)
    if key not in _NC_CACHE:
        _NC_CACHE[key] = build(cfg, part=3)
    nc = _NC_CACHE[key]
    maps = host_inputs(cfg, inputs, fused=True)
    res = run_bass_kernel_spmd(nc, maps, core_ids=list(range(cfg.NCORES))).results
    return host_output(cfg, res, nbatch)


def kernel(**inputs):
    cfg = Cfg(WG=False)
    return run_fused(cfg, inputs, 2)
```

```python
import numpy as np
import ml_dtypes
from contextlib import ExitStack
import concourse.bass as bass
import concourse.mybir as mybir
from concourse.bass_utils import run_bass_kernel_spmd

F32 = mybir.dt.float32
BF16 = mybir.dt.bfloat16
AF = mybir.ActivationFunctionType
ALU = mybir.AluOpType
AX = mybir.AxisListType
EPS = 1e-6
NEGBIG = -1.0e30


class Cfg:
    def __init__(self, D=1024, DFF=2816, H=8, NT=2048, TB=512, G=4, NCORES=8, WG=True):
        self.WG = WG
        self.STOP = 0
        self.D, self.DFF, self.H, self.NT, self.TB, self.G, self.NCORES = D, DFF, H, NT, TB, G, NCORES
        self.KT = D // 128
        self.FT = DFF // 128
        self.NB = NT // TB
        self.CPB = TB // 128
        self.NCH = NT // 128
        assert H * 128 == D
        self.INW = 6 * D + 2 * H + 2 * D
        self.OFF_U, self.OFF_V, self.OFF_Q, self.OFF_Z = 0, D, 2 * D, 5 * D
        self.OFF_AB = 6 * D
        self.OFF_GATE = 6 * D + 2 * H
        KT = self.KT
        self.V_N1, self.V_N2, self.V_N3, self.V_NF = 0, KT, 2 * KT, 3 * KT
        self.V_BADA = 4 * KT
        self.V_LNG = 13 * KT
        self.V_DNG = 14 * KT
        self.V_CONV = 14 * KT + 1
        self.NV = self.V_CONV + 4 * 3 * KT
        self.WSLOT = max(self.FT * 128, KT * 256)


class Stop(Exception):
    pass


class Buf:
    __slots__ = ("w", "r", "name")

    def __init__(self, name=""):
        self.w = None
        self.r = {}
        self.name = name


class Emit:
    def __init__(self, nc, plan_only, wplan=None):
        self.nc = nc
        self.plan_only = plan_only
        self.ops = {e: [] for e in ("pe", "act", "dve", "pool", "sp")}
        self.cnt = {e: 0 for e in ("pe", "act", "dve", "pool")}
        self.waited = {e: {} for e in self.ops}
        self.dcnt = {}
        self.wreq = []
        self.wplan = wplan
        self.wissued = 0
        self.wconsumed = 0
        self.pend = {}

    def barrier(self):
        snap = dict(self.cnt)
        snap.update(self.dcnt)
        for e in self.ops:
            p = self.pend.get(e) or {}
            for k, v in snap.items():
                if v > p.get(k, 0):
                    p[k] = v
            self.pend[e] = p

    def op(self, eng, fn, reads=(), writes=(), dma=None):
        waits = {}

        def need(tok):
            if tok is None:
                return
            key, val = tok
            if key == eng and eng == "pe":
                return
            if self.waited[eng].get(key, 0) >= val:
                return
            if waits.get(key, 0) < val:
                waits[key] = val

        p = self.pend.get(eng)
        if p:
            for k, v in p.items():
                if v > 0:
                    need((k, v))
            self.pend[eng] = None
        for b in reads:
            need(b.w)
        for b in writes:
            need(b.w)
            for k, v in b.r.items():
                need((k, v))
        for k, v in waits.items():
            self.waited[eng][k] = v
        if dma is None:
            self.cnt[eng] += 1
            tok = (eng, self.cnt[eng])
            inc = (eng, 1)
        else:
            self.dcnt[dma] = self.dcnt.get(dma, 0) + 16
            tok = (dma, self.dcnt[dma])
            inc = (dma, 16)
        for b in reads:
            if b.r.get(tok[0], 0) < tok[1]:
                b.r[tok[0]] = tok[1]
        for b in writes:
            b.w = tok
            b.r = {}
        if not self.plan_only:
            self.ops[eng].append((list(waits.items()), fn, inc))
        return tok


def build(cfg, part=0):
    c = cfg
    D, KT, DFF, FT, H, NT, TB, NB, CPB, NCH, G = c.D, c.KT, c.DFF, c.FT, c.H, c.NT, c.TB, c.NB, c.CPB, c.NCH, c.G
    nc = bass.Bass("TRN2", target_bir_lowering=False)
    FUSED = (part == 3)

    def din(name, shape, dt=F32):
        return nc.dram_tensor(name, list(shape), dt, kind="ExternalInput").ap()

    xT_d = (din("xT", [128, KT, G * NT]) if FUSED else din("xT", [128, KT, NT + 4])) if part != 2 else None
    cT_d = din("cT", [128, KT])
    vecs_d = din("vecs", [128, c.NV])
    rowv_d = din("rowv", [1, 2 * D])
    bc8_d = din("bc8", [128, 2 * H])
    masks_d = din("masks", [128, G])
    bmask_d = din("bmask", [128, 4, 128])
    NCs = c.NCORES
    WSPEC = [("w_ada", D, 9 * D), ("ffn1_w_in", D, 2 * DFF), ("ffn1_w_out", DFF, D), ("w_in", D, c.INW),
             ("w_branch", 2 * D, D), ("w_out", D, D), ("ffn2_w_in", D, 2 * DFF), ("ffn2_w_out", DFF, D)]
    wext, wbnc, wfull = {}, {}, {}
    wap = {}
    if part == 1:
        WSPEC = [w for w in WSPEC if w[0] in ("w_ada", "ffn1_w_in", "ffn1_w_out", "w_in")]
    if part == 2:
        WSPEC = [w for w in WSPEC if w[0] not in ("ffn1_w_in", "ffn1_w_out")]
    for (wn, wr, wc) in WSPEC:
        if c.WG:
            wext[wn] = din(wn, [wr // NCs, wc])
            wbnc[wn] = nc.dram_tensor(wn + "_bnc", [wr // NCs, wc], F32)
            wfull[wn] = nc.dram_tensor(wn + "_full", [wr, wc], F32)
            wap[wn] = wfull[wn].ap()
        else:
            wap[wn] = din(wn, [wr, wc])
    w_ada_d = wap["w_ada"]
    f1in_d = wap.get("ffn1_w_in")
    f1out_d = wap.get("ffn1_w_out")
    win_d = wap["w_in"]
    wsT_d = din("w_sT", [KT, 128, 128])
    wbr_full = wap.get("w_branch")
    wbr_d = [wbr_full[0:D, :], wbr_full[D:2 * D, :]] if wbr_full is not None else None
    wout_d = wap.get("w_out")
    f3in_d = wap.get("ffn2_w_in")
    f3out_d = wap.get("ffn2_w_out")
    outT_d = nc.dram_tensor("outT", [128, KT, NT], F32, kind="ExternalOutput").ap() if part != 1 else None

    skw = {} if part in (0, 3) else {"kind": ("ExternalOutput" if part == 1 else "ExternalInput")}
    x1s_d = nc.dram_tensor("x1s", [128, KT, NT], F32, **skw).ap()
    h2s_d = nc.dram_tensor("h2s", [128, KT, NT], BF16, **skw).ap()
    os_d = nc.dram_tensor("o_s", [NCH, 128, H * 128], BF16, **skw).ap()
    rs_d = nc.dram_tensor("r_s", [NCH, 128, H * 128], BF16, **skw).ap()
    if part != 2:
        st_in_t = nc.dram_tensor("st_in", [128, H * 256], F32, **skw)
    akw = {} if part in (0, 3) else {"kind": "ExternalInput"}
    if part != 1:
        st_all_t = nc.dram_tensor("st_all", [G * 128, H * 256], F32, **akw)

    es = ExitStack()
    T = {}

    def sb(name, shape, dt):
        T[name] = nc.alloc_sbuf_tensor(name, list(shape), dt)
        return T[name]

    ARENA_E = 57 * 1024 - 512
    arena = nc.alloc_sbuf_tensor("arena", [128, ARENA_E], BF16)
    vptr = {}

    def cv(view, name, shape, dt):
        n = 1
        for d_ in shape[1:]:
            n *= d_
        ne = n * (2 if dt == F32 else 1)
        ne = (ne + 15) // 16 * 16
        off = vptr.get(view, 0)
        vptr[view] = off + ne
        assert off + ne <= ARENA_E, (view, name, off + ne)
        ap = arena[:, off:off + ne]
        if dt == F32:
            ap = ap.bitcast(F32)
        ap = ap[:, 0:n]
        if len(shape) == 3:
            ap = ap.rearrange("p (a b) -> p a b", a=shape[1])
        T[name] = ap
        return ap

    identb = sb("identb", [128, 128], BF16)
    identf = sb("identf", [128, 128], F32)
    onesb = sb("onesb", [128, 128], BF16)
    onesD = sb("onesD", [128, 128], BF16)
    onesV = sb("onesV", [128, 128], BF16)
    onesf = sb("onesf", [128, 128], F32)
    tri = sb("tri", [128, 128], F32)
    sel127 = sb("sel127", [128, 128], F32)
    neg1 = sb("neg1", [128, 128], F32)
    neg2 = sb("neg2", [128, 128], F32)
    vecs = sb("vecs_sb", [128, c.NV], F32)
    bc8 = sb("bc8_sb", [128, 2 * H], F32)
    masks = sb("masks_sb", [128, G], F32)
    bmask = sb("bmask_sb", [128, 4, 128], F32)
    ealog = sb("ealog", [128, H], F32)
    cTf = sb("cTf", [128, KT], F32)
    cact = sb("cact", [128, KT], BF16)
    modT = sb("modT", [128, 9 * KT], F32)
    gsc = sb("gsc", [128, 3 * KT], F32)
    hga = sb("hga", [128, 3 * KT], F32)
    wab = sb("wab", [128, KT, 2 * H], BF16)
    WmT = sb("WmT", [128, KT, 128], BF16)
    BiasG = sb("BiasG", [128, KT, 128], F32)
    NWS = 5
    wslot = [sb(f"wslot{i}", [128, c.WSLOT], BF16) for i in range(NWS)]
    xt = [sb("xt0", [128, KT, TB], F32)]
    xt.append(xt[0])
    h2 = sb("h2", [128, KT, TB], BF16)
    sq = [sb(f"sq{i}", [128, TB], BF16) for i in range(2)]
    rt = sb("rt", [128, TB], F32)
    rstd = sb("rstd", [128, TB], F32)
    ntmp = [sb(f"ntmp{i}", [128, TB], F32) for i in range(2)]
    tails = sb("tails", [128, 3 * KT, 4], BF16)
    SW = 128 if FUSED else 256
    upad = sb("upad", [128, H, SW], BF16)
    Sf = sb("Sf", [128, H, SW], F32)
    Sb = sb("Sb", [128, H, SW], BF16)
    ostage = [sb(f"ostage{i}", [128, H, 128], BF16) if not FUSED else None for i in range(2)]
    rstage = [sb(f"rstage{i}", [128, H, 128], BF16) if not FUSED else None for i in range(2)]
    Sstb = sb("Sstb", [128, H, 128], BF16) if not FUSED else None
    oTtb = sb("oTtb", [128, H, TB], BF16) if FUSED else None
    rowv = cv("S", "rowv_sb", [1, 2 * D], F32)[0:1, :]
    wsTf = cv("S", "wsTf", [128, KT, 128], F32)
    rsrow = cv("S", "rsrow", [1, KT * 128], F32)[0:1, :]
    pre = [cv("P1", f"pre{i}", [128, 4 + TB], BF16) for i in range(3)]
    dslot = [cv("P1", f"dslot{i}", [128, 4, 128], BF16) for i in range(2)]
    qraw = [cv("P1", f"qraw{i}", [128, TB], F32) for i in range(2)]
    knT = cv("P1", "knT", [128, H, TB], BF16)
    vT = cv("P1", "vT", [128, H, TB], BF16)
    ab = cv("P1", "ab", [128, 2 * H], F32)
    abT = cv("P1", "abT", [128, CPB, 2 * H], F32)
    sm = {n: cv("P1", "sm_" + n, [128, H], F32) for n in
          ("e", "beta", "x", "ex", "sp", "g", "CB", "eCB", "kbs", "dka", "dke", "gle")}
    smT = {n: cv("P1", "smT_" + n, [128, CPB * H], F32) for n in
           ("e", "beta", "x", "ex", "sp", "g", "CB", "eCB", "kbs", "dka", "dke", "gle")}
    dg = cv("P1", "tw", [128, H, 128], F32)
    t1 = dg
    t2 = dg
    RB = cv("P1", "RB", [128, H, 128], F32)
    E1 = cv("P1", "E1", [128, H, 128], BF16)
    E1b = E1
    Nb = [cv("P1", f"Nb{i}", [128, H, 128], BF16) for i in range(2)]
    NTb = [cv("P1", f"NTb{i}", [128, H, 128], BF16) for i in range(2)]
    Ub = [cv("P1", f"Ub{i}", [128, H, 128], BF16) for i in range(2)]
    Afull = cv("P1", "Afull", [128, H, 128], BF16)
    ATfull = cv("P1", "ATfull", [128, H, 128], BF16)
    As1 = cv("P1", "As", [128, H, 128], BF16)
    Ms1 = cv("P1", "Ms", [128, H, 128], BF16)
    Tb = [cv("P1", f"Tb{i}", [128, H, 128], BF16) for i in range(2)]
    Wb = cv("P1", "Wb", [128, H, 128], BF16)
    Vb = cv("P1", "Vb", [128, H, 128], BF16)
    kd = cv("P1", "kd", [128, H, 128], BF16)
    kbg = cv("P1", "kbg", [128, H, 128], BF16)
    vb = cv("P1", "vb", [128, H, 128], BF16)
    wT = cv("P1", "wT", [128, H, 128], BF16)
    vnew = cv("P1", "vnew", [128, H, SW], BF16)
    vptr["F"] = vptr["P1"]
    hT = cv("F", "hT", [128, KT, TB], BF16)
    gT = cv("F", "gT", [128, FT, TB], BF16)
    sa = [cv("F", f"sa{i}", [128, TB], BF16) for i in range(4)]
    ostg = [cv("FO", "ostg0", [128, KT, TB], F32)]
    ostg.append(ostg[0])
    qnT = cv("P1", "qnT", [128, H, TB], BF16)
    E2 = cv("P1", "E2", [128, H, 128], BF16)
    Eg = cv("P1", "Eg", [128, H, 128], BF16)
    qkT = cv("P1", "qkT", [128, H, 128], BF16)
    qdT = cv("P1", "qdT", [128, H, 128], BF16)
    stg = [cv("X", "stg0", [128, H, 256], F32)] * max(G - 1, 1)
    PiT = cv("X", "PiT", [128, H, 128], F32)
    Sst = cv("X", "Sst", [128, H, 128], F32)
    cand = cv("X", "cand", [128, H, 128], F32)
    otrue = cv("P2", "otrue", [128, H, 128], F32)
    osq = cv("P2", "osq", [128, H, 128], BF16)
    ort = cv("P2", "ort", [128, H, 128], F32)
    ors = ort
    obT = cv("P2", "obT", [128, KT, TB], BF16)
    zs = cv("P2", "zs", [128, KT, TB], BF16)
    ug = cv("P2", "ug", [128, KT, TB], BF16)
    vg = cv("P2", "vg", [128, D], F32)
    nrm = cv("P2", "nrm", [128, D], BF16)
    st = {n: cv("P2", "st_" + n, [128, 1], F32) for n in ("s1", "s2", "mean", "msq", "var", "sd", "rstd")}
    mtmp = cv("P2", "mtmp", [128, KT, 128], F32)
    vsq = mtmp.rearrange("p a b -> p (a b)")
    oaT = cv("P2", "oaT", [128, KT, TB], BF16)
    gt = cv("P2", "gt", [128, 2 * KT, TB], BF16)
    mA = cv("P2", "mA", [128, KT, TB], BF16)
    mB = cv("P2", "mB", [128, TB], F32)
    mergedT = cv("P2", "mergedT", [128, KT, TB], BF16)
    wv = cv("P2", "wv", [128, KT, D], BF16)

    ps = [nc.alloc_psum_tensor(f"ps{i}", [128, 512], F32) for i in range(8)]

    def program(E):
        B = {}

        def chk(k):
            if c.STOP == k:
                raise Stop()

        def bf(name):
            if name not in B:
                B[name] = Buf(name)
            return B[name]

        psb = [Buf(f"ps{i}") for i in range(8)]
        pstate = [0]

        def psum():
            i = pstate[0] % 8
            pstate[0] += 1
            return ps[i], psb[i]

        def dma(q, out, in_, reads, writes, sem):
            E.op(q, lambda e, o=out, i=in_: e.dma_start(out=o, in_=i), reads, writes, dma=sem)

        def act(out, in_, func, reads, writes, bias=None, scale=None):
            kw = {}
            if bias is not None:
                kw["bias"] = bias
            if scale is not None:
                kw["scale"] = scale
            E.op("act", lambda e, o=out, i=in_, f=func, k=kw: e.activation(out=o, in_=i, func=f, **k), reads, writes)

        def tt(out, in0, in1, op, reads, writes, eng="dve"):
            E.op(eng, lambda e, o=out, a=in0, b=in1, p=op: e.tensor_tensor(out=o, in0=a, in1=b, op=p), reads, writes)

        def ts(out, in0, s1, s2, op0, op1, reads, writes, eng="dve"):
            if s2 is None:
                E.op(eng, lambda e, o=out, a=in0, x=s1, p=op0: e.tensor_scalar(out=o, in0=a, scalar1=x, scalar2=None, op0=p),
                     reads, writes)
            else:
                E.op(eng, lambda e, o=out, a=in0, x=s1, y=s2, p=op0, q=op1: e.tensor_scalar(
                    out=o, in0=a, scalar1=x, scalar2=y, op0=p, op1=q), reads, writes)

        def stt(out, in0, scalar, in1, op0, op1, reads, writes, eng="dve"):
            E.op(eng, lambda e, o=out, a=in0, s=scalar, b=in1, p=op0, q=op1: e.scalar_tensor_tensor(
                out=o, in0=a, scalar=s, in1=b, op0=p, op1=q), reads, writes)

        def cp(out, in_, reads, writes, eng="dve"):
            if eng == "act":
                E.op("act", lambda e, o=out, i=in_: e.activation(out=o, in_=i, func=AF.Copy), reads, writes)
            else:
                E.op(eng, lambda e, o=out, i=in_: e.tensor_copy(out=o, in_=i), reads, writes)

        def recip(out, in_, reads, writes):
            E.op("dve", lambda e, o=out, i=in_: e.reciprocal(out=o, in_=i), reads, writes)

        def mm(out, lhsT, rhs, start, stop, reads, writes):
            E.op("pe", lambda e, o=out, l=lhsT, r=rhs, s=start, p=stop: e.matmul(o, lhsT=l, rhs=r, start=s, stop=p),
                 reads, writes)

        def tr(out, in_, ident, reads, writes):
            E.op("pe", lambda e, o=out, i=in_, d=ident: e.transpose(o, i, d), reads, writes)

        def bc_mid(ap2d, n):
            return ap2d.unsqueeze(1).broadcast_to([128, n, ap2d.shape[1]])

        def bc_last(ap2d, n):
            return ap2d.unsqueeze(2).broadcast_to([128, ap2d.shape[1], n])

        wslot_b = [Buf(f"wslot{i}") for i in range(NWS)]

        def w_issue(upto):
            plan = E.wplan
            while E.wissued < min(upto, len(plan)):
                j = E.wissued
                s = j % NWS
                for (o_fn, src, dep) in plan[j]:
                    dma("pool", o_fn(wslot[s]), src, [bf(dep)] if dep else [], [wslot_b[s]], f"w{s}")
                E.wissued += 1

        def w_get(loads):
            j = E.wconsumed
            E.wconsumed += 1
            if E.plan_only:
                E.wreq.append(loads)
                return wslot[j % NWS], wslot_b[j % NWS]
            w_issue(j + NWS - 1)
            return wslot[j % NWS], wslot_b[j % NWS]

        def linear_fm(Wd, col0, nft, KTin, rhs, N, evac, CW=2):
            for _ in linear_fm_gen(Wd, col0, nft, KTin, rhs, N, evac, CW):
                pass

        def linear_fm_gen(Wd, col0, nft, KTin, rhs, N, evac, CW=2):
            nchunk = (nft + CW - 1) // CW
            for ci in range(nchunk):
                f0 = ci * CW
                nf = min(CW, nft - f0)
                ncols = nf * 128
                src = Wd[:, col0 + f0 * 128: col0 + f0 * 128 + ncols].rearrange("(kt p) n -> p kt n", p=128)
                nm = Wd.name
                dep = ("g_" + nm[:-5]) if nm.endswith("_full") else None
                slot, sbuf_ = w_get([(lambda s, k=KTin, n=ncols: s[:, 0:k * n].rearrange("p (k n) -> p k n", k=k), src, dep)])
                wv_ = slot[:, 0:KTin * ncols].rearrange("p (k n) -> p k n", k=KTin)
                for fi in range(nf):
                    p_t, p_b = psum()
                    for kt in range(KTin):
                        r_ap, r_bufs = rhs(kt)
                        mm(p_t[:, 0:N], wv_[:, kt, fi * 128:(fi + 1) * 128], r_ap, kt == 0, kt == KTin - 1,
                           [sbuf_] + r_bufs, [p_b])
                    evac(f0 + fi, p_t[:, 0:N], p_b)
                    yield

        allc = [list(range(NCs))]
        for (wn, wr, wc) in (WSPEC if c.WG else []):
            dma("sp", wbnc[wn].ap(), wext[wn], [], [bf("b_" + wn)], "wb_" + wn)
            E.op("pool", lambda e, a=wbnc[wn], o=wfull[wn]: e.collective_compute(
                "AllGather", ALU.bypass, replica_groups=allc, ins=[a.ap().opt()], outs=[o.ap().opt()]),
                [bf("b_" + wn)], [bf("g_" + wn)], dma="cc_" + wn)
        E.op("pool", lambda e: e.memset(identf[:], 0.0), [], [bf("identf")])
        E.op("pool", lambda e: e.affine_select(out=identf[:], in_=identf[:], pattern=[[-1, 128]], compare_op=ALU.not_equal,
                                               fill=1.0, base=0, channel_multiplier=1), [bf("identf")], [bf("identf")])
        E.op("pool", lambda e: e.tensor_copy(out=identb[:], in_=identf[:]), [bf("identf")], [bf("identb")])
        E.op("pool", lambda e: e.memset(onesb[:], 1.0), [], [bf("onesb")])
        E.op("pool", lambda e: e.memset(onesD[:], 1.0 / D), [], [bf("onesD")])
        E.op("pool", lambda e: e.memset(onesV[:], 1.0 / 128), [], [bf("onesV")])
        E.op("pool", lambda e: e.memset(onesf[:], 1.0), [], [bf("onesf")])
        E.op("pool", lambda e: e.memset(tri[:], 1.0), [], [bf("tri")])
        E.op("pool", lambda e: e.affine_select(out=tri[:], in_=tri[:], pattern=[[1, 128]], compare_op=ALU.is_ge,
                                               fill=0.0, base=0, channel_multiplier=-1), [bf("tri")], [bf("tri")])
        E.op("pool", lambda e: e.memset(sel127[:], 1.0), [], [bf("sel127")])
        E.op("pool", lambda e: e.affine_select(out=sel127[:], in_=sel127[:], pattern=[[0, 128]], compare_op=ALU.is_ge,
                                               fill=0.0, base=-127, channel_multiplier=1), [bf("sel127")], [bf("sel127")])
        E.op("pool", lambda e: e.memset(neg1[:], 0.0), [], [bf("neg1")])
        E.op("pool", lambda e: e.affine_select(out=neg1[:], in_=neg1[:], pattern=[[-1, 128]], compare_op=ALU.is_gt,
                                               fill=NEGBIG, base=0, channel_multiplier=1), [bf("neg1")], [bf("neg1")])
        E.op("pool", lambda e: e.memset(neg2[:], 0.0), [], [bf("neg2")])
        E.op("pool", lambda e: e.affine_select(out=neg2[:], in_=neg2[:], pattern=[[1, 128]], compare_op=ALU.is_ge,
                                               fill=NEGBIG, base=0, channel_multiplier=-1), [bf("neg2")], [bf("neg2")])
        E.op("pool", lambda e: e.memset(upad[:], 0.0), [], [bf(f"upad#{hb}") for hb in range(2 if H >= 8 else 1)])

        dma("sp", vecs[:], vecs_d, [], [bf("vecs")], "c_vecs")
        dma("sp", rowv[:], rowv_d, [], [bf("rowv")], "c_rowv")
        dma("sp", bc8[:], bc8_d, [], [bf("bc8")], "c_bc8")
        dma("sp", masks[:], masks_d, [], [bf("masks")], "c_masks")
        dma("sp", bmask[:], bmask_d, [], [bf("bmask")], "c_bmask")
        dma("sp", cTf[:], cT_d, [], [bf("cTf")], "c_cT")
        dma("sp", wsTf[:], wsT_d.rearrange("g s t -> s g t"), [], [bf("wsTf")], "c_wsT")
        dma("pool", wab[:], win_d[:, c.OFF_AB:c.OFF_AB + 2 * H].rearrange("(kt p) n -> p kt n", p=128), [bf("g_w_in")], [bf("wab")],
            "c_wab")

        act(ealog[:], bc8[:, 0:H], AF.Exp, [bf("bc8")], [bf("ealog")])
        act(cact[:], cTf[:], AF.Silu, [bf("cTf")], [bf("cact")])

        chk(1)
        def mod_evac(ft, p_ap, p_b):
            tt(modT[:, ft:ft + 1], p_ap, vecs[:, c.V_BADA + ft:c.V_BADA + ft + 1], ALU.add, [p_b, bf("vecs")], [bf("modT")])

        linear_fm(w_ada_d, 0, 9 * KT, KT, lambda kt: (cact[:, kt:kt + 1], [bf("cact")]), 1, mod_evac)
        for i, (vn, half) in enumerate(((c.V_N1, 0.5), (c.V_N2, 1.0), (c.V_N3, 0.5))):
            stt(gsc[:, i * KT:(i + 1) * KT], modT[:, (3 * i + 1) * KT:(3 * i + 2) * KT], 1.0, vecs[:, vn:vn + KT],
                ALU.add, ALU.mult, [bf("modT"), bf("vecs")], [bf("gsc")])
            ts(hga[:, i * KT:(i + 1) * KT], modT[:, (3 * i + 2) * KT:(3 * i + 3) * KT], half, None, ALU.mult, None,
               [bf("modT")], [bf("hga")])

        tt(WmT[:], wsTf[:], bc_mid(tri[:], KT), ALU.mult, [bf("wsTf"), bf("tri")], [bf("WmT")])
        WmT_flat = WmT[:].rearrange("p g t -> p (g t)")
        for o0 in range(0, KT * 128, 512):
            n = min(512, KT * 128 - o0)
            p_t, p_b = psum()
            mm(p_t[0:1, 0:n], onesb[:, 0:1], WmT_flat[:, o0:o0 + n], True, True, [bf("onesb"), bf("WmT")], [p_b])
            cp(rsrow[0:1, o0:o0 + n], p_t[0:1, 0:n], [p_b], [bf("rsrow")])
        for g in range(KT):
            p_t, p_b = psum()
            mm(p_t[:, 0:128], rowv[0:1, g * 128:(g + 1) * 128], rsrow[0:1, g * 128:(g + 1) * 128], True, False,
               [bf("rowv"), bf("rsrow")], [p_b])
            mm(p_t[:, 0:128], onesf[0:1, 0:128], rowv[0:1, D + g * 128:D + (g + 1) * 128], False, True,
               [bf("rowv"), bf("onesf")], [p_b])
            cp(BiasG[:, g, :], p_t[:, 0:128], [p_b], [bf("BiasG")])

        chk(2)
        E.barrier()

        def rms_stats(xin, xb, N, ones_t, ones_b, out_rstd=rstd, kts=None):
            kts = range(KT) if kts is None else kts
            p_t, p_b = psum()
            kl = list(kts)
            for i, kt in enumerate(kl):
                s = sq[i % 2]
                act(s[:, 0:N], xin(kt), AF.Square, xb, [bf(f"sq{i % 2}")])
                mm(p_t[:, 0:N], ones_t[:], s[:, 0:N], i == 0, i == len(kl) - 1, [bf(f"sq{i % 2}"), ones_b], [p_b])
            act(rt[:, 0:N], p_t[:, 0:N], AF.Ln, [p_b], [bf("rt")], bias=EPS)
            act(out_rstd[:, 0:N], rt[:, 0:N], AF.Exp, [bf("rt")], [bf("rstd")], scale=-0.5)

        def norm_mod(xcur, xb, N, which, dst, dstb):
            rms_stats(lambda kt: xcur[:, kt, 0:N], xb, N, onesD, bf("onesD"))
            for kt in range(KT):
                tmp = ntmp[kt % 2]
                stt(tmp[:, 0:N], xcur[:, kt, 0:N], gsc[:, which * KT + kt:which * KT + kt + 1], rstd[:, 0:N], ALU.mult,
                    ALU.mult, xb + [bf("gsc"), bf("rstd")], [bf(f"ntmp{kt % 2}")])
                shc = (3 * which) * KT + kt
                act(dst[:, kt, 0:N], tmp[:, 0:N], AF.Identity, [bf(f"ntmp{kt % 2}"), bf("modT")], dstb,
                    bias=modT[:, shc:shc + 1])

        def ffn(xcur, xb, N, which, win_d_, wout_d_):
            for _ in ffn_gen(xcur, xb, N, which, win_d_, wout_d_):
                pass

        def ffn_gen(xcur, xb, N, which, win_d_, wout_d_):
            norm_mod(xcur, xb, N, which, hT, [bf("hT")])
            yield

            for j0 in range(0, FT, 2):
                nj = min(2, FT - j0)

                base = (j0 // 2 % 2) * 2

                def ev_a2(ft, p_ap, p_b, base=base):
                    act(sa[base + ft][:, 0:N], p_ap, AF.Silu, [p_b], [bf(f"sa{base + ft}")])

                def ev_b2(ft, p_ap, p_b, base=base, j0=j0):
                    tt(gT[:, j0 + ft, 0:N], sa[base + ft][:, 0:N], p_ap, ALU.mult, [bf(f"sa{base + ft}"), p_b], [bf("gT")])

                yield from linear_fm_gen(win_d_, j0 * 128, nj, KT, lambda kt: (hT[:, kt, 0:N], [bf("hT")]), N, ev_a2)
                yield from linear_fm_gen(win_d_, DFF + j0 * 128, nj, KT, lambda kt: (hT[:, kt, 0:N], [bf("hT")]), N, ev_b2)

            def ev_out(ft, p_ap, p_b):
                stt(xcur[:, ft, 0:N], p_ap, hga[:, which * KT + ft:which * KT + ft + 1], xcur[:, ft, 0:N], ALU.mult, ALU.add,
                    [p_b, bf("hga")] + xb, xb)

            yield from linear_fm_gen(wout_d_, 0, KT, FT, lambda kt: (gT[:, kt, 0:N], [bf("gT")]), N, ev_out, CW=1)

        _sfb = [bf(f"Sf#{hb}") for hb in range(2 if H >= 8 else 1)]
        _sbb = [bf(f"Sb#{hb}") for hb in range(2 if H >= 8 else 1)]
        E.op("dve", lambda e: e.memset(Sf[:], 0.0), [], _sfb)
        if not FUSED:
            cp(Sf[:, :, 128:256], bc_mid(identf[:], H), [bf("identf")] + _sfb, _sfb)
        cp(Sb[:], Sf[:], _sfb, _sbb, eng="act")

        def qkv_conv(N, xb_h2, is_halo, vm=None, own=True, q_tails_only=False):
            ft_off = 0 if own else KT
            pending = []

            def flush():
                bufsets = [(rt, "rt", rstd, "rstd"), (ntmp[0], "ntmp0", ntmp[1], "ntmp1")]
                banks = []
                for i_, (ft_, h_, isq_, qr_, qrb_) in enumerate(pending):
                    p2, p2b = psum()
                    s_ = sq[i_]
                    act(s_[:, 0:N], qr_[:, 0:N], AF.Square, [qrb_], [bf(f"sq{i_}")])
                    mm(p2[:, 0:N], onesb[:], s_[:, 0:N], True, True, [bf(f"sq{i_}"), bf("onesb")], [p2b])
                    banks.append((p2, p2b))
                for i_ in range(len(pending)):
                    a_, an_, b_, bn_ = bufsets[i_]
                    act(a_[:, 0:N], banks[i_][0][:, 0:N], AF.Ln, [banks[i_][1]], [bf(an_)], bias=EPS)
                for i_ in range(len(pending)):
                    a_, an_, b_, bn_ = bufsets[i_]
                    act(b_[:, 0:N], a_[:, 0:N], AF.Exp, [bf(an_)], [bf(bn_)], scale=-0.5)
                for i_, (ft_, h_, isq_, qr_, qrb_) in enumerate(pending):
                    a_, an_, b_, bn_ = bufsets[i_]
                    dst = qnT if isq_ else knT
                    stt(dst[:, h_, 0:N], qr_[:, 0:N], (128.0 ** -0.5) if isq_ else 1.0, b_[:, 0:N], ALU.mult, ALU.mult,
                        [qrb_, bf(bn_)], [bf("qnT" if isq_ else "knT")])
                pending.clear()

            def ev_pre(ft, p_ap, p_b):
                ft = ft + ft_off
                if is_halo:
                    ts(tails[:, ft, :], p_ap, masks[:, 0:1], None, ALU.mult, None, [p_b, bf("masks")], [bf("tails")])
                    return
                slot = pre[ft % 3]
                slb = bf(f"pre{ft % 3}")
                cp(slot[:, 0:4], tails[:, ft, :], [bf("tails")], [slb], eng="act")
                if vm is not None:
                    ts(slot[:, 4:4 + N], p_ap, vm, None, ALU.mult, None, [p_b, bf("masks")], [slb])
                else:
                    cp(slot[:, 4:4 + N], p_ap, [p_b], [slb], eng=("act" if ft % 2 else "dve"))
                cp(tails[:, ft, :], slot[:, N:N + 4], [slb], [bf("tails")])
                if q_tails_only and ft < KT:
                    return
                ds = dslot[ft % 2]
                dsb = bf(f"dslot{ft % 2}")
                for j in range(4):
                    col = c.V_CONV + j * 3 * KT + ft
                    act(ds[:, j, :], identb[:], AF.Copy, [bf("identb"), bf("vecs")], [dsb], scale=vecs[:, col:col + 1])
                p_t, p_b2 = psum()
                for j in range(4):
                    mm(p_t[:, 0:N], ds[:, j, :], slot[:, 1 + j:1 + j + N], j == 0, j == 3, [dsb, slb], [p_b2])
                if ft < 2 * KT:
                    h = ft % KT
                    isq = ft < KT
                    qr = qraw[ft % 2]
                    qrb = bf(f"qraw{ft % 2}")
                    act(qr[:, 0:N], p_t[:, 0:N], AF.Silu, [p_b2], [qrb])
                    pending.append((ft, h, isq, qr, qrb))
                    if len(pending) == 2:
                        flush()
                else:
                    act(vT[:, ft - 2 * KT, 0:N], p_t[:, 0:N], AF.Silu, [p_b2], [bf("vT")])

            linear_fm(win_d, c.OFF_Q + ft_off * 128, 3 * KT - ft_off, KT, lambda kt: (h2[:, kt, 0:N], xb_h2), N, ev_pre)
            if pending:
                flush()

        HB = 4 if H >= 4 else H
        NHB = H // HB

        FILL = [None]

        def fill(k=1):
            g_ = FILL[0]
            if g_ is None:
                return
            for _ in range(k):
                try:
                    next(g_)
                except StopIteration:
                    FILL[0] = None
                    return

        def ab_proj(c0, ci):
            csl = slice(c0, c0 + 128)
            p_t, p_b = psum()
            for kt in range(KT):
                mm(p_t[:, 0:2 * H], h2[:, kt, csl], wab[:, kt, :], kt == 0, kt == KT - 1, [bf("h2"), bf("wab")], [p_b])
            cp(abT[:, ci, :], p_t[:, 0:2 * H], [p_b], [bf("abT")])

        def small_chain_tb(vm):
            Z = lambda n: smT[n][:]
            zB = lambda n: bf("smT_" + n)
            abv = abT[:]
            a_ap = abv[:, :, 0:H]
            b_ap = abv[:, :, H:2 * H]
            v3 = lambda n: smT[n][:].rearrange("p (c h) -> p c h", c=CPB)
            act(v3("e"), b_ap, AF.Exp, [bf("abT")], [zB("e")], scale=-1.0)
            ts(Z("e"), Z("e"), 1.0, None, ALU.add, None, [zB("e")], [zB("e")])
            recip(Z("beta"), Z("e"), [zB("e")], [zB("beta")])
            if vm is not None:
                ts(Z("beta"), Z("beta"), vm, None, ALU.mult, None, [zB("beta"), bf("masks")], [zB("beta")])
            tt(v3("x"), a_ap, bc_mid(bc8[:, H:2 * H], CPB), ALU.add, [bf("abT"), bf("bc8")], [zB("x")])
            act(Z("ex"), Z("x"), AF.Exp, [zB("x")], [zB("ex")])
            act(Z("sp"), Z("ex"), AF.Ln, [zB("ex")], [zB("sp")], bias=1.0)
            stt(v3("g"), v3("sp"), -1.0, bc_mid(ealog[:], CPB), ALU.mult, ALU.mult, [zB("sp"), bf("ealog")], [zB("g")])
            p_t, p_b = psum()
            mm(p_t[:, 0:CPB * H], tri[:], Z("g"), True, True, [bf("tri"), zB("g")], [p_b])
            cp(Z("CB"), p_t[:, 0:CPB * H], [p_b], [zB("CB")])
            act(Z("eCB"), Z("CB"), AF.Exp, [zB("CB")], [zB("eCB")])
            tt(Z("kbs"), Z("eCB"), Z("beta"), ALU.mult, [zB("eCB"), zB("beta")], [zB("kbs")])
            p_t, p_b = psum()
            mm(p_t[:, 0:CPB * H], sel127[:], Z("CB"), True, True, [bf("sel127"), zB("CB")], [p_b])
            tt(Z("dka"), p_t[:, 0:CPB * H], Z("CB"), ALU.subtract, [p_b, zB("CB")], [zB("dka")])
            act(Z("gle"), p_t[:, 0:CPB * H], AF.Exp, [p_b], [zB("gle")])
            act(Z("dke"), Z("dka"), AF.Exp, [zB("dka")], [zB("dke")])

        def chunk_solve(c0, gci, own=True, vm=None, ci=None):
            csl = slice(c0, c0 + 128)
            assert ci is not None
            sm = {n: smT[n][:, ci * H:(ci + 1) * H] for n in smT}
            S = lambda n: sm[n]
            sB = lambda n: bf("smT_" + n)
            fill()

            HS = [slice(hb * HB, (hb + 1) * HB) for hb in range(NHB)]
            hbf = lambda name, hb: bf(f"{name}#{hb}")
            f2 = lambda t, hb: t[:, HS[hb], :].rearrange("p h f -> p (h f)")
            W_ = HB * 128

            def each(fn):
                for hb in range(NHB):
                    fn(hb)
                fill()

            def s_rb(hb):
                hs = HS[hb]
                tt(dg[:, hs, :], bc_mid(identf[:], HB), bc_last(sm["CB"][:, hs], 128), ALU.mult, [bf("identf"), sB("CB")], [hbf("tw", hb)])
                p_t, p_b = psum()
                mm(p_t[:, 0:W_], onesf[:], f2(dg, hb), True, True, [bf("onesf"), hbf("tw", hb)], [p_b])
                cp(f2(RB, hb), p_t[:, 0:W_], [p_b], [hbf("RB", hb)], eng="act")
            each(s_rb)

            def s_e1(hb):
                hs = HS[hb]
                tt(t1[:, hs, :], bc_mid(neg1[:], HB), RB[:, hs, :], ALU.subtract, [bf("neg1"), hbf("RB", hb)], [hbf("tw", hb)])
                tt(t1[:, hs, :], t1[:, hs, :], bc_last(sm["CB"][:, hs], 128), ALU.add, [hbf("tw", hb), sB("CB")], [hbf("tw", hb)])
                act(E1[:, hs, :], t1[:, hs, :], AF.Exp, [hbf("tw", hb)], [hbf("E1", hb)])
                tt(E1[:, hs, :], E1[:, hs, :], bc_last(sm["beta"][:, hs], 128), ALU.mult, [hbf("E1", hb), sB("beta")], [hbf("E1", hb)])
            each(s_e1)

            def s_gram(hb):
                pG, pGb = psum()
                for hh in range(HB):
                    h = hb * HB + hh
                    mm(pG[:, hh * 128:(hh + 1) * 128], knT[:, h, csl], knT[:, h, csl], True, True, [bf("knT")], [pGb])
                tt(f2(Afull, hb), pG[:, 0:W_], f2(E1, hb), ALU.mult, [pGb, hbf("E1", hb)], [hbf("Afull", hb)])
            each(s_gram)
            if own:
                def s_e2(hb):
                    hs = HS[hb]
                    tt(t2[:, hs, :], RB[:, hs, :], bc_mid(neg2[:], HB), ALU.add, [bf("neg2"), hbf("RB", hb)], [hbf("tw", hb)])
                    tt(t2[:, hs, :], t2[:, hs, :], bc_last(sm["CB"][:, hs], 128), ALU.subtract, [hbf("tw", hb), sB("CB")], [hbf("tw", hb)])
                    act(E2[:, hs, :], t2[:, hs, :], AF.Exp, [hbf("tw", hb)], [hbf("E2", hb)])
                    act(Eg[:, hs, :], RB[:, hs, :], AF.Exp, [hbf("RB", hb)], [hbf("Eg", hb)])
                    pQ, pQb = psum()
                    for hh in range(HB):
                        h = hb * HB + hh
                        mm(pQ[:, hh * 128:(hh + 1) * 128], knT[:, h, csl], qnT[:, h, csl], True, True, [bf("knT"), bf("qnT")], [pQb])
                    tt(f2(qkT, hb), pQ[:, 0:W_], f2(E2, hb), ALU.mult, [pQb, hbf("E2", hb)], [hbf("qkT", hb)])
                    tt(qdT[:, hs, :], qnT[:, hs, csl], Eg[:, hs, :], ALU.mult, [bf("qnT"), hbf("Eg", hb)], [hbf("qdT", hb)])
                each(s_e2)

            mk = lambda i: bc_mid(bmask[:, i, :], HB)

            def transp(src, srcb_fn, hb):
                pT, pTb = psum()
                pTv = pT[:].bitcast(BF16)
                for hh in range(HB):
                    h = hb * HB + hh
                    tr(pTv[:, hh * 128:(hh + 1) * 128], src(h), identb[:], [srcb_fn(hb), bf("identb")], [pTb])
                return pTv, pTb

            def s_at(hb):
                hs = HS[hb]
                pTv, pTb = transp(lambda h: Afull[:, h, :], lambda hb_: hbf("Afull", hb_), hb)
                cp(f2(ATfull, hb), pTv[:, 0:W_], [pTb], [hbf("ATfull", hb)], eng="act")
                tt(Nb[0][:, hs, :], Afull[:, hs, :], mk(0), ALU.mult, [hbf("Afull", hb), bf("bmask")], [hbf("Nb0", hb)])
                tt(NTb[0][:, hs, :], ATfull[:, hs, :], mk(0), ALU.mult, [hbf("ATfull", hb), bf("bmask")], [hbf("NTb0", hb)])
                tt(Ub[0][:, hs, :], bc_mid(identb[:], HB), NTb[0][:, hs, :], ALU.subtract, [hbf("NTb0", hb), bf("identb")], [hbf("Ub0", hb)])
            each(s_at)
            cur = 0
            NLEV = 3
            for k in range(1, NLEV + 1):
                nxt = 1 - cur

                def s_lev(hb, cur=cur, nxt=nxt, k=k):
                    pN, pNb = psum()
                    for hh in range(HB):
                        h = hb * HB + hh
                        mm(pN[:, hh * 128:(hh + 1) * 128], NTb[cur][:, h, :], Nb[cur][:, h, :], True, True,
                           [hbf(f"NTb{cur}", hb), hbf(f"Nb{cur}", hb)], [pNb])
                    cp(f2(Nb[nxt], hb), pN[:, 0:W_], [pNb], [hbf(f"Nb{nxt}", hb)], eng="act")
                    if k < NLEV:
                        pM, pMb = psum()
                        for hh in range(HB):
                            h = hb * HB + hh
                            mm(pM[:, hh * 128:(hh + 1) * 128], Nb[cur][:, h, :], NTb[cur][:, h, :], True, True,
                               [hbf(f"NTb{cur}", hb), hbf(f"Nb{cur}", hb)], [pMb])
                        cp(f2(NTb[nxt], hb), pM[:, 0:W_], [pMb], [hbf(f"NTb{nxt}", hb)])
                    pU, pUb = psum()
                    for hh in range(HB):
                        h = hb * HB + hh
                        mm(pU[:, hh * 128:(hh + 1) * 128], Nb[nxt][:, h, :], Ub[cur][:, h, :], True, True,
                           [hbf(f"Nb{nxt}", hb), hbf(f"Ub{cur}", hb)], [pUb])
                    tt(f2(Ub[nxt], hb), f2(Ub[cur], hb), pU[:, 0:W_], ALU.add, [pUb, hbf(f"Ub{cur}", hb)], [hbf(f"Ub{nxt}", hb)])
                each(s_lev)
                cur = nxt

            def s_td(hb, cur=cur):
                pTv, pTb = transp(lambda h: Ub[cur][:, h, :], lambda hb_: hbf(f"Ub{cur}", hb_), hb)
                cp(f2(Tb[0], hb), pTv[:, 0:W_], [pTb], [hbf("Tb0", hb)], eng="act")
            each(s_td)
            tcur = 0
            for i in range(3):
                nxt = 1 - cur
                tnx = 1 - tcur
                lastm = (i == 2)

                def s_mrg(hb, i=i, cur=cur, nxt=nxt, tcur=tcur, tnx=tnx, lastm=lastm):
                    hs = HS[hb]
                    tt(As1[:, hs, :], Afull[:, hs, :], mk(1 + i), ALU.mult, [hbf("Afull", hb), bf("bmask")], [hbf("As", hb)])
                    pW, pWb = psum()
                    for hh in range(HB):
                        h = hb * HB + hh
                        mm(pW[:, hh * 128:(hh + 1) * 128], As1[:, h, :], Ub[cur][:, h, :], True, True,
                           [hbf("As", hb), hbf(f"Ub{cur}", hb)], [pWb])
                    cp(f2(Wb, hb), pW[:, 0:W_], [pWb], [hbf("Wb", hb)], eng="act")
                    if not lastm:
                        tt(Ms1[:, hs, :], ATfull[:, hs, :], mk(1 + i), ALU.mult, [hbf("ATfull", hb), bf("bmask")], [hbf("Ms", hb)])
                        pV, pVb = psum()
                        for hh in range(HB):
                            h = hb * HB + hh
                            mm(pV[:, hh * 128:(hh + 1) * 128], Ms1[:, h, :], Tb[tcur][:, h, :], True, True,
                               [hbf("Ms", hb), hbf(f"Tb{tcur}", hb)], [pVb])
                        cp(f2(Vb, hb), pV[:, 0:W_], [pVb], [hbf("Vb", hb)])
                    pU, pUb = psum()
                    for hh in range(HB):
                        h = hb * HB + hh
                        mm(pU[:, hh * 128:(hh + 1) * 128], Tb[tcur][:, h, :], Wb[:, h, :], True, True,
                           [hbf(f"Tb{tcur}", hb), hbf("Wb", hb)], [pUb])
                    tt(f2(Ub[nxt], hb), f2(Ub[cur], hb), pU[:, 0:W_], ALU.subtract, [pUb, hbf(f"Ub{cur}", hb)], [hbf(f"Ub{nxt}", hb)])
                    if not lastm:
                        pX, pXb = psum()
                        for hh in range(HB):
                            h = hb * HB + hh
                            mm(pX[:, hh * 128:(hh + 1) * 128], Ub[cur][:, h, :], Vb[:, h, :], True, True,
                               [hbf(f"Ub{cur}", hb), hbf("Vb", hb)], [pXb])
                        tt(f2(Tb[tnx], hb), f2(Tb[tcur], hb), pX[:, 0:W_], ALU.subtract, [pXb, hbf(f"Tb{tcur}", hb)], [hbf(f"Tb{tnx}", hb)])
                each(s_mrg)
                cur = nxt
                tcur = tnx
            U = Ub[cur]
            Ubn = f"Ub{cur}"

            def s_kv(hb):
                hs = HS[hb]
                pTv, pTb = transp(lambda h: knT[:, h, csl], lambda hb_: bf("knT"), hb)
                pT3 = pTv[:, 0:W_].rearrange("p (h f) -> p h f", h=HB)
                tt(kd[:, hs, :], pT3, bc_last(sm["dke"][:, hs], 128), ALU.mult, [pTb, sB("dke")], [hbf("kd", hb)])
                tt(kbg[:, hs, :], pT3, bc_last(sm["kbs"][:, hs], 128), ALU.mult, [pTb, sB("kbs")], [hbf("kbg", hb)])
                pTv, pTb = transp(lambda h: vT[:, h, csl], lambda hb_: bf("vT"), hb)
                pT3 = pTv[:, 0:W_].rearrange("p (h f) -> p h f", h=HB)
                tt(vb[:, hs, :], pT3, bc_last(sm["beta"][:, hs], 128), ALU.mult, [pTb, sB("beta")], [hbf("vb", hb)])
            each(s_kv)

            def s_uw(hb):
                hs = HS[hb]
                pu, pub = psum()
                pw, pwb = psum()
                for hh in range(HB):
                    h = hb * HB + hh
                    mm(pu[:, hh * 128:(hh + 1) * 128], U[:, h, :], vb[:, h, :], True, True, [hbf(Ubn, hb), hbf("vb", hb)], [pub])
                    mm(pw[:, hh * 128:(hh + 1) * 128], kbg[:, h, :], U[:, h, :], True, True, [hbf(Ubn, hb), hbf("kbg", hb)], [pwb])
                cp(upad[:, hs, 0:128], pu[:, 0:W_].rearrange("p (h f) -> p h f", h=HB), [pub], [hbf("upad", hb)], eng="act")
                cp(f2(wT, hb), pw[:, 0:W_], [pwb], [hbf("wT", hb)])
            each(s_uw)

            so = ostage[gci % 2]
            sr = rstage[gci % 2]
            sob, srb = bf(f"ostage{gci % 2}"), bf(f"rstage{gci % 2}")

            def s_vn(hb):
                for h2i in range(hb * HB, (hb + 1) * HB, 2):
                    p1, p1b = psum()
                    for hh in range(2):
                        h = h2i + hh
                        mm(p1[:, hh * SW:(hh + 1) * SW], wT[:, h, :], Sb[:, h, :], True, True, [hbf("wT", hb), hbf("Sb", hb)], [p1b])
                    tt(vnew[:, h2i:h2i + 2, :].rearrange("p h f -> p (h f)"), upad[:, h2i:h2i + 2, :].rearrange("p h f -> p (h f)"),
                       p1[:, 0:2 * SW], ALU.subtract, [p1b, hbf("upad", hb)], [hbf("vnew", hb)])
            each(s_vn)
            if own:
                def s_o(hb):
                    hs = HS[hb]
                    po, pob = psum()
                    for hh in range(HB):
                        h = hb * HB + hh
                        mm(po[:, hh * 128:(hh + 1) * 128], Sb[:, h, 0:128], qdT[:, h, :], True, False, [hbf("Sb", hb), hbf("qdT", hb)], [pob])
                        mm(po[:, hh * 128:(hh + 1) * 128], vnew[:, h, 0:128], qkT[:, h, :], False, True, [hbf("vnew", hb), hbf("qkT", hb)], [pob])
                    if FUSED:
                        cp(oTtb[:, hs, csl], po[:, 0:W_].rearrange("p (h f) -> p h f", h=HB), [pob], [bf("oTtb")], eng="act")
                        return
                    pr, prb = psum()
                    for hh in range(HB):
                        h = hb * HB + hh
                        mm(pr[:, hh * 128:(hh + 1) * 128], Sb[:, h, 128:256], qdT[:, h, :], True, False, [hbf("Sb", hb), hbf("qdT", hb)], [prb])
                        mm(pr[:, hh * 128:(hh + 1) * 128], vnew[:, h, 128:256], qkT[:, h, :], False, True, [hbf("vnew", hb), hbf("qkT", hb)], [prb])
                    cp(f2(so, hb), po[:, 0:W_], [pob], [sob], eng="act")
                    cp(f2(sr, hb), pr[:, 0:W_], [prb], [srb], eng="act")
                each(s_o)

            def s_st(hb):
                hs = HS[hb]
                for h2i in range(hb * HB, (hb + 1) * HB, 2):
                    p3, p3b = psum()
                    for hh in range(2):
                        h = h2i + hh
                        mm(p3[:, hh * SW:(hh + 1) * SW], kd[:, h, :], vnew[:, h, :], True, True, [hbf("kd", hb), hbf("vnew", hb)], [p3b])
                    for hh in range(2):
                        h = h2i + hh
                        stt(Sf[:, h, :], Sf[:, h, :], sm["gle"][:, h:h + 1], p3[:, hh * SW:(hh + 1) * SW], ALU.mult, ALU.add,
                            [p3b, hbf("Sf", hb), sB("gle")], [hbf("Sf", hb)])
                cp(Sb[:, hs, :], Sf[:, hs, :], [hbf("Sf", hb)], [hbf("Sb", hb)], eng="act")
            each(s_st)
            if not FUSED:
                dma("sp", os_d[gci], so[:].rearrange("p h f -> p (h f)"), [sob], [bf(f"os{gci}")], f"ost{gci % 2}")
                dma("sp", rs_d[gci], sr[:].rearrange("p h f -> p (h f)"), [srb], [bf(f"rs{gci}")], f"rst{gci % 2}")

        blocks = [(0, 4, True)] + [(4 + i * TB, TB, False) for i in range(NB)]
        for bi, (col0, N, is_halo) in enumerate(blocks if part in (0, 1) else []):
            xi = bi % 2
            xcur = xt[xi]
            xb = [bf("xt0")]
            dma("sp", xcur[:, :, 0:N], xT_d[:, :, col0:col0 + N], [], xb, f"xld{xi}")
            ffn(xcur, xb, N, 0, f1in_d, f1out_d)
            norm_mod(xcur, xb, N, 1, h2, [bf("h2")])
            if not is_halo:
                t0 = col0 - 4
                dma("sp", x1s_d[:, :, t0:t0 + N], xcur[:, :, 0:N], xb, [bf(f"x1s{bi}")], f"x1st{xi}")
                dma("sp", h2s_d[:, :, t0:t0 + N], h2[:, :, 0:N], [bf("h2")], [bf(f"h2s{bi}")], "h2st")
            E.barrier()
            qkv_conv(N, [bf("h2")], is_halo)
            if is_halo:
                chk(4)
            else:
                chk(5)
            if not is_halo:
                for ci in range(CPB):
                    chunk_solve(ci * 128, (bi - 1) * CPB + ci)
                    chk(6)
            E.barrier()

        chk(7)
        if part in (0, 1):
            dma("sp", st_in_t.ap(), Sf[:].rearrange("p h f -> p (h f)"), [bf("Sf")], [bf("st_in")], "stio")
        if part == 1:
            raise Stop()
        if G > 1 and part == 0:
            groups = [list(range(g0, g0 + G)) for g0 in range(0, c.NCORES, G)]
            E.op("pool", lambda e: e.collective_compute("AllGather", ALU.bypass, replica_groups=groups,
                                                        ins=[st_in_t.ap().opt()], outs=[st_all_t.ap().opt()]),
                 [bf("st_in")], [bf("st_all")], dma="ccsem")
        if not FUSED:
            E.op("dve", lambda e: e.memset(Sst[:], 0.0), [], [bf("Sst")])
        for i in (range(G - 1) if not FUSED else []):
            dma("sp", stg[i][:].rearrange("p h f -> p (h f)"), st_all_t.ap()[i * 128:(i + 1) * 128, :], [bf("st_all")],
                [bf("stg0")], f"stgl{i}")
            for hb in range(NHB):
                pT, pTb = psum()
                for hh in range(HB):
                    h = hb * HB + hh
                    tr(pT[:, hh * 128:(hh + 1) * 128], stg[i][:, h, 128:256], identf[:], [bf("stg0"), bf("identf")], [pTb])
                cp(PiT[:, hb * HB:(hb + 1) * HB, :].rearrange("p h f -> p (h f)"), pT[:, 0:HB * 128], [pTb], [bf("PiT")])
            for hb in range(NHB):
                hs = slice(hb * HB, (hb + 1) * HB)
                pc, pcb = psum()
                for hh in range(HB):
                    h = hb * HB + hh
                    mm(pc[:, hh * 128:(hh + 1) * 128], PiT[:, h, :], Sst[:, h, :], True, True, [bf("PiT"), bf("Sst")], [pcb])
                tt(cand[:, hs, :], pc[:, 0:HB * 128].rearrange("p (h f) -> p h f", h=HB), stg[i][:, hs, 0:128], ALU.add,
                   [pcb, bf("stg0")], [bf("cand")])
            tt(cand[:], cand[:], Sst[:], ALU.subtract, [bf("cand"), bf("Sst")], [bf("cand")])
            stt(Sst[:], cand[:], masks[:, 1 + i:2 + i], Sst[:], ALU.mult, ALU.add, [bf("cand"), bf("masks"), bf("Sst")],
                [bf("Sst")])
        if not FUSED:
            cp(Sstb[:], Sst[:], [bf("Sst")], [bf("Sstb")])
        E.barrier()

        chk(8)
        def phase2_tb(tbi):
            t0 = tbi * TB
            N = TB
            xi = tbi % 2
            xcur = xt[xi]
            xb = [bf("xt0")]
            if not FUSED:
                dma("sp", xcur[:], x1s_d[:, :, t0:t0 + N], [bf(f"x1s{tbi + 1}")], xb, f"xld{xi}")
                dma("sp", h2[:], h2s_d[:, :, t0:t0 + N], [bf(f"h2s{tbi + 1}")], [bf("h2")], "h2ld")
            dma("pool", wv[:], win_d[:, c.OFF_V:c.OFF_V + D].rearrange("(kt p) n -> p kt n", p=128), [bf("g_w_in")], [bf("wv")], "wvld")
            h2r = lambda kt: (h2[:, kt, 0:N], [bf("h2")])

            linear_fm(win_d, c.OFF_Z, KT, KT, h2r, N,
                      lambda ft, p_ap, p_b: act(zs[:, ft, :], p_ap, AF.Silu, [p_b], [bf("zs")]))
            linear_fm(win_d, c.OFF_U, KT, KT, h2r, N,
                      lambda ft, p_ap, p_b: act(ug[:, ft, :], p_ap, AF.Gelu, [p_b], [bf("ug")]))
            def o_part(ci):
                gci = tbi * CPB + ci
                csl = slice(ci * 128, (ci + 1) * 128)
                so, sr = ostage[gci % 2], rstage[gci % 2]
                sob, srb = bf(f"ostage{gci % 2}"), bf(f"rstage{gci % 2}")
                if FUSED:
                    cp(otrue[:], oTtb[:, :, csl], [bf("oTtb")], [bf("otrue")])
                else:
                    dma("sp", so[:].rearrange("p h f -> p (h f)"), os_d[gci], [bf(f"os{gci}")], [sob], f"old{gci % 2}")
                    dma("sp", sr[:].rearrange("p h f -> p (h f)"), rs_d[gci], [bf(f"rs{gci}")], [srb], f"rld{gci % 2}")
                for hb in (range(NHB) if not FUSED else []):
                    hs = slice(hb * HB, (hb + 1) * HB)
                    pc, pcb = psum()
                    for hh in range(HB):
                        h = hb * HB + hh
                        mm(pc[:, hh * 128:(hh + 1) * 128], Sstb[:, h, :], sr[:, h, :], True, True, [bf("Sstb"), srb], [pcb])
                    tt(otrue[:, hs, :], pc[:, 0:HB * 128].rearrange("p (h f) -> p h f", h=HB), so[:, hs, :], ALU.add,
                       [pcb, sob], [bf("otrue")])
                act(osq[:], otrue[:], AF.Square, [bf("otrue")], [bf("osq")])
                yield
                for hb in range(NHB):
                    hs = slice(hb * HB, (hb + 1) * HB)
                    pn, pnb = psum()
                    mm(pn[:, 0:HB * 128], onesV[:], osq[:, hs, :].rearrange("p h f -> p (h f)"), True, True,
                       [bf("onesV"), bf("osq")], [pnb])
                    act(ort[:, hs, :].rearrange("p h f -> p (h f)"), pn[:, 0:HB * 128], AF.Ln, [pnb], [bf("ort")], bias=EPS)
                yield
                act(ors[:], ort[:], AF.Exp, [bf("ort")], [bf("ort")], scale=-0.5)
                yield
                tt(otrue[:], otrue[:], ors[:], ALU.mult, [bf("otrue"), bf("ort")], [bf("otrue")])
                yield
                stt(obT[:, :, csl], otrue[:], vecs[:, c.V_DNG:c.V_DNG + 1], zs[:, :, csl], ALU.mult, ALU.mult,
                    [bf("otrue"), bf("vecs"), bf("zs")], [bf("obT")])
            def g_part(ci):
                csl = slice(ci * 128, (ci + 1) * 128)
                for o0 in range(0, D, 512):
                    n = min(512, D - o0)
                    p_t, p_b = psum()
                    for kt in range(KT):
                        mm(p_t[:, 0:n], h2[:, kt, csl], wv[:, kt, o0:o0 + n], kt == 0, kt == KT - 1, [bf("h2"), bf("wv")], [p_b])
                    act(vg[:, o0:o0 + n], p_t[:, 0:n], AF.Gelu, [p_b], [bf("vg")])
                    yield
                SS = lambda n: st[n][:]
                sBB = lambda n: bf("st_" + n)
                E.op("dve", lambda e: e.tensor_reduce(out=st["s1"][:], in_=vg[:], axis=AX.X, op=ALU.add), [bf("vg")], [sBB("s1")])
                tt(vsq[:], vg[:], vg[:], ALU.mult, [bf("vg")], [bf("mtmp")])
                yield
                E.op("dve", lambda e: e.tensor_reduce(out=st["s2"][:], in_=vsq[:], axis=AX.X, op=ALU.add), [bf("mtmp")], [sBB("s2")])
                ts(SS("mean"), SS("s1"), 1.0 / D, None, ALU.mult, None, [sBB("s1")], [sBB("mean")])
                yield
                tt(SS("msq"), SS("mean"), SS("mean"), ALU.mult, [sBB("mean")], [sBB("msq")])
                stt(SS("var"), SS("s2"), 1.0 / D, SS("msq"), ALU.mult, ALU.subtract, [sBB("s2"), sBB("msq")], [sBB("var")])
                yield
                act(SS("sd"), SS("var"), AF.Sqrt, [sBB("var")], [sBB("sd")], bias=EPS)
                recip(SS("rstd"), SS("sd"), [sBB("sd")], [sBB("rstd")])
                yield
                ts(nrm[:], vg[:], SS("mean"), SS("rstd"), ALU.subtract, ALU.mult, [bf("vg"), sBB("mean"), sBB("rstd")], [bf("nrm")])
                yield
                for gb in range(0, KT, 4):
                    ng = min(4, KT - gb)
                    pM, pMb = psum()
                    for gg in range(ng):
                        g = gb + gg
                        mm(pM[:, gg * 128:(gg + 1) * 128], nrm[:, g * 128:(g + 1) * 128], WmT[:, g, :], True, True,
                           [bf("nrm"), bf("WmT")], [pMb])
                    for gg in range(ng):
                        g = gb + gg
                        stt(mtmp[:, g, :], pM[:, gg * 128:(gg + 1) * 128], vecs[:, c.V_LNG + g:c.V_LNG + g + 1], BiasG[:, g, :],
                            ALU.mult, ALU.add, [pMb, bf("vecs"), bf("BiasG")], [bf("mtmp")])
                    yield
                tt(oaT[:, :, csl], mtmp[:], ug[:, :, csl], ALU.mult, [bf("mtmp"), bf("ug")], [bf("oaT")])
            FILL[0] = linear_fm_gen(win_d, c.OFF_GATE, 2 * KT, KT, h2r, N,
                                    lambda ft, p_ap, p_b: act(gt[:, ft, :], p_ap, AF.Sigmoid, [p_b], [bf("gt")]))
            for ci in range(CPB):
                gens = [o_part(ci), g_part(ci)]
                while gens:
                    for g_ in list(gens):
                        try:
                            next(g_)
                        except StopIteration:
                            gens.remove(g_)
                    fill()
            if FILL[0] is not None:
                for _ in FILL[0]:
                    pass
                FILL[0] = None
            linear_fm(wbr_d[0], 0, KT, KT, lambda kt: (oaT[:, kt, :], [bf("oaT")]), N,
                      lambda ft, p_ap, p_b: tt(mA[:, ft, :], p_ap, gt[:, ft, :], ALU.mult, [p_b, bf("gt")], [bf("mA")]))

            def ev_brB(ft, p_ap, p_b):
                tt(mB[:], p_ap, gt[:, KT + ft, :], ALU.mult, [p_b, bf("gt")], [bf("mB")])
                tt(mergedT[:, ft, :], mB[:], mA[:, ft, :], ALU.add, [bf("mB"), bf("mA")], [bf("mergedT")])

            linear_fm(wbr_d[1], 0, KT, KT, lambda kt: (obT[:, kt, :], [bf("obT")]), N, ev_brB)
            linear_fm(wout_d, 0, KT, KT, lambda kt: (mergedT[:, kt, :], [bf("mergedT")]), N,
                      lambda ft, p_ap, p_b: stt(xcur[:, ft, :], p_ap, hga[:, KT + ft:KT + ft + 1], xcur[:, ft, :], ALU.mult,
                                                ALU.add, [p_b, bf("hga")] + xb, xb))
            E.barrier()
            ffn(xcur, xb, N, 2, f3in_d, f3out_d)
            rms_stats(lambda kt: xcur[:, kt, 0:N], xb, N, onesD, bf("onesD"))
            og = ostg[tbi % 2]
            ogb = bf("ostg0")
            for kt in range(KT):
                stt(og[:, kt, :], xcur[:, kt, :], vecs[:, c.V_NF + kt:c.V_NF + kt + 1], rstd[:, 0:N], ALU.mult, ALU.mult,
                    xb + [bf("vecs"), bf("rstd")], [ogb])
            dma("sp", outT_d[:, :, t0:t0 + N], og[:], [ogb], [bf(f"out{tbi}")], f"outst{tbi % 2}")
            E.barrier()

        if not FUSED:
            for tbi in range(NB):
                phase2_tb(tbi)
        else:
            E.op("pool", lambda e: e.memset(tails[:], 0.0), [], [bf("tails")])
            blist = [(wi, tbi) for wi in range(G) for tbi in range(NB)]

            def stageA_gen(wi, tbi):
                col0 = wi * NT + tbi * TB
                xcur = xt[0]
                xb = [bf("xt0")]
                dma("sp", xcur[:, :, 0:TB], xT_d[:, :, col0:col0 + TB], [], xb, "xld0")
                yield from ffn_gen(xcur, xb, TB, 0, f1in_d, f1out_d)
                norm_mod(xcur, xb, TB, 1, h2, [bf("h2")])
                yield

            for _ in stageA_gen(*blist[0]):
                pass
            for bi_, (wi, tbi) in enumerate(blist):
                own = (wi == G - 1)
                vm = None if own else masks[:, wi:wi + 1]
                N = TB
                lastp = (wi == G - 2 and tbi == NB - 1)
                if own:
                    E.barrier()
                qkv_conv(N, [bf("h2")], False, vm=vm, own=(own or lastp), q_tails_only=lastp)
                for ci in range(CPB):
                    ab_proj(ci * 128, ci)
                small_chain_tb(vm)
                nxt_blk = blist[bi_ + 1] if bi_ + 1 < len(blist) else None
                if (not own) and nxt_blk is not None:
                    FILL[0] = stageA_gen(*nxt_blk)
                for ci in range(CPB):
                    chunk_solve(ci * 128, 0, own=own, vm=vm, ci=ci)
                if FILL[0] is not None:
                    for _ in FILL[0]:
                        pass
                    FILL[0] = None
                if own:
                    E.barrier()
                    phase2_tb(tbi)
                    if nxt_blk is not None:
                        for _ in stageA_gen(*nxt_blk):
                            pass
        return B

    Ep = Emit(nc, plan_only=True)
    try:
        program(Ep)
    except Stop:
        pass
    Ee = Emit(nc, plan_only=False, wplan=Ep.wreq)
    try:
        program(Ee)
    except Stop:
        pass

    keys = set()
    for eng, lst in Ee.ops.items():
        for waits, fn, inc in lst:
            keys.add(inc[0])
            for k, v in waits:
                keys.add(k)
    sem = {k: nc.alloc_semaphore(name="s_" + k) for k in sorted(keys)}
    final = dict(Ee.dcnt)

    with nc.Block() as block:
        def replay(e, name, last=False):
            for waits, fn, inc in Ee.ops[name]:
                for k, v in waits:
                    e.wait_ge(sem[k], v)
                ins = fn(e)
                ins.then_inc(sem[inc[0]], inc[1])
            if last:
                for k, v in final.items():
                    e.wait_ge(sem[k], v)
                for k in ("pe", "act", "dve", "pool"):
                    e.wait_ge(sem[k], Ee.cnt[k])

        @block.tensor
        def _(e):
            replay(e, "pe")

        @block.scalar
        def _(e):
            replay(e, "act")

        @block.vector
        def _(e):
            replay(e, "dve")

        @block.gpsimd
        def _(e):
            replay(e, "pool")

        @block.sync
        def _(e):
            replay(e, "sp", last=True)

    nc._n_ops = {k: len(v) for k, v in Ee.ops.items()}
    return nc


def host_inputs(cfg, inp, fused=False):
    c = cfg
    D, KT, H, NT, G = c.D, c.KT, c.H, c.NT, c.G
    f32 = np.float32
    x = np.asarray(inp["x"], f32)
    cc = np.asarray(inp["c"], f32)

    def pk(v):
        return np.asarray(v, f32).reshape(-1, 128).T

    vecs = np.zeros((128, c.NV), f32)
    vecs[:, c.V_N1:c.V_N1 + KT] = pk(inp["norm1_g"][0])
    vecs[:, c.V_N2:c.V_N2 + KT] = pk(inp["norm2_g"][0])
    vecs[:, c.V_N3:c.V_N3 + KT] = pk(inp["norm3_g"][0])
    vecs[:, c.V_NF:c.V_NF + KT] = pk(inp["final_g"])
    vecs[:, c.V_BADA:c.V_BADA + 9 * KT] = pk(inp["b_ada"][0])
    vecs[:, c.V_LNG:c.V_LNG + KT] = pk(inp["gm_ln_g"][0])
    vecs[:, c.V_DNG] = np.asarray(inp["dn_norm_g"][0], f32)
    cw = np.asarray(inp["conv_w"][0], f32)
    for j in range(4):
        vecs[:, c.V_CONV + j * 3 * KT: c.V_CONV + (j + 1) * 3 * KT] = pk(cw[j])
    rowv = np.concatenate([np.asarray(inp["gm_ln_b"][0], f32), np.asarray(inp["gm_b_s"][0], f32).reshape(-1)])[None, :]
    bc8 = np.tile(np.concatenate([np.asarray(inp["a_log"][0], f32), np.asarray(inp["dt_bias"][0], f32)])[None, :], (128, 1))
    w_sT = np.ascontiguousarray(np.transpose(np.asarray(inp["gm_w_s"][0], f32), (0, 2, 1)))
    shared = {
        "vecs": vecs, "rowv": np.ascontiguousarray(rowv), "bc8": np.ascontiguousarray(bc8),
        "w_sT": w_sT,
    }
    wfulls = {"w_ada": inp["w_ada"][0], "ffn1_w_in": inp["ffn1_w_in"][0], "ffn1_w_out": inp["ffn1_w_out"][0],
              "w_in": inp["w_in"][0], "w_branch": np.asarray(inp["w_branch"][0]).reshape(2 * D, D),
              "w_out": inp["w_out"][0], "ffn2_w_in": inp["ffn2_w_in"][0], "ffn2_w_out": inp["ffn2_w_out"][0]}
    wfulls = {k: np.asarray(v, f32) for k, v in wfulls.items()}
    ii = np.arange(128)
    bm = np.zeros((128, 4, 128), f32)
    bm[:, 0, :] = (ii[:, None] // 16 == ii[None, :] // 16)
    for i, sz in enumerate((16, 32, 64)):
        bm[:, 1 + i, :] = (ii[:, None] // (2 * sz) == ii[None, :] // (2 * sz)) & (ii[:, None] // sz != ii[None, :] // sz)
    maps = []
    for r in range(c.NCORES):
        b, j = r // G, r % G
        s0 = j * NT
        m = np.zeros((128, G), f32)
        if fused:
            xs = np.zeros((G * NT, D), f32)
            for sgi in range(G):
                seg = j - (G - 1) + sgi
                if seg >= 0:
                    xs[sgi * NT:(sgi + 1) * NT] = x[b, seg * NT:(seg + 1) * NT]
                    m[:, sgi] = 1.0
            xT = np.ascontiguousarray(xs.T.reshape(KT, 128, G * NT).transpose(1, 0, 2))
        else:
            xs = np.zeros((NT + 4, D), f32)
            xs[4:] = x[b, s0:s0 + NT]
            if j > 0:
                xs[:4] = x[b, s0 - 4:s0]
            xT = np.ascontiguousarray(xs.T.reshape(KT, 128, NT + 4).transpose(1, 0, 2))
            m[:, 0] = 1.0 if j > 0 else 0.0
            for i in range(G - 1):
                m[:, 1 + i] = 1.0 if i < j else 0.0
        d = dict(shared)
        for k, v in wfulls.items():
            rp = v.shape[0] // c.NCORES
            d[k] = np.ascontiguousarray(v[r * rp:(r + 1) * rp]) if c.WG else v
        d["xT"] = xT
        d["cT"] = np.ascontiguousarray(cc[b].reshape(KT, 128).T)
        d["masks"] = m
        d["bmask"] = bm
        maps.append(d)
    return maps


def host_output(cfg, results, B):
    c = cfg
    out = np.zeros((B, c.G * c.NT, c.D), np.float32)
    for r in range(c.NCORES):
        b, j = r // c.G, r % c.G
        oT = np.asarray(results[r]["outT"], np.float32).reshape(128, c.KT, c.NT)
        out[b, j * c.NT:(j + 1) * c.NT] = oT.transpose(2, 1, 0).reshape(c.NT, c.D)
    return out


_NC_CACHE = {}


def run_two_launch(cfg, inputs, nbatch):
    key = ("two", cfg.D, cfg.NT, cfg.NCORES)
    if key not in _NC_CACHE:
        _NC_CACHE[key] = (build(cfg, part=1), build(cfg, part=2))
    nc1, nc2 = _NC_CACHE[key]
    maps = host_inputs(cfg, inputs)
    res1 = run_bass_kernel_spmd(nc1, maps, core_ids=list(range(cfg.NCORES))).results
    G = cfg.G
    maps2 = []
    for r in range(cfg.NCORES):
        g0 = (r // G) * G
        d = dict(maps[r])
        for k in ("x1s", "h2s", "o_s", "r_s"):
            d[k] = np.ascontiguousarray(res1[r][k])
        d["st_all"] = np.ascontiguousarray(np.concatenate(
            [np.asarray(res1[g0 + i]["st_in"]).reshape(128, cfg.H * 256) for i in range(G)], axis=0))
        maps2.append(d)
    res2 = run_bass_kernel_spmd(nc2, maps2, core_ids=list(range(cfg.NCORES))).results
    return host_output(cfg, res2, nbatch)


def run_fused(cfg, inputs, nbatch):
    key = ("fused", cfg.D, cfg.NT, cfg.NCORES)
    if key not in _NC_CACHE:
        _NC_CACHE[key] = build(cfg, part=3)
    nc = _NC_CACHE[key]
    maps = host_inputs(cfg, inputs, fused=True)
    res = run_bass_kernel_spmd(nc, maps, core_ids=list(range(cfg.NCORES))).results
    return host_output(cfg, res, nbatch)


def kernel(**inputs):
    cfg = Cfg(WG=False)
    return run_fused(cfg, inputs, 2)
```

```python
import numpy as np
import ml_dtypes
from contextlib import ExitStack
import concourse.bass as bass
import concourse.mybir as mybir
from concourse.bass_utils import run_bass_kernel_spmd

F32 = mybir.dt.float32
BF16 = mybir.dt.bfloat16
AF = mybir.ActivationFunctionType
ALU = mybir.AluOpType
AX = mybir.AxisListType
EPS = 1e-6
NEGBIG = -1.0e30


class Cfg:
    def __init__(self, D=1024, DFF=2816, H=8, NT=2048, TB=512, G=4, NCORES=8, WG=True):
        self.WG = WG
        self.STOP = 0
        self.D, self.DFF, self.H, self.NT, self.TB, self.G, self.NCORES = D, DFF, H, NT, TB, G, NCORES
        self.KT = D // 128
        self.FT = DFF // 128
        self.NB = NT // TB
        self.CPB = TB // 128
        self.NCH = NT // 128
        assert H * 128 == D
        self.INW = 6 * D + 2 * H + 2 * D
        self.OFF_U, self.OFF_V, self.OFF_Q, self.OFF_Z = 0, D, 2 * D, 5 * D
        self.OFF_AB = 6 * D
        self.OFF_GATE = 6 * D + 2 * H
        KT = self.KT
        self.V_N1, self.V_N2, self.V_N3, self.V_NF = 0, KT, 2 * KT, 3 * KT
        self.V_BADA = 4 * KT
        self.V_LNG = 13 * KT
        self.V_DNG = 14 * KT
        self.V_CONV = 14 * KT + 1
        self.NV = self.V_CONV + 4 * 3 * KT
        self.WSLOT = max(self.FT * 128, KT * 256)


class Stop(Exception):
    pass


class Buf:
    __slots__ = ("w", "r", "name")

    def __init__(self, name=""):
        self.w = None
        self.r = {}
        self.name = name


class Emit:
    def __init__(self, nc, plan_only, wplan=None):
        self.nc = nc
        self.plan_only = plan_only
        self.ops = {e: [] for e in ("pe", "act", "dve", "pool", "sp")}
        self.cnt = {e: 0 for e in ("pe", "act", "dve", "pool")}
        self.waited = {e: {} for e in self.ops}
        self.dcnt = {}
        self.wreq = []
        self.wplan = wplan
        self.wissued = 0
        self.wconsumed = 0
        self.pend = {}

    def barrier(self):
        snap = dict(self.cnt)
        snap.update(self.dcnt)
        for e in self.ops:
            p = self.pend.get(e) or {}
            for k, v in snap.items():
                if v > p.get(k, 0):
                    p[k] = v
            self.pend[e] = p

    def op(self, eng, fn, reads=(), writes=(), dma=None):
        waits = {}

        def need(tok):
            if tok is None:
                return
            key, val = tok
            if key == eng and eng == "pe":
                return
            if self.waited[eng].get(key, 0) >= val:
                return
            if waits.get(key, 0) < val:
                waits[key] = val

        p = self.pend.get(eng)
        if p:
            for k, v in p.items():
                if v > 0:
                    need((k, v))
            self.pend[eng] = None
        for b in reads:
            need(b.w)
        for b in writes:
            need(b.w)
            for k, v in b.r.items():
                need((k, v))
        for k, v in waits.items():
            self.waited[eng][k] = v
        if dma is None:
            self.cnt[eng] += 1
            tok = (eng, self.cnt[eng])
            inc = (eng, 1)
        else:
            self.dcnt[dma] = self.dcnt.get(dma, 0) + 16
            tok = (dma, self.dcnt[dma])
            inc = (dma, 16)
        for b in reads:
            if b.r.get(tok[0], 0) < tok[1]:
                b.r[tok[0]] = tok[1]
        for b in writes:
            b.w = tok
            b.r = {}
        if not self.plan_only:
            self.ops[eng].append((list(waits.items()), fn, inc))
        return tok


def build(cfg, part=0):
    c = cfg
    D, KT, DFF, FT, H, NT, TB, NB, CPB, NCH, G = c.D, c.KT, c.DFF, c.FT, c.H, c.NT, c.TB, c.NB, c.CPB, c.NCH, c.G
    nc = bass.Bass("TRN2", target_bir_lowering=False)
    FUSED = (part == 3)

    def din(name, shape, dt=F32):
        return nc.dram_tensor(name, list(shape), dt, kind="ExternalInput").ap()

    xT_d = (din("xT", [128, KT, G * NT]) if FUSED else din("xT", [128, KT, NT + 4])) if part != 2 else None
    cT_d = din("cT", [128, KT])
    vecs_d = din("vecs", [128, c.NV])
    rowv_d = din("rowv", [1, 2 * D])
    bc8_d = din("bc8", [128, 2 * H])
    masks_d = din("masks", [128, G])
    bmask_d = din("bmask", [128, 4, 128])
    NCs = c.NCORES
    WSPEC = [("w_ada", D, 9 * D), ("ffn1_w_in", D, 2 * DFF), ("ffn1_w_out", DFF, D), ("w_in", D, c.INW),
             ("w_branch", 2 * D, D), ("w_out", D, D), ("ffn2_w_in", D, 2 * DFF), ("ffn2_w_out", DFF, D)]
    wext, wbnc, wfull = {}, {}, {}
    wap = {}
    if part == 1:
        WSPEC = [w for w in WSPEC if w[0] in ("w_ada", "ffn1_w_in", "ffn1_w_out", "w_in")]
    if part == 2:
        WSPEC = [w for w in WSPEC if w[0] not in ("ffn1_w_in", "ffn1_w_out")]
    for (wn, wr, wc) in WSPEC:
        if c.WG:
            wext[wn] = din(wn, [wr // NCs, wc])
            wbnc[wn] = nc.dram_tensor(wn + "_bnc", [wr // NCs, wc], F32)
            wfull[wn] = nc.dram_tensor(wn + "_full", [wr, wc], F32)
            wap[wn] = wfull[wn].ap()
        else:
            wap[wn] = din(wn, [wr, wc])
    w_ada_d = wap["w_ada"]
    f1in_d = wap.get("ffn1_w_in")
    f1out_d = wap.get("ffn1_w_out")
    win_d = wap["w_in"]
    wsT_d = din("w_sT", [KT, 128, 128])
    wbr_full = wap.get("w_branch")
    wbr_d = [wbr_full[0:D, :], wbr_full[D:2 * D, :]] if wbr_full is not None else None
    wout_d = wap.get("w_out")
    f3in_d = wap.get("ffn2_w_in")
    f3out_d = wap.get("ffn2_w_out")
    outT_d = nc.dram_tensor("outT", [128, KT, NT], F32, kind="ExternalOutput").ap() if part != 1 else None

    skw = {} if part in (0, 3) else {"kind": ("ExternalOutput" if part == 1 else "ExternalInput")}
    x1s_d = nc.dram_tensor("x1s", [128, KT, NT], F32, **skw).ap()
    h2s_d = nc.dram_tensor("h2s", [128, KT, NT], BF16, **skw).ap()
    os_d = nc.dram_tensor("o_s", [NCH, 128, H * 128], BF16, **skw).ap()
    rs_d = nc.dram_tensor("r_s", [NCH, 128, H * 128], BF16, **skw).ap()
    if part != 2:
        st_in_t = nc.dram_tensor("st_in", [128, H * 256], F32, **skw)
    akw = {} if part in (0, 3) else {"kind": "ExternalInput"}
    if part != 1:
        st_all_t = nc.dram_tensor("st_all", [G * 128, H * 256], F32, **akw)

    es = ExitStack()
    T = {}

    def sb(name, shape, dt):
        T[name] = nc.alloc_sbuf_tensor(name, list(shape), dt)
        return T[name]

    ARENA_E = 57 * 1024 - 512
    arena = nc.alloc_sbuf_tensor("arena", [128, ARENA_E], BF16)
    vptr = {}

    def cv(view, name, shape, dt):
        n = 1
        for d_ in shape[1:]:
            n *= d_
        ne = n * (2 if dt == F32 else 1)
        ne = (ne + 15) // 16 * 16
        off = vptr.get(view, 0)
        vptr[view] = off + ne
        assert off + ne <= ARENA_E, (view, name, off + ne)
        ap = arena[:, off:off + ne]
        if dt == F32:
            ap = ap.bitcast(F32)
        ap = ap[:, 0:n]
        if len(shape) == 3:
            ap = ap.rearrange("p (a b) -> p a b", a=shape[1])
        T[name] = ap
        return ap

    identb = sb("identb", [128, 128], BF16)
    identf = sb("identf", [128, 128], F32)
    onesb = sb("onesb", [128, 128], BF16)
    onesD = sb("onesD", [128, 128], BF16)
    onesV = sb("onesV", [128, 128], BF16)
    onesf = sb("onesf", [128, 128], F32)
    tri = sb("tri", [128, 128], F32)
    sel127 = sb("sel127", [128, 128], F32)
    neg1 = sb("neg1", [128, 128], F32)
    neg2 = sb("neg2", [128, 128], F32)
    vecs = sb("vecs_sb", [128, c.NV], F32)
    bc8 = sb("bc8_sb", [128, 2 * H], F32)
    masks = sb("masks_sb", [128, G], F32)
    bmask = sb("bmask_sb", [128, 4, 128], F32)
    ealog = sb("ealog", [128, H], F32)
    cTf = sb("cTf", [128, KT], F32)
    cact = sb("cact", [128, KT], BF16)
    modT = sb("modT", [128, 9 * KT], F32)
    gsc = sb("gsc", [128, 3 * KT], F32)
    hga = sb("hga", [128, 3 * KT], F32)
    wab = sb("wab", [128, KT, 2 * H], BF16)
    WmT = sb("WmT", [128, KT, 128], BF16)
    BiasG = sb("BiasG", [128, KT, 128], F32)
    NWS = 5
    wslot = [sb(f"wslot{i}", [128, c.WSLOT], BF16) for i in range(NWS)]
    xt = [sb("xt0", [128, KT, TB], F32)]
    xt.append(xt[0])
    h2 = sb("h2", [128, KT, TB], BF16)
    sq = [sb(f"sq{i}", [128, TB], BF16) for i in range(2)]
    rt = sb("rt", [128, TB], F32)
    rstd = sb("rstd", [128, TB], F32)
    ntmp = [sb(f"ntmp{i}", [128, TB], F32) for i in range(2)]
    tails = sb("tails", [128, 3 * KT, 4], BF16)
    SW = 128 if FUSED else 256
    upad = sb("upad", [128, H, SW], BF16)
    Sf = sb("Sf", [128, H, SW], F32)
    Sb = sb("Sb", [128, H, SW], BF16)
    ostage = [sb(f"ostage{i}", [128, H, 128], BF16) if not FUSED else None for i in range(2)]
    rstage = [sb(f"rstage{i}", [128, H, 128], BF16) if not FUSED else None for i in range(2)]
    Sstb = sb("Sstb", [128, H, 128], BF16) if not FUSED else None
    oTtb = sb("oTtb", [128, H, TB], BF16) if FUSED else None
    rowv = cv("S", "rowv_sb", [1, 2 * D], F32)[0:1, :]
    wsTf = cv("S", "wsTf", [128, KT, 128], F32)
    rsrow = cv("S", "rsrow", [1, KT * 128], F32)[0:1, :]
    pre = [cv("P1", f"pre{i}", [128, 4 + TB], BF16) for i in range(3)]
    dslot = [cv("P1", f"dslot{i}", [128, 4, 128], BF16) for i in range(2)]
    qraw = [cv("P1", f"qraw{i}", [128, TB], F32) for i in range(2)]
    knT = cv("P1", "knT", [128, H, TB], BF16)
    vT = cv("P1", "vT", [128, H, TB], BF16)
    ab = cv("P1", "ab", [128, 2 * H], F32)
    abT = cv("P1", "abT", [128, CPB, 2 * H], F32)
    sm = {n: cv("P1", "sm_" + n, [128, H], F32) for n in
          ("e", "beta", "x", "ex", "sp", "g", "CB", "eCB", "kbs", "dka", "dke", "gle")}
    smT = {n: cv("P1", "smT_" + n, [128, CPB * H], F32) for n in
           ("e", "beta", "x", "ex", "sp", "g", "CB", "eCB", "kbs", "dka", "dke", "gle")}
    dg = cv("P1", "tw", [128, H, 128], F32)
    t1 = dg
    t2 = dg
    RB = cv("P1", "RB", [128, H, 128], F32)
    E1 = cv("P1", "E1", [128, H, 128], BF16)
    E1b = E1
    Nb = [cv("P1", f"Nb{i}", [128, H, 128], BF16) for i in range(2)]
    NTb = [cv("P1", f"NTb{i}", [128, H, 128], BF16) for i in range(2)]
    Ub = [cv("P1", f"Ub{i}", [128, H, 128], BF16) for i in range(2)]
    Afull = cv("P1", "Afull", [128, H, 128], BF16)
    ATfull = cv("P1", "ATfull", [128, H, 128], BF16)
    As1 = cv("P1", "As", [128, H, 128], BF16)
    Ms1 = cv("P1", "Ms", [128, H, 128], BF16)
    Tb = [cv("P1", f"Tb{i}", [128, H, 128], BF16) for i in range(2)]
    Wb = cv("P1", "Wb", [128, H, 128], BF16)
    Vb = cv("P1", "Vb", [128, H, 128], BF16)
    kd = cv("P1", "kd", [128, H, 128], BF16)
    kbg = cv("P1", "kbg", [128, H, 128], BF16)
    vb = cv("P1", "vb", [128, H, 128], BF16)
    wT = cv("P1", "wT", [128, H, 128], BF16)
    vnew = cv("P1", "vnew", [128, H, SW], BF16)
    vptr["F"] = vptr["P1"]
    hT = cv("F", "hT", [128, KT, TB], BF16)
    gT = cv("F", "gT", [128, FT, TB], BF16)
    sa = [cv("F", f"sa{i}", [128, TB], BF16) for i in range(4)]
    ostg = [cv("FO", "ostg0", [128, KT, TB], F32)]
    ostg.append(ostg[0])
    qnT = cv("P1", "qnT", [128, H, TB], BF16)
    E2 = cv("P1", "E2", [128, H, 128], BF16)
    Eg = cv("P1", "Eg", [128, H, 128], BF16)
    qkT = cv("P1", "qkT", [128, H, 128], BF16)
    qdT = cv("P1", "qdT", [128, H, 128], BF16)
    stg = [cv("X", "stg0", [128, H, 256], F32)] * max(G - 1, 1)
    PiT = cv("X", "PiT", [128, H, 128], F32)
    Sst = cv("X", "Sst", [128, H, 128], F32)
    cand = cv("X", "cand", [128, H, 128], F32)
    otrue = cv("P2", "otrue", [128, H, 128], F32)
    osq = cv("P2", "osq", [128, H, 128], BF16)
    ort = cv("P2", "ort", [128, H, 128], F32)
    ors = ort
    obT = cv("P2", "obT", [128, KT, TB], BF16)
    zs = cv("P2", "zs", [128, KT, TB], BF16)
    ug = cv("P2", "ug", [128, KT, TB], BF16)
    vg = cv("P2", "vg", [128, D], F32)
    nrm = cv("P2", "nrm", [128, D], BF16)
    st = {n: cv("P2", "st_" + n, [128, 1], F32) for n in ("s1", "s2", "mean", "msq", "var", "sd", "rstd")}
    mtmp = cv("P2", "mtmp", [128, KT, 128], F32)
    vsq = mtmp.rearrange("p a b -> p (a b)")
    oaT = cv("P2", "oaT", [128, KT, TB], BF16)
    gt = cv("P2", "gt", [128, 2 * KT, TB], BF16)
    mA = cv("P2", "mA", [128, KT, TB], BF16)
    mB = cv("P2", "mB", [128, TB], F32)
    mergedT = cv("P2", "mergedT", [128, KT, TB], BF16)
    wv = cv("P2", "wv", [128, KT, D], BF16)

    ps = [nc.alloc_psum_tensor(f"ps{i}", [128, 512], F32) for i in range(8)]

    def program(E):
        B = {}

        def chk(k):
            if c.STOP == k:
                raise Stop()

        def bf(name):
            if name not in B:
                B[name] = Buf(name)
            return B[name]

        psb = [Buf(f"ps{i}") for i in range(8)]
        pstate = [0]

        def psum():
            i = pstate[0] % 8
            pstate[0] += 1
            return ps[i], psb[i]

        def dma(q, out, in_, reads, writes, sem):
            E.op(q, lambda e, o=out, i=in_: e.dma_start(out=o, in_=i), reads, writes, dma=sem)

        def act(out, in_, func, reads, writes, bias=None, scale=None):
            kw = {}
            if bias is not None:
                kw["bias"] = bias
            if scale is not None:
                kw["scale"] = scale
            E.op("act", lambda e, o=out, i=in_, f=func, k=kw: e.activation(out=o, in_=i, func=f, **k), reads, writes)

        def tt(out, in0, in1, op, reads, writes, eng="dve"):
            E.op(eng, lambda e, o=out, a=in0, b=in1, p=op: e.tensor_tensor(out=o, in0=a, in1=b, op=p), reads, writes)

        def ts(out, in0, s1, s2, op0, op1, reads, writes, eng="dve"):
            if s2 is None:
                E.op(eng, lambda e, o=out, a=in0, x=s1, p=op0: e.tensor_scalar(out=o, in0=a, scalar1=x, scalar2=None, op0=p),
                     reads, writes)
            else:
                E.op(eng, lambda e, o=out, a=in0, x=s1, y=s2, p=op0, q=op1: e.tensor_scalar(
                    out=o, in0=a, scalar1=x, scalar2=y, op0=p, op1=q), reads, writes)

        def stt(out, in0, scalar, in1, op0, op1, reads, writes, eng="dve"):
            E.op(eng, lambda e, o=out, a=in0, s=scalar, b=in1, p=op0, q=op1: e.scalar_tensor_tensor(
                out=o, in0=a, scalar=s, in1=b, op0=p, op1=q), reads, writes)

        def cp(out, in_, reads, writes, eng="dve"):
            if eng == "act":
                E.op("act", lambda e, o=out, i=in_: e.activation(out=o, in_=i, func=AF.Copy), reads, writes)
            else:
                E.op(eng, lambda e, o=out, i=in_: e.tensor_copy(out=o, in_=i), reads, writes)

        def recip(out, in_, reads, writes):
            E.op("dve", lambda e, o=out, i=in_: e.reciprocal(out=o, in_=i), reads, writes)

        def mm(out, lhsT, rhs, start, stop, reads, writes):
            E.op("pe", lambda e, o=out, l=lhsT, r=rhs, s=start, p=stop: e.matmul(o, lhsT=l, rhs=r, start=s, stop=p),
                 reads, writes)

        def tr(out, in_, ident, reads, writes):
            E.op("pe", lambda e, o=out, i=in_, d=ident: e.transpose(o, i, d), reads, writes)

        def bc_mid(ap2d, n):
            return ap2d.unsqueeze(1).broadcast_to([128, n, ap2d.shape[1]])

        def bc_last(ap2d, n):
            return ap2d.unsqueeze(2).broadcast_to([128, ap2d.shape[1], n])

        wslot_b = [Buf(f"wslot{i}") for i in range(NWS)]

        def w_issue(upto):
            plan = E.wplan
            while E.wissued < min(upto, len(plan)):
                j = E.wissued
                s = j % NWS
                for (o_fn, src, dep) in plan[j]:
                    dma("pool", o_fn(wslot[s]), src, [bf(dep)] if dep else [], [wslot_b[s]], f"w{s}")
                E.wissued += 1

        def w_get(loads):
            j = E.wconsumed
            E.wconsumed += 1
            if E.plan_only:
                E.wreq.append(loads)
                return wslot[j % NWS], wslot_b[j % NWS]
            w_issue(j + NWS - 1)
            return wslot[j % NWS], wslot_b[j % NWS]

        def linear_fm(Wd, col0, nft, KTin, rhs, N, evac, CW=2):
            for _ in linear_fm_gen(Wd, col0, nft, KTin, rhs, N, evac, CW):
                pass

        def linear_fm_gen(Wd, col0, nft, KTin, rhs, N, evac, CW=2):
            nchunk = (nft + CW - 1) // CW
            for ci in range(nchunk):
                f0 = ci * CW
                nf = min(CW, nft - f0)
                ncols = nf * 128
                src = Wd[:, col0 + f0 * 128: col0 + f0 * 128 + ncols].rearrange("(kt p) n -> p kt n", p=128)
                nm = Wd.name
                dep = ("g_" + nm[:-5]) if nm.endswith("_full") else None
                slot, sbuf_ = w_get([(lambda s, k=KTin, n=ncols: s[:, 0:k * n].rearrange("p (k n) -> p k n", k=k), src, dep)])
                wv_ = slot[:, 0:KTin * ncols].rearrange("p (k n) -> p k n", k=KTin)
                for fi in range(nf):
                    p_t, p_b = psum()
                    for kt in range(KTin):
                        r_ap, r_bufs = rhs(kt)
                        mm(p_t[:, 0:N], wv_[:, kt, fi * 128:(fi + 1) * 128], r_ap, kt == 0, kt == KTin - 1,
                           [sbuf_] + r_bufs, [p_b])
                    evac(f0 + fi, p_t[:, 0:N], p_b)
                    yield

        allc = [list(range(NCs))]
        for (wn, wr, wc) in (WSPEC if c.WG else []):
            dma("sp", wbnc[wn].ap(), wext[wn], [], [bf("b_" + wn)], "wb_" + wn)
            E.op("pool", lambda e, a=wbnc[wn], o=wfull[wn]: e.collective_compute(
                "AllGather", ALU.bypass, replica_groups=allc, ins=[a.ap().opt()], outs=[o.ap().opt()]),
                [bf("b_" + wn)], [bf("g_" + wn)], dma="cc_" + wn)
        E.op("pool", lambda e: e.memset(identf[:], 0.0), [], [bf("identf")])
        E.op("pool", lambda e: e.affine_select(out=identf[:], in_=identf[:], pattern=[[-1, 128]], compare_op=ALU.not_equal,
                                               fill=1.0, base=0, channel_multiplier=1), [bf("identf")], [bf("identf")])
        E.op("pool", lambda e: e.tensor_copy(out=identb[:], in_=identf[:]), [bf("identf")], [bf("identb")])
        E.op("pool", lambda e: e.memset(onesb[:], 1.0), [], [bf("onesb")])
        E.op("pool", lambda e: e.memset(onesD[:], 1.0 / D), [], [bf("onesD")])
        E.op("pool", lambda e: e.memset(onesV[:], 1.0 / 128), [], [bf("onesV")])
        E.op("pool", lambda e: e.memset(onesf[:], 1.0), [], [bf("onesf")])
        E.op("pool", lambda e: e.memset(tri[:], 1.0), [], [bf("tri")])
        E.op("pool", lambda e: e.affine_select(out=tri[:], in_=tri[:], pattern=[[1, 128]], compare_op=ALU.is_ge,
                                               fill=0.0, base=0, channel_multiplier=-1), [bf("tri")], [bf("tri")])
        E.op("pool", lambda e: e.memset(sel127[:], 1.0), [], [bf("sel127")])
        E.op("pool", lambda e: e.affine_select(out=sel127[:], in_=sel127[:], pattern=[[0, 128]], compare_op=ALU.is_ge,
                                               fill=0.0, base=-127, channel_multiplier=1), [bf("sel127")], [bf("sel127")])
        E.op("pool", lambda e: e.memset(neg1[:], 0.0), [], [bf("neg1")])
        E.op("pool", lambda e: e.affine_select(out=neg1[:], in_=neg1[:], pattern=[[-1, 128]], compare_op=ALU.is_gt,
                                               fill=NEGBIG, base=0, channel_multiplier=1), [bf("neg1")], [bf("neg1")])
        E.op("pool", lambda e: e.memset(neg2[:], 0.0), [], [bf("neg2")])
        E.op("pool", lambda e: e.affine_select(out=neg2[:], in_=neg2[:], pattern=[[1, 128]], compare_op=ALU.is_ge,
                                               fill=NEGBIG, base=0, channel_multiplier=-1), [bf("neg2")], [bf("neg2")])
        E.op("pool", lambda e: e.memset(upad[:], 0.0), [], [bf(f"upad#{hb}") for hb in range(2 if H >= 8 else 1)])

        dma("sp", vecs[:], vecs_d, [], [bf("vecs")], "c_vecs")
        dma("sp", rowv[:], rowv_d, [], [bf("rowv")], "c_rowv")
        dma("sp", bc8[:], bc8_d, [], [bf("bc8")], "c_bc8")
        dma("sp", masks[:], masks_d, [], [bf("masks")], "c_masks")
        dma("sp", bmask[:], bmask_d, [], [bf("bmask")], "c_bmask")
        dma("sp", cTf[:], cT_d, [], [bf("cTf")], "c_cT")
        dma("sp", wsTf[:], wsT_d.rearrange("g s t -> s g t"), [], [bf("wsTf")], "c_wsT")
        dma("pool", wab[:], win_d[:, c.OFF_AB:c.OFF_AB + 2 * H].rearrange("(kt p) n -> p kt n", p=128), [bf("g_w_in")], [bf("wab")],
            "c_wab")

        act(ealog[:], bc8[:, 0:H], AF.Exp, [bf("bc8")], [bf("ealog")])
        act(cact[:], cTf[:], AF.Silu, [bf("cTf")], [bf("cact")])

        chk(1)
        def mod_evac(ft, p_ap, p_b):
            tt(modT[:, ft:ft + 1], p_ap, vecs[:, c.V_BADA + ft:c.V_BADA + ft + 1], ALU.add, [p_b, bf("vecs")], [bf("modT")])

        linear_fm(w_ada_d, 0, 9 * KT, KT, lambda kt: (cact[:, kt:kt + 1], [bf("cact")]), 1, mod_evac)
        for i, (vn, half) in enumerate(((c.V_N1, 0.5), (c.V_N2, 1.0), (c.V_N3, 0.5))):
            stt(gsc[:, i * KT:(i + 1) * KT], modT[:, (3 * i + 1) * KT:(3 * i + 2) * KT], 1.0, vecs[:, vn:vn + KT],
                ALU.add, ALU.mult, [bf("modT"), bf("vecs")], [bf("gsc")])
            ts(hga[:, i * KT:(i + 1) * KT], modT[:, (3 * i + 2) * KT:(3 * i + 3) * KT], half, None, ALU.mult, None,
               [bf("modT")], [bf("hga")])

        tt(WmT[:], wsTf[:], bc_mid(tri[:], KT), ALU.mult, [bf("wsTf"), bf("tri")], [bf("WmT")])
        WmT_flat = WmT[:].rearrange("p g t -> p (g t)")
        for o0 in range(0, KT * 128, 512):
            n = min(512, KT * 128 - o0)
            p_t, p_b = psum()
            mm(p_t[0:1, 0:n], onesb[:, 0:1], WmT_flat[:, o0:o0 + n], True, True, [bf("onesb"), bf("WmT")], [p_b])
            cp(rsrow[0:1, o0:o0 + n], p_t[0:1, 0:n], [p_b], [bf("rsrow")])
        for g in range(KT):
            p_t, p_b = psum()
            mm(p_t[:, 0:128], rowv[0:1, g * 128:(g + 1) * 128], rsrow[0:1, g * 128:(g + 1) * 128], True, False,
               [bf("rowv"), bf("rsrow")], [p_b])
            mm(p_t[:, 0:128], onesf[0:1, 0:128], rowv[0:1, D + g * 128:D + (g + 1) * 128], False, True,
               [bf("rowv"), bf("onesf")], [p_b])
            cp(BiasG[:, g, :], p_t[:, 0:128], [p_b], [bf("BiasG")])

        chk(2)
        E.barrier()

        def rms_stats(xin, xb, N, ones_t, ones_b, out_rstd=rstd, kts=None):
            kts = range(KT) if kts is None else kts
            p_t, p_b = psum()
            kl = list(kts)
            for i, kt in enumerate(kl):
                s = sq[i % 2]
                act(s[:, 0:N], xin(kt), AF.Square, xb, [bf(f"sq{i % 2}")])
                mm(p_t[:, 0:N], ones_t[:], s[:, 0:N], i == 0, i == len(kl) - 1, [bf(f"sq{i % 2}"), ones_b], [p_b])
            act(rt[:, 0:N], p_t[:, 0:N], AF.Ln, [p_b], [bf("rt")], bias=EPS)
            act(out_rstd[:, 0:N], rt[:, 0:N], AF.Exp, [bf("rt")], [bf("rstd")], scale=-0.5)

        def norm_mod(xcur, xb, N, which, dst, dstb):
            rms_stats(lambda kt: xcur[:, kt, 0:N], xb, N, onesD, bf("onesD"))
            for kt in range(KT):
                tmp = ntmp[kt % 2]
                stt(tmp[:, 0:N], xcur[:, kt, 0:N], gsc[:, which * KT + kt:which * KT + kt + 1], rstd[:, 0:N], ALU.mult,
                    ALU.mult, xb + [bf("gsc"), bf("rstd")], [bf(f"ntmp{kt % 2}")])
                shc = (3 * which) * KT + kt
                act(dst[:, kt, 0:N], tmp[:, 0:N], AF.Identity, [bf(f"ntmp{kt % 2}"), bf("modT")], dstb,
                    bias=modT[:, shc:shc + 1])

        def ffn(xcur, xb, N, which, win_d_, wout_d_):
            for _ in ffn_gen(xcur, xb, N, which, win_d_, wout_d_):
                pass

        def ffn_gen(xcur, xb, N, which, win_d_, wout_d_):
            norm_mod(xcur, xb, N, which, hT, [bf("hT")])
            yield

            for j0 in range(0, FT, 2):
                nj = min(2, FT - j0)

                base = (j0 // 2 % 2) * 2

                def ev_a2(ft, p_ap, p_b, base=base):
                    act(sa[base + ft][:, 0:N], p_ap, AF.Silu, [p_b], [bf(f"sa{base + ft}")])

                def ev_b2(ft, p_ap, p_b, base=base, j0=j0):
                    tt(gT[:, j0 + ft, 0:N], sa[base + ft][:, 0:N], p_ap, ALU.mult, [bf(f"sa{base + ft}"), p_b], [bf("gT")])

                yield from linear_fm_gen(win_d_, j0 * 128, nj, KT, lambda kt: (hT[:, kt, 0:N], [bf("hT")]), N, ev_a2)
                yield from linear_fm_gen(win_d_, DFF + j0 * 128, nj, KT, lambda kt: (hT[:, kt, 0:N], [bf("hT")]), N, ev_b2)

            def ev_out(ft, p_ap, p_b):
                stt(xcur[:, ft, 0:N], p_ap, hga[:, which * KT + ft:which * KT + ft + 1], xcur[:, ft, 0:N], ALU.mult, ALU.add,
                    [p_b, bf("hga")] + xb, xb)

            yield from linear_fm_gen(wout_d_, 0, KT, FT, lambda kt: (gT[:, kt, 0:N], [bf("gT")]), N, ev_out, CW=1)

        _sfb = [bf(f"Sf#{hb}") for hb in range(2 if H >= 8 else 1)]
        _sbb = [bf(f"Sb#{hb}") for hb in range(2 if H >= 8 else 1)]
        E.op("dve", lambda e: e.memset(Sf[:], 0.0), [], _sfb)
        if not FUSED:
            cp(Sf[:, :, 128:256], bc_mid(identf[:], H), [bf("identf")] + _sfb, _sfb)
        cp(Sb[:], Sf[:], _sfb, _sbb, eng="act")

        def qkv_conv(N, xb_h2, is_halo, vm=None, own=True, q_tails_only=False):
            ft_off = 0 if own else KT
            pending = []

            def flush():
                bufsets = [(rt, "rt", rstd, "rstd"), (ntmp[0], "ntmp0", ntmp[1], "ntmp1")]
                banks = []
                for i_, (ft_, h_, isq_, qr_, qrb_) in enumerate(pending):
                    p2, p2b = psum()
                    s_ = sq[i_]
                    act(s_[:, 0:N], qr_[:, 0:N], AF.Square, [qrb_], [bf(f"sq{i_}")])
                    mm(p2[:, 0:N], onesb[:], s_[:, 0:N], True, True, [bf(f"sq{i_}"), bf("onesb")], [p2b])
                    banks.append((p2, p2b))
                for i_ in range(len(pending)):
                    a_, an_, b_, bn_ = bufsets[i_]
                    act(a_[:, 0:N], banks[i_][0][:, 0:N], AF.Ln, [banks[i_][1]], [bf(an_)], bias=EPS)
                for i_ in range(len(pending)):
                    a_, an_, b_, bn_ = bufsets[i_]
                    act(b_[:, 0:N], a_[:, 0:N], AF.Exp, [bf(an_)], [bf(bn_)], scale=-0.5)
                for i_, (ft_, h_, isq_, qr_, qrb_) in enumerate(pending):
                    a_, an_, b_, bn_ = bufsets[i_]
                    dst = qnT if isq_ else knT
                    stt(dst[:, h_, 0:N], qr_[:, 0:N], (128.0 ** -0.5) if isq_ else 1.0, b_[:, 0:N], ALU.mult, ALU.mult,
                        [qrb_, bf(bn_)], [bf("qnT" if isq_ else "knT")])
                pending.clear()

            def ev_pre(ft, p_ap, p_b):
                ft = ft + ft_off
                if is_halo:
                    ts(tails[:, ft, :], p_ap, masks[:, 0:1], None, ALU.mult, None, [p_b, bf("masks")], [bf("tails")])
                    return
                slot = pre[ft % 3]
                slb = bf(f"pre{ft % 3}")
                cp(slot[:, 0:4], tails[:, ft, :], [bf("tails")], [slb], eng="act")
                if vm is not None:
                    ts(slot[:, 4:4 + N], p_ap, vm, None, ALU.mult, None, [p_b, bf("masks")], [slb])
                else:
                    cp(slot[:, 4:4 + N], p_ap, [p_b], [slb], eng=("act" if ft % 2 else "dve"))
                cp(tails[:, ft, :], slot[:, N:N + 4], [slb], [bf("tails")])
                if q_tails_only and ft < KT:
                    return
                ds = dslot[ft % 2]
                dsb = bf(f"dslot{ft % 2}")
                for j in range(4):
                    col = c.V_CONV + j * 3 * KT + ft
                    act(ds[:, j, :], identb[:], AF.Copy, [bf("identb"), bf("vecs")], [dsb], scale=vecs[:, col:col + 1])
                p_t, p_b2 = psum()
                for j in range(4):
                    mm(p_t[:, 0:N], ds[:, j, :], slot[:, 1 + j:1 + j + N], j == 0, j == 3, [dsb, slb], [p_b2])
                if ft < 2 * KT:
                    h = ft % KT
                    isq = ft < KT
                    qr = qraw[ft % 2]
                    qrb = bf(f"qraw{ft % 2}")
                    act(qr[:, 0:N], p_t[:, 0:N], AF.Silu, [p_b2], [qrb])
                    pending.append((ft, h, isq, qr, qrb))
                    if len(pending) == 2:
                        flush()
                else:
                    act(vT[:, ft - 2 * KT, 0:N], p_t[:, 0:N], AF.Silu, [p_b2], [bf("vT")])

            linear_fm(win_d, c.OFF_Q + ft_off * 128, 3 * KT - ft_off, KT, lambda kt: (h2[:, kt, 0:N], xb_h2), N, ev_pre)
            if pending:
                flush()

        HB = 4 if H >= 4 else H
        NHB = H // HB

        FILL = [None]

        def fill(k=1):
            g_ = FILL[0]
            if g_ is None:
                return
            for _ in range(k):
                try:
                    next(g_)
                except StopIteration:
                    FILL[0] = None
                    return

        def ab_proj(c0, ci):
            csl = slice(c0, c0 + 128)
            p_t, p_b = psum()
            for kt in range(KT):
                mm(p_t[:, 0:2 * H], h2[:, kt, csl], wab[:, kt, :], kt == 0, kt == KT - 1, [bf("h2"), bf("wab")], [p_b])
            cp(abT[:, ci, :], p_t[:, 0:2 * H], [p_b], [bf("abT")])

        def small_chain_tb(vm):
            Z = lambda n: smT[n][:]
            zB = lambda n: bf("smT_" + n)
            abv = abT[:]
            a_ap = abv[:, :, 0:H]
            b_ap = abv[:, :, H:2 * H]
            v3 = lambda n: smT[n][:].rearrange("p (c h) -> p c h", c=CPB)
            act(v3("e"), b_ap, AF.Exp, [bf("abT")], [zB("e")], scale=-1.0)
            ts(Z("e"), Z("e"), 1.0, None, ALU.add, None, [zB("e")], [zB("e")])
            recip(Z("beta"), Z("e"), [zB("e")], [zB("beta")])
            if vm is not None:
                ts(Z("beta"), Z("beta"), vm, None, ALU.mult, None, [zB("beta"), bf("masks")], [zB("beta")])
            tt(v3("x"), a_ap, bc_mid(bc8[:, H:2 * H], CPB), ALU.add, [bf("abT"), bf("bc8")], [zB("x")])
            act(Z("ex"), Z("x"), AF.Exp, [zB("x")], [zB("ex")])
            act(Z("sp"), Z("ex"), AF.Ln, [zB("ex")], [zB("sp")], bias=1.0)
            stt(v3("g"), v3("sp"), -1.0, bc_mid(ealog[:], CPB), ALU.mult, ALU.mult, [zB("sp"), bf("ealog")], [zB("g")])
            p_t, p_b = psum()
            mm(p_t[:, 0:CPB * H], tri[:], Z("g"), True, True, [bf("tri"), zB("g")], [p_b])
            cp(Z("CB"), p_t[:, 0:CPB * H], [p_b], [zB("CB")])
            act(Z("eCB"), Z("CB"), AF.Exp, [zB("CB")], [zB("eCB")])
            tt(Z("kbs"), Z("eCB"), Z("beta"), ALU.mult, [zB("eCB"), zB("beta")], [zB("kbs")])
            p_t, p_b = psum()
            mm(p_t[:, 0:CPB * H], sel127[:], Z("CB"), True, True, [bf("sel127"), zB("CB")], [p_b])
            tt(Z("dka"), p_t[:, 0:CPB * H], Z("CB"), ALU.subtract, [p_b, zB("CB")], [zB("dka")])
            act(Z("gle"), p_t[:, 0:CPB * H], AF.Exp, [p_b], [zB("gle")])
            act(Z("dke"), Z("dka"), AF.Exp, [zB("dka")], [zB("dke")])

        def chunk_solve(c0, gci, own=True, vm=None, ci=None):
            csl = slice(c0, c0 + 128)
            assert ci is not None
            sm = {n: smT[n][:, ci * H:(ci + 1) * H] for n in smT}
            S = lambda n: sm[n]
            sB = lambda n: bf("smT_" + n)
            fill()

            HS = [slice(hb * HB, (hb + 1) * HB) for hb in range(NHB)]
            hbf = lambda name, hb: bf(f"{name}#{hb}")
            f2 = lambda t, hb: t[:, HS[hb], :].rearrange("p h f -> p (h f)")
            W_ = HB * 128

            def each(fn):
                for hb in range(NHB):
                    fn(hb)
                fill()

            def s_rb(hb):
                hs = HS[hb]
                tt(dg[:, hs, :], bc_mid(identf[:], HB), bc_last(sm["CB"][:, hs], 128), ALU.mult, [bf("identf"), sB("CB")], [hbf("tw", hb)])
                p_t, p_b = psum()
                mm(p_t[:, 0:W_], onesf[:], f2(dg, hb), True, True, [bf("onesf"), hbf("tw", hb)], [p_b])
                cp(f2(RB, hb), p_t[:, 0:W_], [p_b], [hbf("RB", hb)], eng="act")
            each(s_rb)

            def s_e1(hb):
                hs = HS[hb]
                tt(t1[:, hs, :], bc_mid(neg1[:], HB), RB[:, hs, :], ALU.subtract, [bf("neg1"), hbf("RB", hb)], [hbf("tw", hb)])
                tt(t1[:, hs, :], t1[:, hs, :], bc_last(sm["CB"][:, hs], 128), ALU.add, [hbf("tw", hb), sB("CB")], [hbf("tw", hb)])
                act(E1[:, hs, :], t1[:, hs, :], AF.Exp, [hbf("tw", hb)], [hbf("E1", hb)])
                tt(E1[:, hs, :], E1[:, hs, :], bc_last(sm["beta"][:, hs], 128), ALU.mult, [hbf("E1", hb), sB("beta")], [hbf("E1", hb)])
            each(s_e1)

            def s_gram(hb):
                pG, pGb = psum()
                for hh in range(HB):
                    h = hb * HB + hh
                    mm(pG[:, hh * 128:(hh + 1) * 128], knT[:, h, csl], knT[:, h, csl], True, True, [bf("knT")], [pGb])
                tt(f2(Afull, hb), pG[:, 0:W_], f2(E1, hb), ALU.mult, [pGb, hbf("E1", hb)], [hbf("Afull", hb)])
            each(s_gram)
            if own:
                def s_e2(hb):
                    hs = HS[hb]
                    tt(t2[:, hs, :], RB[:, hs, :], bc_mid(neg2[:], HB), ALU.add, [bf("neg2"), hbf("RB", hb)], [hbf("tw", hb)])
                    tt(t2[:, hs, :], t2[:, hs, :], bc_last(sm["CB"][:, hs], 128), ALU.subtract, [hbf("tw", hb), sB("CB")], [hbf("tw", hb)])
                    act(E2[:, hs, :], t2[:, hs, :], AF.Exp, [hbf("tw", hb)], [hbf("E2", hb)])
                    act(Eg[:, hs, :], RB[:, hs, :], AF.Exp, [hbf("RB", hb)], [hbf("Eg", hb)])
                    pQ, pQb = psum()
                    for hh in range(HB):
                        h = hb * HB + hh
                        mm(pQ[:, hh * 128:(hh + 1) * 128], knT[:, h, csl], qnT[:, h, csl], True, True, [bf("knT"), bf("qnT")], [pQb])
                    tt(f2(qkT, hb), pQ[:, 0:W_], f2(E2, hb), ALU.mult, [pQb, hbf("E2", hb)], [hbf("qkT", hb)])
                    tt(qdT[:, hs, :], qnT[:, hs, csl], Eg[:, hs, :], ALU.mult, [bf("qnT"), hbf("Eg", hb)], [hbf("qdT", hb)])
                each(s_e2)

            mk = lambda i: bc_mid(bmask[:, i, :], HB)

            def transp(src, srcb_fn, hb):
                pT, pTb = psum()
                pTv = pT[:].bitcast(BF16)
                for hh in range(HB):
                    h = hb * HB + hh
                    tr(pTv[:, hh * 128:(hh + 1) * 128], src(h), identb[:], [srcb_fn(hb), bf("identb")], [pTb])
                return pTv, pTb

            def s_at(hb):
                hs = HS[hb]
                tt(Nb[0][:, hs, :], Afull[:, hs, :], mk(0), ALU.mult, [hbf("Afull", hb), bf("bmask")], [hbf("Nb0", hb)])
                tt(As1[:, hs, :], Afull[:, hs, :], mk(1), ALU.mult, [hbf("Afull", hb), bf("bmask")], [hbf("As", hb)])
                pTv, pTb = transp(lambda h: Afull[:, h, :], lambda hb_: hbf("Afull", hb_), hb)
                cp(f2(ATfull, hb), pTv[:, 0:W_], [pTb], [hbf("ATfull", hb)], eng="act")
                tt(NTb[0][:, hs, :], ATfull[:, hs, :], mk(0), ALU.mult, [hbf("ATfull", hb), bf("bmask")], [hbf("NTb0", hb)])
                tt(Ub[0][:, hs, :], bc_mid(identb[:], HB), NTb[0][:, hs, :], ALU.subtract, [hbf("NTb0", hb), bf("identb")], [hbf("Ub0", hb)])
                tt(Ms1[:, hs, :], ATfull[:, hs, :], mk(1), ALU.mult, [hbf("ATfull", hb), bf("bmask")], [hbf("Ms", hb)])
            each(s_at)
            cur = 0
            NLEV = 3
            for k in range(1, NLEV + 1):
                nxt = 1 - cur

                def s_lev(hb, cur=cur, nxt=nxt, k=k):
                    pN, pNb = psum()
                    for hh in range(HB):
                        h = hb * HB + hh
                        mm(pN[:, hh * 128:(hh + 1) * 128], NTb[cur][:, h, :], Nb[cur][:, h, :], True, True,
                           [hbf(f"NTb{cur}", hb), hbf(f"Nb{cur}", hb)], [pNb])
                    cp(f2(Nb[nxt], hb), pN[:, 0:W_], [pNb], [hbf(f"Nb{nxt}", hb)], eng="act")
                    if k < NLEV:
                        pM, pMb = psum()
                        for hh in range(HB):
                            h = hb * HB + hh
                            mm(pM[:, hh * 128:(hh + 1) * 128], Nb[cur][:, h, :], NTb[cur][:, h, :], True, True,
                               [hbf(f"NTb{cur}", hb), hbf(f"Nb{cur}", hb)], [pMb])
                        cp(f2(NTb[nxt], hb), pM[:, 0:W_], [pMb], [hbf(f"NTb{nxt}", hb)])
                    pU, pUb = psum()
                    for hh in range(HB):
                        h = hb * HB + hh
                        mm(pU[:, hh * 128:(hh + 1) * 128], Nb[nxt][:, h, :], Ub[cur][:, h, :], True, True,
                           [hbf(f"Nb{nxt}", hb), hbf(f"Ub{cur}", hb)], [pUb])
                    tt(f2(Ub[nxt], hb), f2(Ub[cur], hb), pU[:, 0:W_], ALU.add, [pUb, hbf(f"Ub{cur}", hb)], [hbf(f"Ub{nxt}", hb)])
                each(s_lev)
                cur = nxt

            def s_td(hb, cur=cur):
                pTv, pTb = transp(lambda h: Ub[cur][:, h, :], lambda hb_: hbf(f"Ub{cur}", hb_), hb)
                cp(f2(Tb[0], hb), pTv[:, 0:W_], [pTb], [hbf("Tb0", hb)], eng="act")
            each(s_td)
            tcur = 0
            for i in range(3):
                nxt = 1 - cur
                tnx = 1 - tcur
                lastm = (i == 2)

                def s_mrg(hb, i=i, cur=cur, nxt=nxt, tcur=tcur, tnx=tnx, lastm=lastm):
                    hs = HS[hb]
                    pW, pWb = psum()
                    for hh in range(HB):
                        h = hb * HB + hh
                        mm(pW[:, hh * 128:(hh + 1) * 128], As1[:, h, :], Ub[cur][:, h, :], True, True,
                           [hbf("As", hb), hbf(f"Ub{cur}", hb)], [pWb])
                    cp(f2(Wb, hb), pW[:, 0:W_], [pWb], [hbf("Wb", hb)], eng="act")
                    if not lastm:
                        pV, pVb = psum()
                        for hh in range(HB):
                            h = hb * HB + hh
                            mm(pV[:, hh * 128:(hh + 1) * 128], Ms1[:, h, :], Tb[tcur][:, h, :], True, True,
                               [hbf("Ms", hb), hbf(f"Tb{tcur}", hb)], [pVb])
                        cp(f2(Vb, hb), pV[:, 0:W_], [pVb], [hbf("Vb", hb)])
                    if i < 2:
                        tt(As1[:, hs, :], Afull[:, hs, :], mk(2 + i), ALU.mult, [hbf("Afull", hb), bf("bmask")], [hbf("As", hb)])
                    if i < 1:
                        tt(Ms1[:, hs, :], ATfull[:, hs, :], mk(2 + i), ALU.mult, [hbf("ATfull", hb), bf("bmask")], [hbf("Ms", hb)])
                    pU, pUb = psum()
                    for hh in range(HB):
                        h = hb * HB + hh
                        mm(pU[:, hh * 128:(hh + 1) * 128], Tb[tcur][:, h, :], Wb[:, h, :], True, True,
                           [hbf(f"Tb{tcur}", hb), hbf("Wb", hb)], [pUb])
                    tt(f2(Ub[nxt], hb), f2(Ub[cur], hb), pU[:, 0:W_], ALU.subtract, [pUb, hbf(f"Ub{cur}", hb)], [hbf(f"Ub{nxt}", hb)])
                    if not lastm:
                        pX, pXb = psum()
                        for hh in range(HB):
                            h = hb * HB + hh
                            mm(pX[:, hh * 128:(hh + 1) * 128], Ub[cur][:, h, :], Vb[:, h, :], True, True,
                               [hbf(f"Ub{cur}", hb), hbf("Vb", hb)], [pXb])
                        tt(f2(Tb[tnx], hb), f2(Tb[tcur], hb), pX[:, 0:W_], ALU.subtract, [pXb, hbf(f"Tb{tcur}", hb)], [hbf(f"Tb{tnx}", hb)])
                each(s_mrg)
                cur = nxt
                tcur = tnx
            U = Ub[cur]
            Ubn = f"Ub{cur}"

            def s_kv(hb):
                hs = HS[hb]
                pTv, pTb = transp(lambda h: knT[:, h, csl], lambda hb_: bf("knT"), hb)
                pT3 = pTv[:, 0:W_].rearrange("p (h f) -> p h f", h=HB)
                tt(kd[:, hs, :], pT3, bc_last(sm["dke"][:, hs], 128), ALU.mult, [pTb, sB("dke")], [hbf("kd", hb)])
                tt(kbg[:, hs, :], pT3, bc_last(sm["kbs"][:, hs], 128), ALU.mult, [pTb, sB("kbs")], [hbf("kbg", hb)])
                pTv, pTb = transp(lambda h: vT[:, h, csl], lambda hb_: bf("vT"), hb)
                pT3 = pTv[:, 0:W_].rearrange("p (h f) -> p h f", h=HB)
                tt(vb[:, hs, :], pT3, bc_last(sm["beta"][:, hs], 128), ALU.mult, [pTb, sB("beta")], [hbf("vb", hb)])
            each(s_kv)

            def s_uw(hb):
                hs = HS[hb]
                pu, pub = psum()
                pw, pwb = psum()
                for hh in range(HB):
                    h = hb * HB + hh
                    mm(pu[:, hh * 128:(hh + 1) * 128], U[:, h, :], vb[:, h, :], True, True, [hbf(Ubn, hb), hbf("vb", hb)], [pub])
                    mm(pw[:, hh * 128:(hh + 1) * 128], kbg[:, h, :], U[:, h, :], True, True, [hbf(Ubn, hb), hbf("kbg", hb)], [pwb])
                cp(upad[:, hs, 0:128], pu[:, 0:W_].rearrange("p (h f) -> p h f", h=HB), [pub], [hbf("upad", hb)], eng="act")
                cp(f2(wT, hb), pw[:, 0:W_], [pwb], [hbf("wT", hb)])
            each(s_uw)

            so = ostage[gci % 2]
            sr = rstage[gci % 2]
            sob, srb = bf(f"ostage{gci % 2}"), bf(f"rstage{gci % 2}")

            def s_vn(hb):
                for h2i in range(hb * HB, (hb + 1) * HB, 2):
                    p1, p1b = psum()
                    for hh in range(2):
                        h = h2i + hh
                        mm(p1[:, hh * SW:(hh + 1) * SW], wT[:, h, :], Sb[:, h, :], True, True, [hbf("wT", hb), hbf("Sb", hb)], [p1b])
                    tt(vnew[:, h2i:h2i + 2, :].rearrange("p h f -> p (h f)"), upad[:, h2i:h2i + 2, :].rearrange("p h f -> p (h f)"),
                       p1[:, 0:2 * SW], ALU.subtract, [p1b, hbf("upad", hb)], [hbf("vnew", hb)])
            each(s_vn)
            if own:
                def s_o(hb):
                    hs = HS[hb]
                    po, pob = psum()
                    for hh in range(HB):
                        h = hb * HB + hh
                        mm(po[:, hh * 128:(hh + 1) * 128], Sb[:, h, 0:128], qdT[:, h, :], True, False, [hbf("Sb", hb), hbf("qdT", hb)], [pob])
                        mm(po[:, hh * 128:(hh + 1) * 128], vnew[:, h, 0:128], qkT[:, h, :], False, True, [hbf("vnew", hb), hbf("qkT", hb)], [pob])
                    if FUSED:
                        cp(oTtb[:, hs, csl], po[:, 0:W_].rearrange("p (h f) -> p h f", h=HB), [pob], [bf("oTtb")], eng="act")
                        return
                    pr, prb = psum()
                    for hh in range(HB):
                        h = hb * HB + hh
                        mm(pr[:, hh * 128:(hh + 1) * 128], Sb[:, h, 128:256], qdT[:, h, :], True, False, [hbf("Sb", hb), hbf("qdT", hb)], [prb])
                        mm(pr[:, hh * 128:(hh + 1) * 128], vnew[:, h, 128:256], qkT[:, h, :], False, True, [hbf("vnew", hb), hbf("qkT", hb)], [prb])
                    cp(f2(so, hb), po[:, 0:W_], [pob], [sob], eng="act")
                    cp(f2(sr, hb), pr[:, 0:W_], [prb], [srb], eng="act")
                each(s_o)

            def s_st(hb):
                hs = HS[hb]
                for h2i in range(hb * HB, (hb + 1) * HB, 2):
                    p3, p3b = psum()
                    for hh in range(2):
                        h = h2i + hh
                        mm(p3[:, hh * SW:(hh + 1) * SW], kd[:, h, :], vnew[:, h, :], True, True, [hbf("kd", hb), hbf("vnew", hb)], [p3b])
                    for hh in range(2):
                        h = h2i + hh
                        stt(Sf[:, h, :], Sf[:, h, :], sm["gle"][:, h:h + 1], p3[:, hh * SW:(hh + 1) * SW], ALU.mult, ALU.add,
                            [p3b, hbf("Sf", hb), sB("gle")], [hbf("Sf", hb)])
                cp(Sb[:, hs, :], Sf[:, hs, :], [hbf("Sf", hb)], [hbf("Sb", hb)], eng="act")
            each(s_st)
            if not FUSED:
                dma("sp", os_d[gci], so[:].rearrange("p h f -> p (h f)"), [sob], [bf(f"os{gci}")], f"ost{gci % 2}")
                dma("sp", rs_d[gci], sr[:].rearrange("p h f -> p (h f)"), [srb], [bf(f"rs{gci}")], f"rst{gci % 2}")

        blocks = [(0, 4, True)] + [(4 + i * TB, TB, False) for i in range(NB)]
        for bi, (col0, N, is_halo) in enumerate(blocks if part in (0, 1) else []):
            xi = bi % 2
            xcur = xt[xi]
            xb = [bf("xt0")]
            dma("sp", xcur[:, :, 0:N], xT_d[:, :, col0:col0 + N], [], xb, f"xld{xi}")
            ffn(xcur, xb, N, 0, f1in_d, f1out_d)
            norm_mod(xcur, xb, N, 1, h2, [bf("h2")])
            if not is_halo:
                t0 = col0 - 4
                dma("sp", x1s_d[:, :, t0:t0 + N], xcur[:, :, 0:N], xb, [bf(f"x1s{bi}")], f"x1st{xi}")
                dma("sp", h2s_d[:, :, t0:t0 + N], h2[:, :, 0:N], [bf("h2")], [bf(f"h2s{bi}")], "h2st")
            E.barrier()
            qkv_conv(N, [bf("h2")], is_halo)
            if is_halo:
                chk(4)
            else:
                chk(5)
            if not is_halo:
                for ci in range(CPB):
                    chunk_solve(ci * 128, (bi - 1) * CPB + ci)
                    chk(6)
            E.barrier()

        chk(7)
        if part in (0, 1):
            dma("sp", st_in_t.ap(), Sf[:].rearrange("p h f -> p (h f)"), [bf("Sf")], [bf("st_in")], "stio")
        if part == 1:
            raise Stop()
        if G > 1 and part == 0:
            groups = [list(range(g0, g0 + G)) for g0 in range(0, c.NCORES, G)]
            E.op("pool", lambda e: e.collective_compute("AllGather", ALU.bypass, replica_groups=groups,
                                                        ins=[st_in_t.ap().opt()], outs=[st_all_t.ap().opt()]),
                 [bf("st_in")], [bf("st_all")], dma="ccsem")
        if not FUSED:
            E.op("dve", lambda e: e.memset(Sst[:], 0.0), [], [bf("Sst")])
        for i in (range(G - 1) if not FUSED else []):
            dma("sp", stg[i][:].rearrange("p h f -> p (h f)"), st_all_t.ap()[i * 128:(i + 1) * 128, :], [bf("st_all")],
                [bf("stg0")], f"stgl{i}")
            for hb in range(NHB):
                pT, pTb = psum()
                for hh in range(HB):
                    h = hb * HB + hh
                    tr(pT[:, hh * 128:(hh + 1) * 128], stg[i][:, h, 128:256], identf[:], [bf("stg0"), bf("identf")], [pTb])
                cp(PiT[:, hb * HB:(hb + 1) * HB, :].rearrange("p h f -> p (h f)"), pT[:, 0:HB * 128], [pTb], [bf("PiT")])
            for hb in range(NHB):
                hs = slice(hb * HB, (hb + 1) * HB)
                pc, pcb = psum()
                for hh in range(HB):
                    h = hb * HB + hh
                    mm(pc[:, hh * 128:(hh + 1) * 128], PiT[:, h, :], Sst[:, h, :], True, True, [bf("PiT"), bf("Sst")], [pcb])
                tt(cand[:, hs, :], pc[:, 0:HB * 128].rearrange("p (h f) -> p h f", h=HB), stg[i][:, hs, 0:128], ALU.add,
                   [pcb, bf("stg0")], [bf("cand")])
            tt(cand[:], cand[:], Sst[:], ALU.subtract, [bf("cand"), bf("Sst")], [bf("cand")])
            stt(Sst[:], cand[:], masks[:, 1 + i:2 + i], Sst[:], ALU.mult, ALU.add, [bf("cand"), bf("masks"), bf("Sst")],
                [bf("Sst")])
        if not FUSED:
            cp(Sstb[:], Sst[:], [bf("Sst")], [bf("Sstb")])
        E.barrier()

        chk(8)
        def phase2_tb(tbi):
            t0 = tbi * TB
            N = TB
            xi = tbi % 2
            xcur = xt[xi]
            xb = [bf("xt0")]
            if not FUSED:
                dma("sp", xcur[:], x1s_d[:, :, t0:t0 + N], [bf(f"x1s{tbi + 1}")], xb, f"xld{xi}")
                dma("sp", h2[:], h2s_d[:, :, t0:t0 + N], [bf(f"h2s{tbi + 1}")], [bf("h2")], "h2ld")
            dma("pool", wv[:], win_d[:, c.OFF_V:c.OFF_V + D].rearrange("(kt p) n -> p kt n", p=128), [bf("g_w_in")], [bf("wv")], "wvld")
            h2r = lambda kt: (h2[:, kt, 0:N], [bf("h2")])

            linear_fm(win_d, c.OFF_Z, KT, KT, h2r, N,
                      lambda ft, p_ap, p_b: act(zs[:, ft, :], p_ap, AF.Silu, [p_b], [bf("zs")]))
            linear_fm(win_d, c.OFF_U, KT, KT, h2r, N,
                      lambda ft, p_ap, p_b: act(ug[:, ft, :], p_ap, AF.Gelu, [p_b], [bf("ug")]))
            def o_part(ci):
                gci = tbi * CPB + ci
                csl = slice(ci * 128, (ci + 1) * 128)
                so, sr = ostage[gci % 2], rstage[gci % 2]
                sob, srb = bf(f"ostage{gci % 2}"), bf(f"rstage{gci % 2}")
                if FUSED:
                    cp(otrue[:], oTtb[:, :, csl], [bf("oTtb")], [bf("otrue")])
                else:
                    dma("sp", so[:].rearrange("p h f -> p (h f)"), os_d[gci], [bf(f"os{gci}")], [sob], f"old{gci % 2}")
                    dma("sp", sr[:].rearrange("p h f -> p (h f)"), rs_d[gci], [bf(f"rs{gci}")], [srb], f"rld{gci % 2}")
                for hb in (range(NHB) if not FUSED else []):
                    hs = slice(hb * HB, (hb + 1) * HB)
                    pc, pcb = psum()
                    for hh in range(HB):
                        h = hb * HB + hh
                        mm(pc[:, hh * 128:(hh + 1) * 128], Sstb[:, h, :], sr[:, h, :], True, True, [bf("Sstb"), srb], [pcb])
                    tt(otrue[:, hs, :], pc[:, 0:HB * 128].rearrange("p (h f) -> p h f", h=HB), so[:, hs, :], ALU.add,
                       [pcb, sob], [bf("otrue")])
                act(osq[:], otrue[:], AF.Square, [bf("otrue")], [bf("osq")])
                yield
                for hb in range(NHB):
                    hs = slice(hb * HB, (hb + 1) * HB)
                    pn, pnb = psum()
                    mm(pn[:, 0:HB * 128], onesV[:], osq[:, hs, :].rearrange("p h f -> p (h f)"), True, True,
                       [bf("onesV"), bf("osq")], [pnb])
                    act(ort[:, hs, :].rearrange("p h f -> p (h f)"), pn[:, 0:HB * 128], AF.Ln, [pnb], [bf("ort")], bias=EPS)
                yield
                act(ors[:], ort[:], AF.Exp, [bf("ort")], [bf("ort")], scale=-0.5)
                yield
                tt(otrue[:], otrue[:], ors[:], ALU.mult, [bf("otrue"), bf("ort")], [bf("otrue")])
                yield
                stt(obT[:, :, csl], otrue[:], vecs[:, c.V_DNG:c.V_DNG + 1], zs[:, :, csl], ALU.mult, ALU.mult,
                    [bf("otrue"), bf("vecs"), bf("zs")], [bf("obT")])
            def g_part(ci):
                csl = slice(ci * 128, (ci + 1) * 128)
                for o0 in range(0, D, 512):
                    n = min(512, D - o0)
                    p_t, p_b = psum()
                    for kt in range(KT):
                        mm(p_t[:, 0:n], h2[:, kt, csl], wv[:, kt, o0:o0 + n], kt == 0, kt == KT - 1, [bf("h2"), bf("wv")], [p_b])
                    act(vg[:, o0:o0 + n], p_t[:, 0:n], AF.Gelu, [p_b], [bf("vg")])
                    yield
                SS = lambda n: st[n][:]
                sBB = lambda n: bf("st_" + n)
                E.op("dve", lambda e: e.tensor_reduce(out=st["s1"][:], in_=vg[:], axis=AX.X, op=ALU.add), [bf("vg")], [sBB("s1")])
                tt(vsq[:], vg[:], vg[:], ALU.mult, [bf("vg")], [bf("mtmp")])
                yield
                E.op("dve", lambda e: e.tensor_reduce(out=st["s2"][:], in_=vsq[:], axis=AX.X, op=ALU.add), [bf("mtmp")], [sBB("s2")])
                ts(SS("mean"), SS("s1"), 1.0 / D, None, ALU.mult, None, [sBB("s1")], [sBB("mean")])
                yield
                tt(SS("msq"), SS("mean"), SS("mean"), ALU.mult, [sBB("mean")], [sBB("msq")])
                stt(SS("var"), SS("s2"), 1.0 / D, SS("msq"), ALU.mult, ALU.subtract, [sBB("s2"), sBB("msq")], [sBB("var")])
                yield
                act(SS("sd"), SS("var"), AF.Sqrt, [sBB("var")], [sBB("sd")], bias=EPS)
                recip(SS("rstd"), SS("sd"), [sBB("sd")], [sBB("rstd")])
                yield
                ts(nrm[:], vg[:], SS("mean"), SS("rstd"), ALU.subtract, ALU.mult, [bf("vg"), sBB("mean"), sBB("rstd")], [bf("nrm")])
                yield
                for gb in range(0, KT, 4):
                    ng = min(4, KT - gb)
                    pM, pMb = psum()
                    for gg in range(ng):
                        g = gb + gg
                        mm(pM[:, gg * 128:(gg + 1) * 128], nrm[:, g * 128:(g + 1) * 128], WmT[:, g, :], True, True,
                           [bf("nrm"), bf("WmT")], [pMb])
                    for gg in range(ng):
                        g = gb + gg
                        stt(mtmp[:, g, :], pM[:, gg * 128:(gg + 1) * 128], vecs[:, c.V_LNG + g:c.V_LNG + g + 1], BiasG[:, g, :],
                            ALU.mult, ALU.add, [pMb, bf("vecs"), bf("BiasG")], [bf("mtmp")])
                    yield
                tt(oaT[:, :, csl], mtmp[:], ug[:, :, csl], ALU.mult, [bf("mtmp"), bf("ug")], [bf("oaT")])
            for ci in range(CPB):
                gens = [o_part(ci), g_part(ci)]
                while gens:
                    for g_ in list(gens):
                        try:
                            next(g_)
                        except StopIteration:
                            gens.remove(g_)
            linear_fm(win_d, c.OFF_GATE, 2 * KT, KT, h2r, N,
                      lambda ft, p_ap, p_b: act(gt[:, ft, :], p_ap, AF.Sigmoid, [p_b], [bf("gt")]))
            linear_fm(wbr_d[0], 0, KT, KT, lambda kt: (oaT[:, kt, :], [bf("oaT")]), N,
                      lambda ft, p_ap, p_b: tt(mA[:, ft, :], p_ap, gt[:, ft, :], ALU.mult, [p_b, bf("gt")], [bf("mA")]))

            def ev_brB(ft, p_ap, p_b):
                tt(mB[:], p_ap, gt[:, KT + ft, :], ALU.mult, [p_b, bf("gt")], [bf("mB")])
                tt(mergedT[:, ft, :], mB[:], mA[:, ft, :], ALU.add, [bf("mB"), bf("mA")], [bf("mergedT")])

            linear_fm(wbr_d[1], 0, KT, KT, lambda kt: (obT[:, kt, :], [bf("obT")]), N, ev_brB)
            linear_fm(wout_d, 0, KT, KT, lambda kt: (mergedT[:, kt, :], [bf("mergedT")]), N,
                      lambda ft, p_ap, p_b: stt(xcur[:, ft, :], p_ap, hga[:, KT + ft:KT + ft + 1], xcur[:, ft, :], ALU.mult,
                                                ALU.add, [p_b, bf("hga")] + xb, xb))
            E.barrier()
            ffn(xcur, xb, N, 2, f3in_d, f3out_d)
            rms_stats(lambda kt: xcur[:, kt, 0:N], xb, N, onesD, bf("onesD"))
            og = ostg[tbi % 2]
            ogb = bf("ostg0")
            for kt in range(KT):
                stt(og[:, kt, :], xcur[:, kt, :], vecs[:, c.V_NF + kt:c.V_NF + kt + 1], rstd[:, 0:N], ALU.mult, ALU.mult,
                    xb + [bf("vecs"), bf("rstd")], [ogb])
            dma("sp", outT_d[:, :, t0:t0 + N], og[:], [ogb], [bf(f"out{tbi}")], f"outst{tbi % 2}")
            E.barrier()

        if not FUSED:
            for tbi in range(NB):
                phase2_tb(tbi)
        else:
            E.op("pool", lambda e: e.memset(tails[:], 0.0), [], [bf("tails")])
            blist = [(wi, tbi) for wi in range(G) for tbi in range(NB)]

            def stageA_gen(wi, tbi):
                col0 = wi * NT + tbi * TB
                xcur = xt[0]
                xb = [bf("xt0")]
                dma("sp", xcur[:, :, 0:TB], xT_d[:, :, col0:col0 + TB], [], xb, "xld0")
                yield from ffn_gen(xcur, xb, TB, 0, f1in_d, f1out_d)
                norm_mod(xcur, xb, TB, 1, h2, [bf("h2")])
                yield

            for _ in stageA_gen(*blist[0]):
                pass
            for bi_, (wi, tbi) in enumerate(blist):
                own = (wi == G - 1)
                vm = None if own else masks[:, wi:wi + 1]
                N = TB
                lastp = (wi == G - 2 and tbi == NB - 1)
                if own:
                    E.barrier()
                qkv_conv(N, [bf("h2")], False, vm=vm, own=(own or lastp), q_tails_only=lastp)
                for ci in range(CPB):
                    ab_proj(ci * 128, ci)
                small_chain_tb(vm)
                nxt_blk = blist[bi_ + 1] if bi_ + 1 < len(blist) else None
                if (not own) and nxt_blk is not None:
                    FILL[0] = stageA_gen(*nxt_blk)
                for ci in range(CPB):
                    chunk_solve(ci * 128, 0, own=own, vm=vm, ci=ci)
                if FILL[0] is not None:
                    for _ in FILL[0]:
                        pass
                    FILL[0] = None
                if own:
                    E.barrier()
                    phase2_tb(tbi)
                    if nxt_blk is not None:
                        for _ in stageA_gen(*nxt_blk):
                            pass
        return B

    Ep = Emit(nc, plan_only=True)
    try:
        program(Ep)
    except Stop:
        pass
    Ee = Emit(nc, plan_only=False, wplan=Ep.wreq)
    try:
        program(Ee)
    except Stop:
        pass

    keys = set()
    for eng, lst in Ee.ops.items():
        for waits, fn, inc in lst:
            keys.add(inc[0])
            for k, v in waits:
                keys.add(k)
    sem = {k: nc.alloc_semaphore(name="s_" + k) for k in sorted(keys)}
    final = dict(Ee.dcnt)

    with nc.Block() as block:
        def replay(e, name, last=False):
            for waits, fn, inc in Ee.ops[name]:
                for k, v in waits:
                    e.wait_ge(sem[k], v)
                ins = fn(e)
                ins.then_inc(sem[inc[0]], inc[1])
            if last:
                for k, v in final.items():
                    e.wait_ge(sem[k], v)
                for k in ("pe", "act", "dve", "pool"):
                    e.wait_ge(sem[k], Ee.cnt[k])

        @block.tensor
        def _(e):
            replay(e, "pe")

        @block.scalar
        def _(e):
            replay(e, "act")

        @block.vector
        def _(e):
            replay(e, "dve")

        @block.gpsimd
        def _(e):
            replay(e, "pool")

        @block.sync
        def _(e):
            replay(e, "sp", last=True)

    nc._n_ops = {k: len(v) for k, v in Ee.ops.items()}
    return nc


def host_inputs(cfg, inp, fused=False):
    c = cfg
    D, KT, H, NT, G = c.D, c.KT, c.H, c.NT, c.G
    f32 = np.float32
    x = np.asarray(inp["x"], f32)
    cc = np.asarray(inp["c"], f32)

    def pk(v):
        return np.asarray(v, f32).reshape(-1, 128).T

    vecs = np.zeros((128, c.NV), f32)
    vecs[:, c.V_N1:c.V_N1 + KT] = pk(inp["norm1_g"][0])
    vecs[:, c.V_N2:c.V_N2 + KT] = pk(inp["norm2_g"][0])
    vecs[:, c.V_N3:c.V_N3 + KT] = pk(inp["norm3_g"][0])
    vecs[:, c.V_NF:c.V_NF + KT] = pk(inp["final_g"])
    vecs[:, c.V_BADA:c.V_BADA + 9 * KT] = pk(inp["b_ada"][0])
    vecs[:, c.V_LNG:c.V_LNG + KT] = pk(inp["gm_ln_g"][0])
    vecs[:, c.V_DNG] = np.asarray(inp["dn_norm_g"][0], f32)
    cw = np.asarray(inp["conv_w"][0], f32)
    for j in range(4):
        vecs[:, c.V_CONV + j * 3 * KT: c.V_CONV + (j + 1) * 3 * KT] = pk(cw[j])
    rowv = np.concatenate([np.asarray(inp["gm_ln_b"][0], f32), np.asarray(inp["gm_b_s"][0], f32).reshape(-1)])[None, :]
    bc8 = np.tile(np.concatenate([np.asarray(inp["a_log"][0], f32), np.asarray(inp["dt_bias"][0], f32)])[None, :], (128, 1))
    w_sT = np.ascontiguousarray(np.transpose(np.asarray(inp["gm_w_s"][0], f32), (0, 2, 1)))
    shared = {
        "vecs": vecs, "rowv": np.ascontiguousarray(rowv), "bc8": np.ascontiguousarray(bc8),
        "w_sT": w_sT,
    }
    wfulls = {"w_ada": inp["w_ada"][0], "ffn1_w_in": inp["ffn1_w_in"][0], "ffn1_w_out": inp["ffn1_w_out"][0],
              "w_in": inp["w_in"][0], "w_branch": np.asarray(inp["w_branch"][0]).reshape(2 * D, D),
              "w_out": inp["w_out"][0], "ffn2_w_in": inp["ffn2_w_in"][0], "ffn2_w_out": inp["ffn2_w_out"][0]}
    wfulls = {k: np.asarray(v, f32) for k, v in wfulls.items()}
    ii = np.arange(128)
    bm = np.zeros((128, 4, 128), f32)
    bm[:, 0, :] = (ii[:, None] // 16 == ii[None, :] // 16)
    for i, sz in enumerate((16, 32, 64)):
        bm[:, 1 + i, :] = (ii[:, None] // (2 * sz) == ii[None, :] // (2 * sz)) & (ii[:, None] // sz != ii[None, :] // sz)
    maps = []
    for r in range(c.NCORES):
        b, j = r // G, r % G
        s0 = j * NT
        m = np.zeros((128, G), f32)
        if fused:
            xs = np.zeros((G * NT, D), f32)
            for sgi in range(G):
                seg = j - (G - 1) + sgi
                if seg >= 0:
                    xs[sgi * NT:(sgi + 1) * NT] = x[b, seg * NT:(seg + 1) * NT]
                    m[:, sgi] = 1.0
            xT = np.ascontiguousarray(xs.T.reshape(KT, 128, G * NT).transpose(1, 0, 2))
        else:
            xs = np.zeros((NT + 4, D), f32)
            xs[4:] = x[b, s0:s0 + NT]
            if j > 0:
                xs[:4] = x[b, s0 - 4:s0]
            xT = np.ascontiguousarray(xs.T.reshape(KT, 128, NT + 4).transpose(1, 0, 2))
            m[:, 0] = 1.0 if j > 0 else 0.0
            for i in range(G - 1):
                m[:, 1 + i] = 1.0 if i < j else 0.0
        d = dict(shared)
        for k, v in wfulls.items():
            rp = v.shape[0] // c.NCORES
            d[k] = np.ascontiguousarray(v[r * rp:(r + 1) * rp]) if c.WG else v
        d["xT"] = xT
        d["cT"] = np.ascontiguousarray(cc[b].reshape(KT, 128).T)
        d["masks"] = m
        d["bmask"] = bm
        maps.append(d)
    return maps


def host_output(cfg, results, B):
    c = cfg
    out = np.zeros((B, c.G * c.NT, c.D), np.float32)
    for r in range(c.NCORES):
        b, j = r // c.G, r % c.G
        oT = np.asarray(results[r]["outT"], np.float32).reshape(128, c.KT, c.NT)
        out[b, j * c.NT:(j + 1) * c.NT] = oT.transpose(2, 1, 0).reshape(c.NT, c.D)
    return out


_NC_CACHE = {}


def run_two_launch(cfg, inputs, nbatch):
    key = ("two", cfg.D, cfg.NT, cfg.NCORES)
    if key not in _NC_CACHE:
        _NC_CACHE[key] = (build(cfg, part=1), build(cfg, part=2))
    nc1, nc2 = _NC_CACHE[key]
    maps = host_inputs(cfg, inputs)
    res1 = run_bass_kernel_spmd(nc1, maps, core_ids=list(range(cfg.NCORES))).results
    G = cfg.G
    maps2 = []
    for r in range(cfg.NCORES):
        g0 = (r // G) * G
        d = dict(maps[r])
        for k in ("x1s", "h2s", "o_s", "r_s"):
            d[k] = np.ascontiguousarray(res1[r][k])
        d["st_all"] = np.ascontiguousarray(np.concatenate(
            [np.asarray(res1[g0 + i]["st_in"]).reshape(128, cfg.H * 256) for i in range(G)], axis=0))
        maps2.append(d)
    res2 = run_bass_kernel_spmd(nc2, maps2, core_ids=list(range(cfg.NCORES))).results
    return host_output(cfg, res2, nbatch)


def run_fused(cfg, inputs, nbatch):
    key = ("fused", cfg.D, cfg.NT, cfg.NCORES)
    if key not in _NC_CACHE:
        _NC_CACHE[key] = build(cfg, part=3)
    nc = _NC_CACHE[key]
    maps = host_inputs(cfg, inputs, fused=True)
    res = run_bass_kernel_spmd(nc, maps, core_ids=list(range(cfg.NCORES))).results
    return host_output(cfg, res, nbatch)


def kernel(**inputs):
    cfg = Cfg(WG=False)
    return run_fused(cfg, inputs, 2)
```
